# Optimizing a Trainium2 kernel written in Bass

```python
import math
import jax, jax.numpy as jnp
from jax import lax
import numpy as np

D_MODEL = 1024
BATCH = 8
SEQ = 4096
DEPTH = 4

GRID_W = 64
CTX_LEN = 256
N_MIXERS = 3
D_FF = 4 * D_MODEL
NORM_EPS = 1e-6
N_MOD = 6
S5_GROUP = 16
S5_GROUPS = D_MODEL // S5_GROUP
S5_STATE = 64
S5_DT_MIN = 1e-3
S5_DT_MAX = 1e-1
DIFF_HEAD_DIM = 64
DIFF_HEADS = D_MODEL // (2 * DIFF_HEAD_DIM)
DIFF_V_DIM = 2 * DIFF_HEAD_DIM
ROPE_BASE = 10000.0
Q_BLOCK = 128
FOURIER_GROUPS = 4
FOURIER_CH = D_MODEL // FOURIER_GROUPS
N_S5_LAYERS = (DEPTH + 2) // 3
N_DIFF_LAYERS = (DEPTH + 1) // 3
N_FOURIER_LAYERS = DEPTH // 3

kernel_name = "hybrid_s5_diffattn_fourier_prefix_dit"


def _rmsnorm(x, g):
    x32 = x.astype(jnp.float32)
    y = x32 * lax.rsqrt(jnp.mean(x32 * x32, axis=-1, keepdims=True) + NORM_EPS)
    return (y * g.astype(jnp.float32)).astype(x.dtype)


def _modulate(h, shift, scale):
    return h * (1 + scale) + shift


def _sqrelu_mlp(h, w1, w2):
    a = jax.nn.relu(h @ w1)
    return (a * a) @ w2


def _s5_discretize(lam_re, lam_im, log_dt, b_re, b_im):
    f32 = jnp.float32
    lam_re = lam_re.astype(f32)
    lam_im = lam_im.astype(f32)
    dt = jnp.exp(log_dt.astype(f32))[:, None]
    mag = jnp.exp(lam_re * dt)
    a_re = mag * jnp.cos(lam_im * dt)
    a_im = mag * jnp.sin(lam_im * dt)
    n_re = a_re - 1.0
    n_im = a_im
    den = lam_re * lam_re + lam_im * lam_im
    k_re = (n_re * lam_re + n_im * lam_im) / den
    k_im = (n_im * lam_re - n_re * lam_im) / den
    b_re = b_re.astype(f32)
    b_im = b_im.astype(f32)
    bb_re = k_re[..., None] * b_re - k_im[..., None] * b_im
    bb_im = k_re[..., None] * b_im + k_im[..., None] * b_re
    return a_re, a_im, bb_re, bb_im


def _complex_combine(e1, e2):
    a1r, a1i, b1r, b1i = e1
    a2r, a2i, b2r, b2i = e2
    return (a2r * a1r - a2i * a1i,
            a2r * a1i + a2i * a1r,
            a2r * b1r - a2i * b1i + b2r,
            a2r * b1i + a2i * b1r + b2i)


def _s5_states(u, a_re, a_im, bb_re, bb_im, h0, reverse):
    bu_re = jnp.einsum('blgh,gph->blgp', u, bb_re)
    bu_im = jnp.einsum('blgh,gph->blgp', u, bb_im)
    if reverse:
        bu_re = jnp.flip(bu_re, 1)
        bu_im = jnp.flip(bu_im, 1)
    shape = (1, u.shape[1]) + a_re.shape
    ar = jnp.broadcast_to(a_re, shape)
    ai = jnp.broadcast_to(a_im, shape)
    acr, aci, hr, hi = lax.associative_scan(_complex_combine, (ar, ai, bu_re, bu_im), axis=1)
    if h0 is not None:
        h0r = h0[0][:, None]
        h0i = h0[1][:, None]
        hr = hr + acr * h0r - aci * h0i
        hi = hi + acr * h0i + aci * h0r
    final = (hr[:, -1], hi[:, -1])
    if reverse:
        hr = jnp.flip(hr, 1)
        hi = jnp.flip(hi, 1)
    return hr, hi, final


def _s5_readout(s_re, s_im, c_re, c_im):
    return jnp.einsum('blgp,ghp->blgh', s_re, c_re) - jnp.einsum('blgp,ghp->blgh', s_im, c_im)


def _s5_output(y, u, d, w_glu, dtype):
    bsz, n = y.shape[:2]
    z = jax.nn.gelu(y.reshape(bsz, n, D_MODEL) + d.astype(jnp.float32) * u.reshape(bsz, n, D_MODEL))
    g = z.astype(dtype) @ w_glu
    return g[..., :D_MODEL] * jax.nn.sigmoid(g[..., D_MODEL:])


def _s5_mixer(h_lat, h_ctx, lam_re, lam_im, log_dt, b_re, b_im, c_re, c_im, d, w_glu, ctx_out):
    f32 = jnp.float32
    bsz, n_lat, _ = h_lat.shape
    n_ctx = h_ctx.shape[1]
    u_lat = h_lat.astype(f32).reshape(bsz, n_lat, S5_GROUPS, S5_GROUP)
    u_ctx = h_ctx.astype(f32).reshape(bsz, n_ctx, S5_GROUPS, S5_GROUP)
    ys_lat = []
    ys_ctx = []
    for direction in range(2):
        reverse = direction == 1
        a_re, a_im, bb_re, bb_im = _s5_discretize(lam_re[direction], lam_im[direction], log_dt[direction], b_re[direction], b_im[direction])
        cr = c_re[direction].astype(f32)
        ci = c_im[direction].astype(f32)
        s_re, s_im, final = _s5_states(u_ctx, a_re, a_im, bb_re, bb_im, None, reverse)
        if ctx_out:
            ys_ctx.append(_s5_readout(s_re, s_im, cr, ci))
        s_re, s_im, _ = _s5_states(u_lat, a_re, a_im, bb_re, bb_im, final, reverse)
        ys_lat.append(_s5_readout(s_re, s_im, cr, ci))
    out_lat = _s5_output(ys_lat[0] + ys_lat[1], u_lat, d, w_glu, h_lat.dtype)
    out_ctx = _s5_output(ys_ctx[0] + ys_ctx[1], u_ctx, d, w_glu, h_ctx.dtype) if ctx_out else None
    return out_lat, out_ctx


def _axial_rope_tables(n_tokens):
    f32 = jnp.float32
    rows = n_tokens // GRID_W
    row = jnp.repeat(jnp.arange(rows, dtype=f32), GRID_W)
    col = jnp.tile(jnp.arange(GRID_W, dtype=f32), rows)
    half = DIFF_HEAD_DIM // 2
    inv = jnp.power(ROPE_BASE, -jnp.arange(0, half, 2, dtype=f32) / half)
    ang_r = row[:, None] * inv
    ang_c = col[:, None] * inv
    ex = lambda a: a[:, None, None, :]
    return (ex(jnp.cos(ang_r)), ex(jnp.sin(ang_r)), ex(jnp.cos(ang_c)), ex(jnp.sin(ang_c)))


def _rot_half(xh, cos, sin):
    x1, x2 = jnp.split(xh, 2, axis=-1)
    return jnp.concatenate([x1 * cos - x2 * sin, x1 * sin + x2 * cos], axis=-1)


def _axial_rope(x, rope):
    cr, sr, cc, sc = rope
    half = DIFF_HEAD_DIM // 2
    xf = x.astype(jnp.float32)
    out = jnp.concatenate([_rot_half(xf[..., :half], cr, sr), _rot_half(xf[..., half:], cc, sc)], axis=-1)
    return out.astype(x.dtype)


def _diff_qkv(h, w_qkv, q_norm, k_norm):
    bsz, n, _ = h.shape
    q, k, v = jnp.split(h @ w_qkv, 3, axis=-1)
    q = _rmsnorm(q.reshape(bsz, n, DIFF_HEADS, 2, DIFF_HEAD_DIM), q_norm)
    k = _rmsnorm(k.reshape(bsz, n, DIFF_HEADS, 2, DIFF_HEAD_DIM), k_norm)
    v = v.reshape(bsz, n, DIFF_HEADS, DIFF_V_DIM)
    return q, k, v


def _diff_attend(q, k, v, lam):
    s = jnp.einsum('bqhcd,bkhcd->bhcqk', q, k, preferred_element_type=jnp.float32) * (DIFF_HEAD_DIM ** -0.5)
    p = jax.nn.softmax(s, axis=-1)
    w = p[:, :, 0] - lam * p[:, :, 1]
    return jnp.einsum('bhqk,bkhe->bqhe', w.astype(v.dtype), v)


def _diff_mixer(h_lat, h_ctx, rope, w_qkv, q_norm, k_norm, lam_params, subln, w_o, lam_init, ctx_out):
    bsz, n_lat, _ = h_lat.shape
    q_l, k_l, v_l = _diff_qkv(h_lat, w_qkv, q_norm, k_norm)
    q_c, k_c, v_c = _diff_qkv(h_ctx, w_qkv, q_norm, k_norm)
    q_l = _axial_rope(q_l, rope)
    k_l = _axial_rope(k_l, rope)
    lp = lam_params.astype(jnp.float32)
    lam = jnp.exp(jnp.sum(lp[0] * lp[1])) - jnp.exp(jnp.sum(lp[2] * lp[3])) + lam_init
    k_all = jnp.concatenate([k_l, k_c], axis=1)
    v_all = jnp.concatenate([v_l, v_c], axis=1)
    n_blocks = n_lat // Q_BLOCK
    qb = q_l.reshape(bsz, n_blocks, Q_BLOCK, DIFF_HEADS, 2, DIFF_HEAD_DIM).transpose(1, 0, 2, 3, 4, 5)
    o = lax.map(lambda qq: _diff_attend(qq, k_all, v_all, lam), qb)
    o_l = o.transpose(1, 0, 2, 3, 4).reshape(bsz, n_lat, DIFF_HEADS, DIFF_V_DIM)

    def finish(oo):
        oo = _rmsnorm(oo, subln) * (1.0 - lam_init)
        return oo.reshape(oo.shape[0], oo.shape[1], D_MODEL) @ w_o

    out_l = finish(o_l)
    out_c = finish(_diff_attend(q_c, k_c, v_c, lam)) if ctx_out else None
    return out_l, out_c


def _fourier_mixer(h, w_f, b_f):
    bsz, n, _ = h.shape
    hg = h.astype(jnp.float32).reshape(bsz, n, FOURIER_GROUPS, FOURIER_CH)
    f = jnp.fft.fft2(hg, axes=(1, 3), norm='ortho').real
    return f.reshape(bsz, n, D_MODEL).astype(h.dtype) @ w_f + b_f


def setup_inputs(seed: int = 0) -> dict:
    key = jax.random.key(seed)
    ks = jax.random.split(key, 32)
    f32 = jnp.float32
    D = D_MODEL
    G, P, H = S5_GROUPS, S5_STATE, S5_GROUP
    nA, nB, nC = N_S5_LAYERS, N_DIFF_LAYERS, N_FOURIER_LAYERS

    def nrm(k, shape, std):
        return std * jax.random.normal(k, shape, f32)

    n_idx = jnp.arange(P, dtype=f32)
    return {
        'x': nrm(ks[0], (BATCH, SEQ, D), 1.0),
        'c': nrm(ks[1], (BATCH, D), 1.0),
        'ctx': nrm(ks[2], (BATCH, CTX_LEN, D), 1.0),
        'c_ctx': nrm(ks[3], (D,), 1.0),
        'w_mod': nrm(ks[4], (DEPTH, D, N_MOD * D), 0.5 * D ** -0.5),
        'b_mod': nrm(ks[5], (DEPTH, N_MOD * D), 0.02),
        'norm_g': 1.0 + nrm(ks[6], (DEPTH, 2, D), 0.02),
        'mlp_w1': nrm(ks[7], (DEPTH, D, D_FF), D ** -0.5),
        'mlp_w2': nrm(ks[8], (DEPTH, D_FF, D), D_FF ** -0.5),
        's5_lambda_re': -0.5 + nrm(ks[9], (nA, 2, G, P), 0.01),
        's5_lambda_im': math.pi * n_idx + nrm(ks[10], (nA, 2, G, P), 0.01),
        's5_log_dt': jax.random.uniform(ks[11], (nA, 2, G), f32, math.log(S5_DT_MIN), math.log(S5_DT_MAX)),
        's5_b_re': nrm(ks[12], (nA, 2, G, P, H), (2.0 * H) ** -0.5),
        's5_b_im': nrm(ks[13], (nA, 2, G, P, H), (2.0 * H) ** -0.5),
        's5_c_re': nrm(ks[14], (nA, 2, G, H, P), (2.0 * P) ** -0.5),
        's5_c_im': nrm(ks[15], (nA, 2, G, H, P), (2.0 * P) ** -0.5),
        's5_d': nrm(ks[16], (nA, D), 1.0),
        's5_w_glu': nrm(ks[17], (nA, D, 2 * D), D ** -0.5),
        'diff_w_qkv': nrm(ks[18], (nB, D, 3 * D), D ** -0.5),
        'diff_q_norm': 1.0 + nrm(ks[19], (nB, DIFF_HEAD_DIM), 0.02),
        'diff_k_norm': 1.0 + nrm(ks[20], (nB, DIFF_HEAD_DIM), 0.02),
        'diff_lambda': nrm(ks[21], (nB, 4, DIFF_HEAD_DIM), 0.1),
        'diff_subln': 1.0 + nrm(ks[22], (nB, DIFF_V_DIM), 0.02),
        'diff_w_o': nrm(ks[23], (nB, D, D), D ** -0.5),
        'fourier_w': nrm(ks[24], (nC, D, D), D ** -0.5),
        'fourier_b': nrm(ks[25], (nC, D), 0.02),
    }


def reference(x, c, ctx, c_ctx, w_mod, b_mod, norm_g, mlp_w1, mlp_w2,
              s5_lambda_re, s5_lambda_im, s5_log_dt, s5_b_re, s5_b_im, s5_c_re, s5_c_im,
              s5_d, s5_w_glu, diff_w_qkv, diff_q_norm, diff_k_norm, diff_lambda, diff_subln,
              diff_w_o, fourier_w, fourier_b):
    n_lat = x.shape[1]
    rope = _axial_rope_tables(n_lat)
    cond_lat = jax.nn.silu(c)
    cond_ctx = jax.nn.silu(c_ctx)
    for i in range(DEPTH):
        last = i == DEPTH - 1
        mod_l = (cond_lat @ w_mod[i] + b_mod[i]).reshape(-1, N_MOD, 1, D_MODEL)
        mod_c = (cond_ctx @ w_mod[i] + b_mod[i]).reshape(N_MOD, D_MODEL)
        h_l = _modulate(_rmsnorm(x, norm_g[i, 0]), mod_l[:, 0], mod_l[:, 1])
        h_c = _modulate(_rmsnorm(ctx, norm_g[i, 0]), mod_c[0], mod_c[1])
        kind = i % N_MIXERS
        j = i // N_MIXERS
        if kind == 0:
            o_l, o_c = _s5_mixer(h_l, h_c, s5_lambda_re[j], s5_lambda_im[j], s5_log_dt[j],
                                 s5_b_re[j], s5_b_im[j], s5_c_re[j], s5_c_im[j], s5_d[j],
                                 s5_w_glu[j], not last)
        elif kind == 1:
            lam_init = 0.8 - 0.6 * math.exp(-0.3 * i)
            o_l, o_c = _diff_mixer(h_l, h_c, rope, diff_w_qkv[j], diff_q_norm[j], diff_k_norm[j],
                                   diff_lambda[j], diff_subln[j], diff_w_o[j], lam_init, not last)
        else:
            o_l = _fourier_mixer(h_l, fourier_w[j], fourier_b[j])
            o_c = None if last else _fourier_mixer(h_c, fourier_w[j], fourier_b[j])
        x = x + mod_l[:, 2] * o_l
        x = x + mod_l[:, 5] * _sqrelu_mlp(
            _modulate(_rmsnorm(x, norm_g[i, 1]), mod_l[:, 3], mod_l[:, 4]), mlp_w1[i], mlp_w2[i])
        if not last:
            ctx = ctx + mod_c[2] * o_c
            ctx = ctx + mod_c[5] * _sqrelu_mlp(
                _modulate(_rmsnorm(ctx, norm_g[i, 1]), mod_c[3], mod_c[4]), mlp_w1[i], mlp_w2[i])
    return x
```

```python
import contextlib
import math
import numpy as np
import ml_dtypes
import concourse.bass as bass
import concourse.mybir as mybir
from concourse.bass_utils import run_bass_kernel_spmd

F32 = mybir.dt.float32
BF16 = mybir.dt.bfloat16
I32 = mybir.dt.int32
ALU = mybir.AluOpType
AF = mybir.ActivationFunctionType
AX = mybir.AxisListType

ENGS = ("pe", "act", "dve", "pool", "sp")


class Buf:
    __slots__ = ("name", "lw", "rd")

    def __init__(self, name="b"):
        self.name = name
        self.lw = None
        self.rd = {}

    def inherit(self, others):
        for o in others:
            for d in ([o.lw] if o.lw is not None else []) + list(o.rd.values()):
                _rd_add(self.rd, d, force=True)


def _rd_add(rd, d, force=False):
    k = (d[0], d[1]) + (("f",) if force else ())
    old = rd.get(k)
    if old is None or old[2] < d[2]:
        rd[k] = d


class Instr:
    __slots__ = ("eng", "fn", "waits", "needs_inc", "is_dma", "dsem", "dval", "idx")

    def __init__(self, eng, fn):
        self.eng = eng
        self.fn = fn
        self.waits = []
        self.needs_inc = False
        self.is_dma = False
        self.dsem = None
        self.dval = 0
        self.idx = -1


class Prog:
    def __init__(self, nc):
        self.nc = nc
        self.stack = contextlib.ExitStack()
        self.streams = {e: [] for e in ENGS}
        self.known = {e: {} for e in ENGS}
        self.dma_cnt = {}
        self._rr = {}

    def sbuf(self, name, shape, dtype):
        return self.stack.enter_context(self.nc.sbuf_tensor(name, list(shape), dtype))

    def psum(self, name, shape, dtype):
        return self.stack.enter_context(self.nc.psum_tensor(name, list(shape), dtype))

    def _need(self, ins, dep):
        if dep is None:
            return
        if dep[0] == 'E':
            _, e2, idx = dep
            if e2 == ins.eng and e2 == "pe":
                return
            key = ('E', e2)
            val = idx
        else:
            key = ('D', dep[1])
            val = dep[2]
        kn = self.known[ins.eng]
        if kn.get(key, -1) >= val:
            return
        kn[key] = val
        if dep[0] == 'E':
            self.streams[e2][idx].needs_inc = True
            ins.waits.append(('E', e2, idx))
        else:
            ins.waits.append(('D', dep[1], val))

    def _track(self, ins, reads, writes):
        for b in reads:
            self._need(ins, b.lw)
        for b in writes:
            self._need(ins, b.lw)
            for k, d in b.rd.items():
                if d[0] == 'E' and d[1] == ins.eng and not ins.is_dma and len(k) == 2:
                    continue
                self._need(ins, d)

    def _commit(self, dep, reads, writes):
        for b in reads:
            _rd_add(b.rd, dep)
        for b in writes:
            b.lw = dep
            b.rd = {}

    def record_begin(self):
        self._rec = []

    def record_end(self):
        r, self._rec = self._rec, None
        return r

    def replay(self, rec, n):
        for _ in range(min(n, len(rec))):
            kind, a, kw = rec.pop(0)
            if kind == "op":
                self.op(*a, **kw)
            else:
                self.dma(*a, **kw)

    def op(self, eng, fn, reads=(), writes=(), **kw):
        if getattr(self, "_rec", None) is not None:
            self._rec.append(("op", (eng, fn, list(reads), list(writes)), kw))
            return None
        if isinstance(fn, str):
            name = fn

            def fn(e, name=name, kw=kw):
                return getattr(e, name)(**kw)
        ins = Instr(eng, fn)
        st = self.streams[eng]
        ins.idx = len(st)
        self._track(ins, reads, writes)
        st.append(ins)
        self._commit(('E', eng, ins.idx), reads, writes)
        return ins

    def dma(self, out_ap, in_ap, reads, writes, semkey, q="sp", **kw):
        if getattr(self, "_rec", None) is not None:
            self._rec.append(("dma", (out_ap, in_ap, list(reads), list(writes), semkey), dict(q=q, **kw)))
            return None

        def fn(e, out_ap=out_ap, in_ap=in_ap, kw=kw):
            return e.dma_start(out=out_ap, in_=in_ap, **kw)
        ins = Instr(q, fn)
        ins.is_dma = True
        st = self.streams[q]
        ins.idx = len(st)
        pool = self.DMA_SEMS[q]
        rr = self._rr.get(q, 0)
        self._rr[q] = rr + 1
        semkey = f"{q}{rr % pool}"
        prev = self.dma_cnt.get(semkey, 0)
        if prev:
            self._need(ins, ('D', semkey, prev))
        self._track(ins, reads, writes)
        v = prev + 16
        self.dma_cnt[semkey] = v
        ins.dsem = semkey
        ins.dval = v
        st.append(ins)
        self._commit(('D', semkey, v), reads, writes)
        return ins

    DMA_SEMS = {"sp": 64, "pool": 32, "act": 2}

    def emit(self):
        nc = self.nc
        sems = {}
        for e in ENGS:
            sems[('E', e)] = self.stack.enter_context(nc.semaphore(f"s_{e}"))
        for k in self.dma_cnt:
            sems[('D', k)] = self.stack.enter_context(nc.semaphore(f"d_{k}"))
        cnt = {}
        for e in ENGS:
            c = 0
            for ins in self.streams[e]:
                if ins.needs_inc:
                    c += 1
                    cnt[(e, ins.idx)] = c
        engmap = {"pe": "tensor", "act": "scalar", "dve": "vector", "pool": "gpsimd", "sp": "sync"}

        def run(e, eng):
            for ins in self.streams[e]:
                for w in ins.waits:
                    if w[0] == 'E':
                        eng.wait_ge(sems[('E', w[1])], cnt[(w[1], w[2])])
                    else:
                        eng.wait_ge(sems[('D', w[1])], w[2])
                r = ins.fn(eng)
                if ins.is_dma:
                    r.then_inc(sems[('D', ins.dsem)], 16)
                elif ins.needs_inc:
                    r.then_inc(sems[('E', e)], 1)
            if e == "sp":
                for k, v in self.dma_cnt.items():
                    eng.wait_ge(sems[('D', k)], v)

        with nc.Block() as block:
            for e in ENGS:
                getattr(block, engmap[e])(lambda eng, e=e: run(e, eng))


D = 1024
DFF = 4096
NMOD = 6
EPS = 1e-6
KD = D // 128


class Cfg:
    def __init__(self, n_lat=4096, n_ctx=256, layers=(0, 1, 2, 3), depth=4, mixers=True):
        self.n_lat = n_lat
        self.n_ctx = n_ctx
        self.layers = tuple(layers)
        self.depth = depth
        self.mixers = mixers
        self.mixer_kinds = (0, 1, 2)
        self.nt_ctx = n_ctx // 128
        self.nt_lat = n_lat // 128
        self.nt = self.nt_ctx + self.nt_lat
        self.ntok = n_lat + n_ctx


class K:
    def __init__(self, cfg):
        self.cfg = cfg
        nc = self.nc = bass.Bass("TRN2", target_bir_lowering=False)
        self.P = Prog(nc)
        self.dram = {}

    def din(self, name, shape, dtype=F32):
        t = self.nc.dram_tensor(name, list(shape), dtype, kind="ExternalInput").ap()
        self.dram[name] = t
        return t

    def dump(self, name, ap, buf, shape, dtype=F32):
        t = self.nc.dram_tensor(name, list(shape), dtype, kind="ExternalOutput").ap()
        self.P.dma(t, ap, [buf], [], "dbg")

    def dscratch(self, name, shape, dtype):
        t = self.nc.dram_tensor(name, list(shape), dtype).ap()
        self.dram[name] = t
        return t

    def build(self):
        cfg = self.cfg
        nc = self.nc
        P = self.P
        x_in = self.din("x", [cfg.n_lat, D])
        ctx_in = self.din("ctx", [cfg.n_ctx, D])
        self.din("c_t", [128, KD])
        self.din("cctx_t", [128, KD])
        self.din("w_mod", [4, D, NMOD * D])
        self.din("b_mod", [4, NMOD * D])
        self.din("norm_g", [4, 2, D])
        self.din("mlp_w1", [4, D, DFF])
        self.din("mlp_w2", [4, DFF, D])
        self.din("s5_J", [8, 128, 240], BF16)
        self.din("s5_mask01", [cfg.ntok // 8])
        self.din("s5_maskFB", [2, 128, 128])
        self.din("ident32", [128, 128])
        self.din("s5_lamr_t", [2, 128, 64])
        self.din("s5_lami_t", [2, 128, 64])
        self.din("s5_log_dt", [2, 2, 64])
        for nm in ("s5_br_t", "s5_bi_t", "s5_cr_t", "s5_ci_t"):
            self.din(nm, [2, 128, 64, 16])
        self.din("s5_dcol", [2, 128, 64])
        self.din("s5_w_glu", [2, D, 2 * D])
        self.din("diff_w_qkv", [1, D, 3 * D])
        self.din("diff_q_norm", [1, 64])
        self.din("diff_k_norm", [1, 64])
        self.din("diff_lambda", [1, 4, 64])
        self.din("diff_subln", [1, 128])
        self.din("diff_w_o", [1, D, D])
        self.din("rope_tab", [cfg.n_lat, 64])
        self.dscratch("qT_d", [8, 128, cfg.ntok], BF16)
        self.dscratch("kT_d", [8, 128, cfg.ntok], BF16)
        self.dscratch("v_d", [cfg.ntok, D], BF16)
        self.dscratch("onT_d", [8, 128, cfg.ntok], BF16)
        self.din("fourier_w", [1, D, D])
        self.din("fourier_b", [1, D])
        self.din("cs256n", [256, 512], BF16)
        self.din("cs256p", [256, 512], BF16)
        self.din("dft_n", [cfg.nt_lat, 128, 2, cfg.nt_lat, 128], BF16)
        self.din("ident_bf", [128, 128], BF16)
        self.din("onehot", [128, 1])
        out = self.nc.dram_tensor("out", [cfg.n_lat, D], F32, kind="ExternalOutput").ap()
        self.dram["out"] = out
        cs = self.dscratch("cs", [cfg.n_ctx, D], F32)
        self.dscratch("hT_d", [KD, 128, cfg.ntok], BF16)
        self.dscratch("modrow", [2, 4, D], F32)
        self.modrowb = Buf("modrow")
        self.hTdb = Buf("hT_d")

        with P.stack:
            self.alloc()
            self.xsrc = [None] * cfg.nt
            self.xbuf = [Buf(f"x{t}") for t in range(cfg.nt)]
            for t in range(cfg.nt):
                if t < cfg.nt_ctx:
                    self.xsrc[t] = ctx_in[t * 128:(t + 1) * 128, :]
                else:
                    tl = t - cfg.nt_ctx
                    self.xsrc[t] = x_in[tl * 128:(tl + 1) * 128, :]
            self.xdst = []
            for t in range(cfg.nt):
                if t < cfg.nt_ctx:
                    self.xdst.append(cs[t * 128:(t + 1) * 128, :])
                else:
                    tl = t - cfg.nt_ctx
                    self.xdst.append(out[tl * 128:(tl + 1) * 128, :])
            self.prologue()
            for i in cfg.layers:
                last = (i == cfg.depth - 1)
                self.mod_phase(i)
                if getattr(cfg, "debug", False) and i == cfg.layers[0]:
                    self.dump("dbg_cols", self.cols[:].rearrange("p s v k -> p (s v k)"), self.colsb, [128, 64])
                    self.dump("dbg_gates", self.gates[:].rearrange("p s w d -> p (s w d)"), self.gatesb, [128, 4 * D])
                kind = i % 3
                if cfg.mixers and kind in cfg.mixer_kinds:
                    if kind == 2:
                        self.fourier_phase(i, last)
                    elif kind == 1:
                        self.attn_phase(i, last)
                    else:
                        self.s5_phase(i, last)
                self.mlp_phase(i, last)
            if self.xsrc[cfg.nt - 1] is not self.xdst[cfg.nt - 1]:
                raise RuntimeError("no layer wrote the output")
            P.emit()
        return nc

    ARENA_WORDS = 41472
    WREG = 16384

    def alloc(self):
        P = self.P
        self.arena = P.sbuf("arena", [128, self.ARENA_WORDS], F32)
        self.alive = []
        self.ident = P.sbuf("ident", [128, 128], BF16)
        self.onehot = P.sbuf("onehot_s", [128, 1], F32)
        self.eps_t = P.sbuf("eps_t", [128, 1], F32)
        self.cb = Buf("consts")
        self.cols = P.sbuf("cols", [128, 2, 4, KD], F32)
        self.colsb = Buf("cols")
        self.gates = P.sbuf("gates", [128, 2, 2, D], F32)
        self.gatesb = Buf("gates")
        self.cond = P.sbuf("cond", [128, 2, KD], F32)
        self.condrep = P.sbuf("condrep", [128, 2, KD, 128], BF16)
        self.condb = Buf("cond")
        self.xt = [P.sbuf(f"xt{j}", [128, D], F32) for j in range(2)]
        self.xtb = [Buf(f"xt{j}") for j in range(2)]
        self.xn = [P.sbuf(f"xn{j}", [128, D], BF16) for j in range(2)]
        self.xnb = [Buf(f"xn{j}") for j in range(2)]
        self.ss = [P.sbuf(f"ss{j}", [128, 2], F32) for j in range(2)]
        self.ssb = [Buf(f"ss{j}") for j in range(2)]
        self.norm_i = 0
        self.xe = [P.sbuf(f"xe{j}", [128, D], F32) for j in range(2)]
        self.xeb = [Buf(f"xe{j}") for j in range(2)]
        self.xe_i = 0
        self.tmpe = [P.sbuf(f"tmpe{j}", [128, 512], F32) for j in range(2)]
        self.tmpeb = [Buf(f"tmpe{j}") for j in range(2)]
        self.tmpe_i = 0
        self.ps = P.psum("ps", [128, 8, 512], F32)
        self.psb = [Buf(f"ps{j}") for j in range(8)]

    def aalloc(self, off, words, dtype=F32, name="a"):
        assert off >= 0 and off + words <= self.ARENA_WORDS, (name, off, words)
        ap = self.arena[:, off:off + words]
        if dtype != F32:
            ap = ap.bitcast(dtype)
        b = Buf(name)
        keep = []
        over = []
        for (o, n, ob) in self.alive:
            if o < off + words and off < o + n:
                over.append(ob)
                if not (off <= o and o + n <= off + words):
                    keep.append((o, n, ob))
            else:
                keep.append((o, n, ob))
        b.inherit(over)
        keep.append((off, words, b))
        self.alive = keep
        return ap, b

    def next_region(self):
        r = getattr(self, "_region", 1) ^ 1
        self._region = r
        return r

    def prologue(self):
        P = self.P
        d = self.dram
        P.dma(self.ident[:], d["ident_bf"], [], [self.cb], "const")
        P.dma(self.onehot[:], d["onehot"], [], [self.cb], "const")
        P.op("pool", "memset", [], [self.cb], ap=self.eps_t[:], constant=EPS)
        P.dma(self.cond[:, 0, :], d["c_t"], [], [self.condb], "const")
        P.dma(self.cond[:, 1, :], d["cctx_t"], [], [self.condb], "const")
        P.op("act", "activation", [self.condb], [self.condb], out=self.cond[:], in_=self.cond[:], func=AF.Silu)
        P.op("dve", "tensor_copy", [self.condb], [self.condb], out=self.condrep[:],
             in_=self.cond[:].unsqueeze(3).to_broadcast([128, 2, KD, 128]))

    def load_weight(self, off_words, src, n_k, n_cols, name, kgroup=1):
        P = self.P
        words = n_k * n_cols // 2
        bufs = []
        apfull = None
        for g0 in range(0, n_k, kgroup):
            gw = kgroup * n_cols // 2
            ap, b = self.aalloc(off_words + (g0 // kgroup) * gw, gw, BF16, f"{name}{g0}")
            ap3 = ap.rearrange("p (k n) -> p k n", k=kgroup)
            P.dma(ap3, src[g0 * 128:(g0 + kgroup) * 128, :].rearrange("(k p) n -> p k n", p=128), [], [b], f"w{(g0 // kgroup) % 4}", q="pool")
            bufs += [b] * kgroup
        full = self.arena[:, off_words:off_words + words].bitcast(BF16).rearrange("p (k n) -> p k n", k=n_k)
        return full, bufs

    def mod_phase(self, i):
        P = self.P
        d = self.dram
        ps = self.ps
        WOFF = 2 * self.WREG
        grow, gb = self.aalloc(WOFF, 2 * D, F32, "grow")
        P.dma(grow.rearrange("p (a n) -> p a n", a=2), d["norm_g"][i].partition_broadcast(128), [], [gb], "modc")
        tmps = [self.aalloc(WOFF + 2 * D + j * 512, 512, F32, f"modtmp{j}") for j in range(2)]
        ti = 0
        for half in range(2):
            region = self.next_region()
            slab, sbufs = self.load_weight(region * self.WREG, d["w_mod"][i][:, half * 3072:(half + 1) * 3072], KD, 3072, f"wmod{half}_", kgroup=2)
            bmod_bc, bmb = self.aalloc(WOFF + 2 * D + 1024, 3072, F32, f"bmod{half}")
            P.dma(bmod_bc, d["b_mod"][i, half * 3072:(half + 1) * 3072].partition_broadcast(128), [], [bmb], "modc")
            for bl in range(6):
                blk = half * 6 + bl
                v = blk // 2
                hcol = (blk % 2) * 512
                for s_ in range(2):
                    bank = 6 + s_
                    for k in range(KD):
                        P.op("pe", "matmul", [self.condb, sbufs[k]], [self.psb[bank]], out=ps[:, bank, :], lhsT=self.condrep[:, s_, k, :],
                             rhs=slab[:, k, bl * 512:(bl + 1) * 512], start=(k == 0), stop=(k == KD - 1))
                    bsl = bmod_bc[:, bl * 512:(bl + 1) * 512]
                    if v == 2 or v == 5:
                        dst = self.gates[:, s_, 0 if v == 2 else 1, hcol:hcol + 512]
                        P.op("dve", "tensor_tensor", [self.psb[bank], bmb], [self.gatesb], out=dst, in0=ps[:, bank, :], in1=bsl, op=ALU.add)
                        continue
                    tm, tmb = tmps[ti % 2]
                    ti += 1
                    P.op("dve", "tensor_tensor", [self.psb[bank], bmb], [tmb], out=tm, in0=ps[:, bank, :], in1=bsl, op=ALU.add)
                    is_scale = v in (1, 4)
                    which = 0 if v < 3 else 1
                    if is_scale:
                        P.op("dve", "scalar_tensor_tensor", [tmb, gb], [tmb], out=tm, in0=tm, scalar=1.0,
                             in1=grow[:, which * D + hcol: which * D + hcol + 512], op0=ALU.add, op1=ALU.mult)
                    vec = which * 2 + (0 if is_scale else 1)
                    P.dma(d["modrow"][s_, vec, hcol:hcol + 512].unsqueeze(0), tm[0:1, :], [tmb], [self.modrowb], "modrow")
                    for kk in range(4):
                        cidx = s_ * 32 + vec * KD + (blk % 2) * 4 + kk
                        P.op("pe", "matmul", [tmb, self.cb], [self.psb[5]], out=ps[:, 5, cidx:cidx + 1], lhsT=tm[:, kk * 128:(kk + 1) * 128],
                             rhs=self.onehot[:, 0:1], start=True, stop=True)
        P.op("dve", "tensor_copy", [self.psb[5]], [self.colsb], out=self.cols[:].rearrange("p s v k -> p (s v k)"), in_=ps[:, 5, 0:64])

    def norm_stats(self, t):
        P = self.P
        j = self.norm_i % 2
        self.norm_i += 1
        xt, xtb = self.xt[j], self.xtb[j]
        xn, xnb = self.xn[j], self.xnb[j]
        ss, ssb = self.ss[j], self.ssb[j]
        P.dma(xt[:], self.xsrc[t], [self.xbuf[t]], [xtb], f"xt{j}")
        P.op("act", "activation", [xtb], [xnb, ssb], out=xn[:], in_=xt[:], func=AF.Square, accum_out=ss[:, 0:1])
        P.op("act", "activation", [ssb, self.cb], [ssb], out=ss[:, 1:2], in_=ss[:, 0:1], func=AF.Sqrt, scale=1.0 / D, bias=self.eps_t[:])
        P.op("dve", "reciprocal", [ssb], [ssb], out=ss[:, 1:2], in_=ss[:, 1:2])
        return xt, xtb, xn, xnb, ss, ssb

    def transpose8(self, src, srcb, dst, dstb, engine="dve", bank=4):
        P = self.P
        psT = self.ps[:, bank, :].bitcast(BF16)
        for k in range(KD):
            P.op("pe", "transpose", [srcb, self.cb], [self.psb[bank]], out=psT[:, k * 128:(k + 1) * 128], in_=src[:, k * 128:(k + 1) * 128], identity=self.ident[:])
        if engine == "dve":
            P.op("dve", "tensor_copy", [self.psb[bank]], [dstb], out=dst.rearrange("p k n -> p (k n)"), in_=psT)
        else:
            P.op("act", "copy", [self.psb[bank]], [dstb], out=dst.rearrange("p k n -> p (k n)"), in_=psT)

    def norm_T(self, t, which, dstT, dstb, col0, bank=4):
        h = self.norm_part1(t)
        self.norm_part2(h, t, which, dstT, dstb, col0, bank)

    def norm_part1(self, t):
        P = self.P
        xt, xtb, xn, xnb, ss, ssb = self.norm_stats(t)
        P.op("dve", "tensor_scalar", [xtb, ssb], [xnb], out=xn[:], in0=xt[:], scalar1=ss[:, 1:2], scalar2=None, op0=ALU.mult)
        return (xn, xnb)

    def norm_part2(self, h, t, which, dstT, dstb, col0, bank=4):
        P = self.P
        xn, xnb = h
        s_ = 1 if t < self.cfg.nt_ctx else 0
        psT = self.ps[:, bank, :].bitcast(BF16)
        for k in range(KD):
            P.op("pe", "transpose", [xnb, self.cb], [self.psb[bank]], out=psT[:, k * 128:(k + 1) * 128], in_=xn[:, k * 128:(k + 1) * 128], identity=self.ident[:])
        gsi, shi = (0, 1) if which == 0 else (2, 3)
        for k in range(KD):
            P.op("act", "activation", [self.psb[bank], self.colsb], [dstb], out=dstT[:, k, col0:col0 + 128], in_=psT[:, k * 128:(k + 1) * 128], func=AF.Identity,
                 scale=self.cols[:, s_, gsi, k:k + 1], bias=self.cols[:, s_, shi, k:k + 1])

    def resid_begin(self, t):
        P = self.P
        j = self.xe_i % 2
        self.xe_i += 1
        xe, xeb = self.xe[j], self.xeb[j]
        P.dma(xe[:], self.xsrc[t], [self.xbuf[t]], [xeb], f"xe{j}")
        return (t, j, xe, xeb)

    def resid_half(self, st, which, h, pap, pb, bias_row=None, bias_buf=None):
        P = self.P
        t, j, xe, xeb = st
        s_ = 1 if t < self.cfg.nt_ctx else 0
        jj = self.tmpe_i % 2
        self.tmpe_i += 1
        tm, tmb = self.tmpe[jj], self.tmpeb[jj]
        g = self.gates[:, s_, which, h * 512:(h + 1) * 512]
        if bias_row is not None:
            P.op("dve", "tensor_tensor", [pb, bias_buf], [tmb], out=tm[:], in0=pap, in1=bias_row[:, h * 512:(h + 1) * 512], op=ALU.add)
            P.op("dve", "tensor_tensor", [tmb, self.gatesb], [tmb], out=tm[:], in0=tm[:], in1=g, op=ALU.mult)
        else:
            P.op("dve", "tensor_tensor", [pb, self.gatesb], [tmb], out=tm[:], in0=pap, in1=g, op=ALU.mult)
        P.op("dve", "tensor_tensor", [tmb, xeb], [xeb], out=xe[:, h * 512:(h + 1) * 512], in0=xe[:, h * 512:(h + 1) * 512], in1=tm[:], op=ALU.add)

    def resid_end(self, st):
        t, j, xe, xeb = st
        self.P.dma(self.xdst[t], xe[:], [xeb], [self.xbuf[t]], f"xe{j}")
        self.xsrc[t] = self.xdst[t]

    def resid(self, t, which, ps_halves, bias_row=None, bias_buf=None):
        st = self.resid_begin(t)
        for h, (pap, pb) in enumerate(ps_halves):
            self.resid_half(st, which, h, pap, pb, bias_row, bias_buf)
        self.resid_end(st)

    def mlp_phase(self, i, last):
        P = self.P
        cfg = self.cfg
        d = self.dram
        ps = self.ps
        FH = DFF // 2
        NFC = FH // 128
        t0 = cfg.nt_ctx if last else 0
        tiles = list(range(t0, cfg.nt))
        blocks = [tiles[a:a + 4] for a in range(0, len(tiles), 4)]
        hT_d = d["hT_d"]
        hTdb = self.hTdb
        WOFF = 2 * self.WREG
        for pas in range(2):
            region = 1 - pas
            roff = region * self.WREG
            w1, w1b = self.load_weight(roff, d["mlp_w1"][i][:, pas * FH:(pas + 1) * FH], KD, FH, f"w1p{pas}_", kgroup=2)
            w2, w2b = self.load_weight(roff + KD * FH // 2, d["mlp_w2"][i][pas * FH:(pas + 1) * FH, :], NFC, D, f"w2p{pas}_", kgroup=4)
            hTs = []
            for j in range(2):
                ap, b = self.aalloc(WOFF + j * 2048, 2048, BF16, f"hT{j}")
                hTs.append((ap.rearrange("p (k n) -> p k n", k=KD), b))
            aT, aTb = self.aalloc(WOFF + 4096, 4096, BF16, "aT")
            aT = aT.rearrange("p (k n) -> p k n", k=NFC)
            rl = [self.aalloc(WOFF + 8192 + j * 256, 256, BF16, f"relu{j}") for j in range(2)]
            def store_or_load(bi):
                blk = blocks[bi]
                nb = len(blk) * 128
                hT, hTb = hTs[bi % 2]
                tok0 = blk[0] * 128
                if pas == 0:
                    P.dma(hT_d[:, :, tok0:tok0 + nb].rearrange("k p n -> p k n"), hT[:, :, :nb], [hTb], [hTdb], f"hTd{bi % 2}")
                else:
                    P.dma(hT[:, :, :nb], hT_d[:, :, tok0:tok0 + nb].rearrange("k p n -> p k n"), [hTdb], [hTb], f"hTd{bi % 2}")

            if pas == 0:
                hT0, hT0b = hTs[0]
                for q, t in enumerate(blocks[0]):
                    self.norm_T(t, 1, hT0, hT0b, q * 128)
            store_or_load(0)
            for bi, blk in enumerate(blocks):
                nb = len(blk) * 128
                hT, hTb = hTs[bi % 2]
                for fc in range(NFC):
                    bank = fc % 2
                    for k in range(KD):
                        P.op("pe", "matmul", [w1b[k], hTb], [self.psb[bank]], out=ps[:, bank, :nb], lhsT=w1[:, k, fc * 128:(fc + 1) * 128], rhs=hT[:, k, :nb],
                             start=(k == 0), stop=(k == KD - 1))
                    r, rb_ = rl[fc % 2]
                    P.op("act", "activation", [self.psb[bank]], [rb_], out=r[:, :nb], in_=ps[:, bank, :nb], func=AF.Relu)
                    P.op("dve", "tensor_tensor", [rb_], [aTb], out=aT[:, fc, :nb], in0=r[:, :nb], in1=r[:, :nb], op=ALU.mult)
                nxt = blocks[bi + 1] if bi + 1 < len(blocks) else []
                hTn, hTnb = hTs[(bi + 1) % 2]
                pend = None
                if pas == 0 and nxt:
                    pend = self.norm_part1(nxt[0])
                elif pas == 1 and nxt:
                    store_or_load(bi + 1)
                for q, t in enumerate(blk):
                    halves = []
                    for h in range(2):
                        bank = 2 + h
                        for fc in range(NFC):
                            P.op("pe", "matmul", [aTb, w2b[fc]], [self.psb[bank]], out=ps[:, bank, :], lhsT=aT[:, fc, q * 128:(q + 1) * 128],
                                 rhs=w2[:, fc, h * 512:(h + 1) * 512], start=(fc == 0), stop=(fc == NFC - 1))
                        halves.append((ps[:, bank, :], self.psb[bank]))
                    self.resid(t, 1, halves)
                    if pas == 0 and q < len(nxt):
                        self.norm_part2(pend, nxt[q], 1, hTn, hTnb, q * 128)
                        if q + 1 < len(nxt):
                            pend = self.norm_part1(nxt[q + 1])
                if pas == 0 and nxt:
                    for q in range(len(blk), len(nxt)):
                        if q > len(blk):
                            pend = self.norm_part1(nxt[q])
                        self.norm_part2(pend, nxt[q], 1, hTn, hTnb, q * 128)
                        if q + 1 < len(nxt):
                            pend = self.norm_part1(nxt[q + 1])
                    store_or_load(bi + 1)

    def fourier_phase(self, i, last):
        P = self.P
        cfg = self.cfg
        d = self.dram
        ps = self.ps
        j = i // 3
        NTL = cfg.nt_lat
        o = 0
        H, Hb = self.aalloc(o, NTL * 512, BF16, "fH"); o += NTL * 512
        H = H.rearrange("p (c n) -> p c n", c=NTL)
        slabs = []
        for q in range(2):
            ap, b = self.aalloc(o, NTL * 128, BF16, f"fslab{q}"); o += NTL * 128
            slabs.append((ap.rearrange("p (a c k) -> p a c k", a=2, c=NTL), b))
        wf, wfb = self.load_weight(o, d["fourier_w"][j], KD, D, "wf_", kgroup=8); o += KD * D // 2
        csn, csnb = self.aalloc(o, 512, BF16, "csn"); o += 512
        csp, cspb = self.aalloc(o, 512, BF16, "csp"); o += 512
        csn = csn.rearrange("p (c n) -> p c n", c=2)
        csp = csp.rearrange("p (c n) -> p c n", c=2)
        P.dma(csn, d["cs256n"].rearrange("(c p) n -> p c n", p=128), [], [csnb], "fconst")
        P.dma(csp, d["cs256p"].rearrange("(c p) n -> p c n", p=128), [], [cspb], "fconst")
        rows, rowsb = self.aalloc(o, 4 * D, F32, "frows"); o += 4 * D
        rows4 = rows.rearrange("p (s v n) -> p s v n", s=2, v=2)
        for s_ in range(2):
            P.dma(rows4[:, s_], d["modrow"][s_, 0:2, :].partition_broadcast(128), [self.modrowb], [rowsb], "fconst")
        tmp32, tmp32b = self.aalloc(o, D, F32, "ftmp32"); o += D
        UV, UVb = self.aalloc(o, D, BF16, "fUV"); o += D
        UV = UV.rearrange("p (a n) -> p a n", a=2)
        UVT, UVTb = self.aalloc(o, D, BF16, "fUVT"); o += D
        UVT = UVT.rearrange("p (a k n) -> p a k n", a=2, k=KD)
        Fb, Fbb = self.aalloc(o, 512, BF16, "fF"); o += 512
        FT, FTb = self.aalloc(o, 512, BF16, "fFT"); o += 512
        FT = FT.rearrange("p (k n) -> p k n", k=KD)
        fbias, fbiasb = self.aalloc(o, D, F32, "fbias"); o += D
        P.dma(fbias, d["fourier_b"][j].partition_broadcast(128), [], [fbiasb], "fconst")
        Hc, Hcb = self.aalloc(o, cfg.nt_ctx * 512, BF16, "fHc"); o += cfg.nt_ctx * 512
        Hc = Hc.rearrange("p (c n) -> p c n", c=cfg.nt_ctx)

        for t in range(cfg.nt):
            if last and t < cfg.nt_ctx:
                continue
            s_ = 1 if t < cfg.nt_ctx else 0
            xt, xtb, xn, xnb, ss, ssb = self.norm_stats(t)
            P.op("dve", "scalar_tensor_tensor", [xtb, ssb, rowsb], [tmp32b], out=tmp32, in0=xt[:], scalar=ss[:, 1:2], in1=rows4[:, s_, 0, :],
                 op0=ALU.mult, op1=ALU.mult)
            dst = Hc[:, t, :] if s_ else H[:, t - cfg.nt_ctx, :]
            P.op("dve", "tensor_tensor", [tmp32b, rowsb], [Hcb if s_ else Hb], out=dst, in0=tmp32, in1=rows4[:, s_, 1, :], op=ALU.add)

        def out_tile(t, n_chunks, Hsrc, Hsrcb, lhs_c, lhs_s, lhsb, scale):
            dft_part(n_chunks, Hsrc, Hsrcb, lhs_c, lhs_s, lhsb)
            evac_part()
            tail_part(t, scale)

        def dft_part(n_chunks, Hsrc, Hsrcb, lhs_c, lhs_s, lhsb):
            for (a, lhs) in ((0, lhs_c), (1, lhs_s)):
                for h in range(2):
                    bank = a * 2 + h
                    for c in range(n_chunks):
                        P.op("pe", "matmul", [lhsb, Hsrcb], [self.psb[bank]], out=ps[:, bank, :], lhsT=lhs(c), rhs=Hsrc[:, c, h * 512:(h + 1) * 512],
                             start=(c == 0), stop=(c == n_chunks - 1))

        def evac_part():
            for a in range(2):
                P.op("act", "copy", [self.psb[a * 2], self.psb[a * 2 + 1]], [UVb], out=UV[:, a, :], in_=ps[:, a * 2:a * 2 + 2, :].rearrange("p a n -> p (a n)"))

        def tail_part(t, scale):
            for a in range(2):
                self.transpose8(UV[:, a, :], UVb, UVT[:, a], UVTb)
            for g in range(4):
                outp = ps[:, 5 + g // 2, (g % 2) * 256:(g % 2 + 1) * 256]
                n = 0
                for a in range(2):
                    for cc in range(2):
                        P.op("pe", "matmul", [UVTb, csnb], [self.psb[5 + g // 2]], out=outp, lhsT=UVT[:, a, 2 * g + cc, :], rhs=csn[:, cc, a * 256:(a + 1) * 256],
                             start=(n == 0), stop=(n == 3))
                        n += 1
            P.op("act", "activation", [self.psb[5], self.psb[6]], [Fbb], out=Fb, in_=ps[:, 5:7, :].rearrange("p a n -> p (a n)"), func=AF.Copy, scale=scale)
            self.transpose8(Fb, Fbb, FT, FTb)
            st = self.resid_begin(t)
            for h in range(2):
                for k in range(KD):
                    P.op("pe", "matmul", [FTb, wfb[k]], [self.psb[7]], out=ps[:, 7, :], lhsT=FT[:, k, :], rhs=wf[:, k, h * 512:(h + 1) * 512],
                         start=(k == 0), stop=(k == KD - 1))
                self.resid_half(st, 0, h, ps[:, 7, :], self.psb[7], fbias, fbiasb)
            self.resid_end(st)

        if not last:
            assert cfg.n_ctx == 256
            for kt in range(cfg.nt_ctx):
                out_tile(kt, cfg.nt_ctx, Hc, Hcb,
                         lambda c, kt=kt: csp[:, c, kt * 128:(kt + 1) * 128],
                         lambda c, kt=kt: csp[:, c, 256 + kt * 128:256 + (kt + 1) * 128], cspb, 1.0 / math.sqrt(cfg.n_ctx * 256))
        def lat_dft(kt):
            sl, slb = slabs[kt % 2]
            P.dma(sl, d["dft_n"][kt], [], [slb], f"fslab{kt % 2}")
            dft_part(NTL, H, Hb, lambda c, sl=sl: sl[:, 0, c, :], lambda c, sl=sl: sl[:, 1, c, :], slb)

        lat_dft(0)
        for kt in range(NTL):
            evac_part()
            if kt + 1 < NTL:
                lat_dft(kt + 1)
            tail_part(cfg.nt_ctx + kt, 1.0 / math.sqrt(cfg.n_lat * 256))

    def attn_phase(self, i, last):
        P = self.P
        cfg = self.cfg
        d = self.dram
        ps = self.ps
        j = i // 3
        lam_init = 0.8 - 0.6 * math.exp(-0.3 * i)
        NT, NTC, NTOK = cfg.nt, cfg.nt_ctx, cfg.ntok
        ctx_out = not last
        qT_d, kT_d, v_d, onT_d = d["qT_d"], d["kT_d"], d["v_d"], d["onT_d"]
        qTdb, kTdb, vdb, onTdb = Buf("qT_d"), Buf("kT_d"), Buf("v_d"), Buf("onT_d")

        o = 0
        wqkv, wqkvb = self.load_weight(o, d["diff_w_qkv"][j], KD, 3 * D, "wqkv_", kgroup=2); o += KD * 3 * D // 2
        hTa = []
        for q in range(2):
            ap, b = self.aalloc(o, 512, BF16, f"a1hT{q}"); o += 512
            hTa.append((ap.rearrange("p (k n) -> p k n", k=KD), b))
        sq, sqb_ = self.aalloc(o, 2048, F32, "a1sq"); o += 2048
        qk32, qk32b = self.aalloc(o, 2048, F32, "a1qk32"); o += 2048
        qk32v = qk32.rearrange("p (a g e) -> p a g e", a=2, g=16)
        rt = {}
        for eng in ("dve", "pool"):
            rt[eng] = []
            for q in range(4):
                ap, b = self.aalloc(o, 512, F32, f"a1rt{eng}{q}"); o += 512
                rt[eng].append((ap.rearrange("p (g a f) -> p g a f", g=16, a=2), b))
        qkbf, qkbfb = [], []
        for a in range(2):
            ap, b = self.aalloc(o, 512, BF16, f"a1qkbf{a}"); o += 512
            qkbf.append(ap); qkbfb.append(b)
        vbf, vbfb = self.aalloc(o, 512, BF16, "a1vbf"); o += 512
        qkT = []
        for q in range(2):
            row = []
            for a in range(2):
                ap, b = self.aalloc(o, 512, BF16, f"a1qkT{q}{a}"); o += 512
                row.append((ap.rearrange("p (k n) -> p k n", k=KD), b))
            qkT.append(row)
        ropes = []
        for q in range(2):
            ap, b = self.aalloc(o, 64, F32, f"a1rope{q}"); o += 64
            ropes.append((ap.rearrange("p (cs a f) -> p cs a f", cs=2, a=2), b))
        gqk, gqkb = self.aalloc(o, 128, F32, "a1g"); o += 128
        gqk = gqk.rearrange("p (a e) -> p a e", a=2)
        ssq, ssqb = self.aalloc(o, 64, F32, "a1ssq"); o += 64
        ssq = ssq.rearrange("p (a g) -> p a g", a=2)
        P.dma(gqk[:, 0, :], d["diff_q_norm"][j].partition_broadcast(128), [], [gqkb], "aconst")
        P.dma(gqk[:, 1, :], d["diff_k_norm"][j].partition_broadcast(128), [], [gqkb], "aconst")
        P.op("dve", "tensor_scalar", [gqkb], [gqkb], out=gqk[:, 0, :], in0=gqk[:, 0, :], scalar1=64 ** -0.5, scalar2=None, op0=ALU.mult)

        qk32bs = [Buf("a1qk32_q"), Buf("a1qk32_k")]
        for b_ in qk32bs:
            b_.inherit([qk32b])
        pend_a1 = self.norm_part1(0)
        for t in range(NT):
            is_ctx = t < NTC
            hT, hTb = hTa[t % 2]
            nxt_a1 = self.norm_part1(t + 1) if t + 1 < NT else None
            self.norm_part2(pend_a1, t, 0, hT, hTb, 0, bank=6)
            pend_a1 = nxt_a1
            for blk in range(6):
                for k in range(KD):
                    P.op("pe", "matmul", [hTb, wqkvb[k]], [self.psb[blk]], out=ps[:, blk, :], lhsT=hT[:, k, :], rhs=wqkv[:, k, blk * 512:(blk + 1) * 512],
                         start=(k == 0), stop=(k == KD - 1))
            qkps = ps[:, 0:4, :].rearrange("p a n -> p (a n)")
            P.op("act", "activation", [self.psb[0], self.psb[1], self.psb[2], self.psb[3]], [sqb_], out=sq, in_=qkps, func=AF.Square)
            P.op("act", "copy", [self.psb[4], self.psb[5]], [vbfb], out=vbf, in_=ps[:, 4:6, :].rearrange("p a n -> p (a n)"))
            P.dma(v_d[t * 128:(t + 1) * 128, :], vbf, [vbfb], [vdb], "a1v")
            P.op("dve", "tensor_reduce", [sqb_], [ssqb], out=ssq[:, 0, :], in_=sq.rearrange("p (g e) -> p g e", e=64), axis=AX.X, op=ALU.add)
            P.op("act", "activation", [ssqb, self.cb], [ssqb], out=ssq[:, 1, :], in_=ssq[:, 0, :], func=AF.Sqrt, scale=1.0 / 64, bias=self.eps_t[:])
            P.op("dve", "reciprocal", [ssqb], [ssqb], out=ssq[:, 1, :], in_=ssq[:, 1, :])
            if not is_ctx:
                rp, rpb = ropes[t % 2]
                tl = t - NTC
                P.dma(rp.rearrange("p cs a f -> p (cs a f)"), d["rope_tab"][tl * 128:(tl + 1) * 128, :], [], [rpb], f"a1rope{t % 2}")
            for a in range(2):
                eng = "dve" if a == 0 else "pool"
                qk32b = qk32bs[a]
                src = ps[:, 2 * a:2 * a + 2, :].rearrange("p a (g e) -> p (a g) e", e=64)
                P.op("dve", "tensor_tensor", [self.psb[2 * a], self.psb[2 * a + 1], ssqb], [qk32b], out=qk32v[:, a], in0=src,
                     in1=ssq[:, 1, a * 16:(a + 1) * 16].unsqueeze(2).to_broadcast([128, 16, 64]), op=ALU.mult)
                gb_ = gqk[:, a, :].unsqueeze(1).to_broadcast([128, 16, 64])
                if is_ctx:
                    P.op(eng, "tensor_tensor", [qk32b, gqkb], [qkbfb[a]], out=qkbf[a].rearrange("p (g e) -> p g e", e=64), in0=qk32v[:, a], in1=gb_, op=ALU.mult)
                else:
                    P.op(eng, "tensor_tensor", [qk32b, gqkb], [qk32b], out=qk32v[:, a], in0=qk32v[:, a], in1=gb_, op=ALU.mult)
                    xv = qk32v[:, a].rearrange("p g (a h f) -> p g a h f", a=2, h=2)
                    ov = qkbf[a].rearrange("p (g a h f) -> p g a h f", g=16, a=2, h=2)
                    x1, x2 = xv[:, :, :, 0, :], xv[:, :, :, 1, :]
                    cosb = rp[:, 0].unsqueeze(1).to_broadcast([128, 16, 2, 16])
                    sinb = rp[:, 1].unsqueeze(1).to_broadcast([128, 16, 2, 16])
                    (ta, tab), (tb, tbb), (tc, tcb), (td, tdb) = rt[eng]
                    P.op(eng, "tensor_tensor", [qk32b, rpb], [tab], out=ta, in0=x1, in1=cosb, op=ALU.mult)
                    P.op(eng, "tensor_tensor", [qk32b, rpb], [tbb], out=tb, in0=x2, in1=sinb, op=ALU.mult)
                    P.op(eng, "tensor_tensor", [tab, tbb], [qkbfb[a]], out=ov[:, :, :, 0, :], in0=ta, in1=tb, op=ALU.subtract)
                    P.op(eng, "tensor_tensor", [qk32b, rpb], [tcb], out=tc, in0=x1, in1=sinb, op=ALU.mult)
                    P.op(eng, "tensor_tensor", [qk32b, rpb], [tdb], out=td, in0=x2, in1=cosb, op=ALU.mult)
                    P.op(eng, "tensor_tensor", [tcb, tdb], [qkbfb[a]], out=ov[:, :, :, 1, :], in0=tc, in1=td, op=ALU.add)
                dT, dTb = qkT[t % 2][a]
                self.transpose8(qkbf[a], qkbfb[a], dT, dTb, engine="act" if a == 0 else "dve", bank=7)
                dst = (qT_d if a == 0 else kT_d)[:, :, t * 128:(t + 1) * 128].rearrange("h p n -> p h n")
                P.dma(dst, dT, [dTb], [qTdb if a == 0 else kTdb], f"a1qk{t % 2}{a}")

        o = 0
        HW = NTOK // 2
        hb = []
        for q in range(2):
            row = {}
            for nm in ("k", "v", "q"):
                ap, b = self.aalloc(o, HW, BF16, f"a2{nm}{q}"); o += HW
                row[nm] = (ap, b)
            hb.append(row)
        PT = []
        for q in range(3):
            ap, b = self.aalloc(o, 256, BF16, f"a2PT{q}"); o += 256
            PT.append((ap, b))
        ones, onesb = self.aalloc(o, 64, BF16, "a2ones"); o += 64
        ones = ones.rearrange("p (a b) -> p a b", a=1)[:, 0, :]
        onesm, onesmb = self.aalloc(o, 64, BF16, "a2onesm"); o += 64
        f32t = {}
        for nm in ("r0", "r1", "t0", "o32", "rstd"):
            f32t[nm] = self.aalloc(o, 512, F32, f"a2{nm}"); o += 512
        sqh, sqhb = self.aalloc(o, 256, BF16, "a2sq"); o += 256
        onTs = []
        for q in range(2):
            onTs.append(self.aalloc(o, 256, BF16, f"a2onT{q}")); o += 256
        lamt, lamb = self.aalloc(o, 256 + 8, F32, "a2lam"); o += 264
        lam4 = lamt[:, 0:256].rearrange("p (a e) -> p a e", a=4)
        lsc = lamt[:, 256:264]
        subc, subcb = self.aalloc(o, 2, F32, "a2subc"); o += 2
        A2END = o
        P.op("pool", "memset", [], [onesb], ap=ones, constant=1.0)
        P.op("pool", "memset", [], [onesmb], ap=onesm, constant=1.0 / 128)
        P.dma(lam4, d["diff_lambda"][j].partition_broadcast(128), [], [lamb], "aconst")
        P.op("dve", "tensor_tensor", [lamb], [lamb], out=lam4[:, 0, :], in0=lam4[:, 0, :], in1=lam4[:, 1, :], op=ALU.mult)
        P.op("dve", "tensor_tensor", [lamb], [lamb], out=lam4[:, 2, :], in0=lam4[:, 2, :], in1=lam4[:, 3, :], op=ALU.mult)
        P.op("dve", "tensor_reduce", [lamb], [lamb], out=lsc[:, 0:1], in_=lam4[:, 0, :], axis=AX.X, op=ALU.add)
        P.op("dve", "tensor_reduce", [lamb], [lamb], out=lsc[:, 1:2], in_=lam4[:, 2, :], axis=AX.X, op=ALU.add)
        P.op("act", "activation", [lamb], [lamb], out=lsc[:, 2:4], in_=lsc[:, 0:2], func=AF.Exp)
        P.op("dve", "tensor_tensor", [lamb], [lamb], out=lsc[:, 4:5], in0=lsc[:, 3:4], in1=lsc[:, 2:3], op=ALU.subtract)
        P.op("dve", "tensor_scalar", [lamb], [lamb], out=lsc[:, 5:6], in0=lsc[:, 4:5], scalar1=-lam_init, scalar2=None, op0=ALU.add)
        P.dma(subc[:, 0:1], d["diff_subln"][j].unsqueeze(1), [], [subcb], "aconst")
        P.op("dve", "tensor_scalar", [subcb], [subcb], out=subc[:, 1:2], in0=subc[:, 0:1], scalar1=1.0 - lam_init, scalar2=None, op0=ALU.mult)
        neglam = lsc[:, 5:6]

        A3OFF = max(A2END + 1536, 20480)
        wo, wob = self.load_weight(A3OFF, d["diff_w_o"][j], KD, D, "wo_", kgroup=8)

        qblocks = []
        if ctx_out:
            qblocks.append((0, cfg.n_ctx, list(range(NTC))))
        for b0 in range(0, cfg.n_lat, 512):
            qblocks.append((cfg.n_ctx + b0, min(512, cfg.n_lat - b0), list(range(NT))))
        STAGES = [(0, 1), (6, 7)]
        MSB = 5
        (r0, r0b), (r1, r1b), (t0, t0b), (o32, o32b), (rstd, rstdb) = (f32t[n] for n in ("r0", "r1", "t0", "o32", "rstd"))

        def load_head(h):
            kh, khb = hb[h % 2]["k"]
            vh, vhb = hb[h % 2]["v"]
            qh, qhb = hb[h % 2]["q"]
            vh3 = vh.rearrange("p (t e) -> p t e", e=128)
            P.dma(kh, kT_d[h], [kTdb], [khb], f"a2k{h % 2}")
            P.dma(qh, qT_d[h], [qTdb], [qhb], f"a2q{h % 2}")
            P.dma(vh3, v_d[:, h * 128:(h + 1) * 128].rearrange("(t p) e -> p t e", p=128), [vdb], [vhb], f"a2v{h % 2}")

        items = []
        for h in range(8):
            for qi, (q0, nq, ktiles) in enumerate(qblocks):
                for c in range(2):
                    for g0 in range(0, len(ktiles), 2):
                        items.append((h, qi, c, g0, ktiles[g0:g0 + 2], len(ktiles)))
        PTP = []
        o2 = A2END
        for q in range(3):
            PTP.append(self.aalloc(o2, 512, BF16, f"a2PTP{q}")); o2 += 512
        A2END = o2
        oni = [0]

        def issue_S(i):
            h, qi, c, g0, kts, nk = items[i]
            q0, nq, _ = qblocks[qi]
            kh, khb = hb[h % 2]["k"]
            qh, qhb = hb[h % 2]["q"]
            for u, kt in enumerate(kts):
                bank = STAGES[i % 2][u]
                P.op("pe", "matmul", [khb, qhb], [self.psb[bank]], out=ps[:, bank, :nq], lhsT=kh[c * 64:(c + 1) * 64, kt * 128:(kt + 1) * 128],
                     rhs=qh[c * 64:(c + 1) * 64, q0:q0 + nq], start=True, stop=True)

        def issue_rest(i):
            h, qi, c, g0, kts, nk = items[i]
            q0, nq, _ = qblocks[qi]
            vh, vhb = hb[h % 2]["v"]
            vh3 = vh.rearrange("p (t e) -> p t e", e=128)
            b0 = STAGES[i % 2][0]
            ng = len(kts)
            pt, ptb = PTP[i % 3]
            pt3 = pt.rearrange("p (u n) -> p u n", u=2)
            ob, zb = 2 + 2 * c, 3 + 2 * c
            P.op("act", "activation", [self.psb[STAGES[i % 2][u]] for u in range(ng)], [ptb], out=pt3[:, 0:ng, :nq], in_=ps[:, b0:b0 + ng, :nq], func=AF.Exp)
            for u, kt in enumerate(kts):
                P.op("pe", "matmul", [ptb, vhb], [self.psb[ob]], out=ps[:, ob, :nq], lhsT=vh3[:, kt, :], rhs=pt3[:, u, :nq], start=(g0 + u == 0), stop=(g0 + u == nk - 1))
            for u, kt in enumerate(kts):
                P.op("pe", "matmul", [ptb, onesb], [self.psb[zb]], out=ps[:, zb, :nq], lhsT=ones, rhs=pt3[:, u, :nq], start=(g0 + u == 0), stop=(g0 + u == nk - 1))
            if g0 + ng < nk:
                return
            if c == 0:
                P.op("dve", "reciprocal", [self.psb[3]], [r0b], out=r0[:, :nq], in_=ps[:, 3, :nq])
                P.op("dve", "tensor_tensor", [self.psb[2], r0b], [t0b], out=t0[:, :nq], in0=ps[:, 2, :nq], in1=r0[:, :nq], op=ALU.mult)
                return
            P.op("dve", "reciprocal", [self.psb[5]], [r1b], out=r1[:, :nq], in_=ps[:, 5, :nq])
            P.op("dve", "tensor_tensor", [self.psb[4], r1b], [r1b], out=r1[:, :nq], in0=ps[:, 4, :nq], in1=r1[:, :nq], op=ALU.mult)
            P.op("dve", "scalar_tensor_tensor", [r1b, t0b, lamb], [o32b], out=o32[:, :nq], in0=r1[:, :nq], scalar=neglam, in1=t0[:, :nq], op0=ALU.mult, op1=ALU.add)
            P.op("dve", "tensor_tensor", [o32b], [sqhb], out=sqh[:, :nq], in0=o32[:, :nq], in1=o32[:, :nq], op=ALU.mult)
            P.op("pe", "matmul", [sqhb, onesmb], [self.psb[MSB]], out=ps[:, MSB, :nq], lhsT=onesm, rhs=sqh[:, :nq], start=True, stop=True)
            P.op("act", "activation", [self.psb[MSB], self.cb], [rstdb], out=rstd[:, :nq], in_=ps[:, MSB, :nq], func=AF.Ln, bias=self.eps_t[:])
            P.op("act", "activation", [rstdb], [rstdb], out=rstd[:, :nq], in_=rstd[:, :nq], func=AF.Exp, scale=-0.5)
            onT, onTb = onTs[oni[0] % 2]
            oni[0] += 1
            P.op("dve", "scalar_tensor_tensor", [o32b, rstdb, subcb], [onTb], out=onT[:, :nq], in0=o32[:, :nq], scalar=subc[:, 1:2], in1=rstd[:, :nq],
                 op0=ALU.mult, op1=ALU.mult)
            P.dma(onT_d[h, :, q0:q0 + nq], onT[:, :nq], [onTb], [onTdb], f"a2on{oni[0] % 2}")

        load_head(0)
        load_head(1)
        per_head = len(items) // 8
        DEPTH = 1
        for i in range(len(items) + DEPTH):
            if i < len(items):
                issue_S(i)
            jx = i - DEPTH
            if jx >= 0:
                issue_rest(jx)
                if jx % per_head == per_head - 1 and jx // per_head + 2 < 8:
                    load_head(jx // per_head + 2)

        o = A3OFF + KD * D // 2
        onb = []
        for q in range(2):
            ap, b = self.aalloc(o, 2048, BF16, f"a3on{q}"); o += 2048
            onb.append((ap.rearrange("p (h n) -> p h n", h=8), b))
        tiles = list(range(0 if ctx_out else NTC, NT))
        blocks = [tiles[a:a + 4] for a in range(0, len(tiles), 4)]
        for bi, blk in enumerate(blocks):
            nb = len(blk) * 128
            tok0 = blk[0] * 128
            on, onbb = onb[bi % 2]
            P.dma(on[:, :, :nb], onT_d[:, :, tok0:tok0 + nb].rearrange("h p n -> p h n"), [onTdb], [onbb], f"a3on{bi % 2}")
            for q, t in enumerate(blk):
                st = self.resid_begin(t)
                for hf in range(2):
                    bank = (2 * t + hf) % 4
                    for hh in range(8):
                        P.op("pe", "matmul", [onbb, wob[hh]], [self.psb[bank]], out=ps[:, bank, :], lhsT=on[:, hh, q * 128:(q + 1) * 128],
                             rhs=wo[:, hh, hf * 512:(hf + 1) * 512], start=(hh == 0), stop=(hh == 7))
                    self.resid_half(st, 0, hf, ps[:, bank, :], self.psb[bank])
                self.resid_end(st)

    def cmul(self, eng, o_r, o_i, a_r, a_i, b_r, b_i, t1, t2, rb, wb, neg_im=False):
        P = self.P
        P.op(eng, "tensor_tensor", rb, wb, out=t1, in0=a_r, in1=b_r, op=ALU.mult)
        P.op(eng, "tensor_tensor", rb, wb, out=t2, in0=a_i, in1=b_i, op=ALU.mult)
        P.op(eng, "tensor_tensor", rb + wb, wb, out=o_r, in0=t1, in1=t2, op=ALU.subtract)
        P.op(eng, "tensor_tensor", rb + wb, wb, out=t1, in0=a_r, in1=b_i, op=ALU.mult)
        P.op(eng, "tensor_tensor", rb + wb, wb, out=t2, in0=a_i, in1=b_r, op=ALU.mult)
        if neg_im:
            P.op(eng, "scalar_tensor_tensor", rb + wb, wb, out=o_i, in0=t1, scalar=-1.0, in1=t2, op0=ALU.mult, op1=ALU.subtract)
        else:
            P.op(eng, "tensor_tensor", rb + wb, wb, out=o_i, in0=t1, in1=t2, op=ALU.add)

    def s5_phase(self, i, last):
        P = self.P
        cfg = self.cfg
        d = self.dram
        ps = self.ps
        j = i // 3
        NT, NTC, NTOK = cfg.nt, cfg.nt_ctx, cfg.ntok
        NCH = NTOK // 8
        NCC = cfg.n_ctx // 8
        NCL = cfg.n_lat // 8
        L1 = 32
        NL1 = NCH // L1
        assert NCH % L1 == 0 and NCC == 32 and NCL <= 512
        ctx_out = not last
        hT_d = d["hT_d"]
        G = 64
        TWO_PI = 2.0 * math.pi

        o = 0
        hTa = []
        for q in range(2):
            ap, b = self.aalloc(o, 512, BF16, f"s0hT{q}"); o += 512
            hTa.append((ap.rearrange("p (k n) -> p k n", k=KD), b))
        S0END = o

        P.record_begin()
        o = S0END
        ot = self.ARENA_WORDS - 13600
        tb = Buf("s5tabs")

        def talloc(words, dtype=F32, tmp=False):
            nonlocal o, ot
            off = ot if tmp else o
            ap, b = self.aalloc(off, words, dtype, "s5t")
            if tmp:
                ot += words
            else:
                o += words
            tb.inherit([b])
            self.alive = [(a_, n_, b_) for (a_, n_, b_) in self.alive if b_ is not b] + [(off, words, tb)]
            return ap

        Jc = talloc(8 * 240 // 2, BF16).rearrange("p (a c) -> p a c", a=8)
        mask01 = talloc(NCH)
        maskFB = talloc(256).rearrange("p (a n) -> p a n", a=2)
        id32 = talloc(128)
        negpi = talloc(2)
        P.dma(Jc, d["s5_J"].rearrange("a p c -> p a c"), [], [tb], "s5c")
        P.dma(mask01, d["s5_mask01"].partition_broadcast(128), [], [tb], "s5c")
        P.dma(maskFB, d["s5_maskFB"].rearrange("a p n -> p a n"), [], [tb], "s5c")
        P.dma(id32, d["ident32"], [], [tb], "s5c")
        P.op("dve", "memset", [], [tb], ap=negpi[:, 0:1], constant=-math.pi * (1 - 1e-6))
        lamr = talloc(G, tmp=True); lami = talloc(G, tmp=True); dt = talloc(G, tmp=True)
        dcol = talloc(G)
        P.dma(lamr, d["s5_lamr_t"][j], [], [tb], "s5c")
        P.dma(lami, d["s5_lami_t"][j], [], [tb], "s5c")
        for dr in range(2):
            P.dma(dt[dr * 64:(dr + 1) * 64, :], d["s5_log_dt"][j, dr].partition_broadcast(64), [], [tb], "s5c")
        P.dma(dcol, d["s5_dcol"][j], [], [tb], "s5c")
        T = [tb]

        def tt(out, in0, in1, op):
            P.op("dve", "tensor_tensor", T, T, out=out, in0=in0, in1=in1, op=op)

        def ts(out, in0, s1, op0, s2=None, op1=None):
            if op1 is None:
                P.op("dve", "tensor_scalar", T, T, out=out, in0=in0, scalar1=s1, scalar2=None, op0=op0)
            else:
                P.op("dve", "tensor_scalar", T, T, out=out, in0=in0, scalar1=s1, scalar2=s2, op0=op0, op1=op1)

        def act(out, in_, func, **kw):
            P.op("act", "activation", T, T, out=out, in_=in_, func=func, **kw)

        sc = [talloc(G * 32, tmp=True) for _ in range(2)]
        sc1 = [talloc(G, tmp=True) for _ in range(6)]
        lrd = talloc(G, tmp=True); lid = talloc(G, tmp=True); mag = talloc(G, tmp=True)
        ar = talloc(G, tmp=True); ai = talloc(G, tmp=True)
        kr = talloc(G, tmp=True); ki = talloc(G, tmp=True)
        def stt(out, in0, scalar, in1, op0, op1):
            P.op("dve", "scalar_tensor_tensor", T, T, out=out, in0=in0, scalar=scalar, in1=in1, op0=op0, op1=op1)

        def exp_taylor(out, x, deg):
            r_ = sc1[0]
            P.op("dve", "memset", T, T, ap=r_, constant=1.0)
            for k in range(deg, 0, -1):
                tt(r_, r_, x, ALU.mult)
                ts(r_, r_, 1.0 / k, ALU.mult, 1.0, ALU.add)
            P.op("dve", "tensor_copy", T, T, out=out, in_=r_)

        ts(dt, dt, 0.125, ALU.mult)
        exp_taylor(dt, dt, 11)
        for _ in range(3):
            tt(dt, dt, dt, ALU.mult)
        tt(lrd, lamr, dt, ALU.mult)
        tt(lid, lami, dt, ALU.mult)
        exp_taylor(mag, lrd, 7)
        iscr = talloc(G, I32, tmp=True)

        def sin_of(out, theta, offs):
            y, n_, fr, lt = sc1[0], sc1[1], sc1[2], sc1[3]
            ts(y, theta, 1.0 / TWO_PI, ALU.mult, offs, ALU.add)
            P.op("dve", "tensor_copy", T, T, out=iscr, in_=y)
            P.op("dve", "tensor_copy", T, T, out=n_, in_=iscr)
            tt(fr, y, n_, ALU.subtract)
            ts(lt, fr, 0.0, ALU.is_lt)
            tt(fr, fr, lt, ALU.add)
            ts(lt, fr, 1.0, ALU.is_ge)
            tt(fr, fr, lt, ALU.subtract)
            xx, x2, r_ = sc1[0], sc1[1], sc1[3]
            ts(xx, fr, TWO_PI, ALU.mult, -math.pi, ALU.add)
            tt(x2, xx, xx, ALU.mult)
            coef = [(-1.0) ** k / math.factorial(2 * k + 1) for k in range(12)]
            ts(r_, x2, coef[11], ALU.mult)
            for k in range(10, 0, -1):
                stt(r_, r_, coef[k], x2, ALU.add, ALU.mult)
            stt(out, r_, coef[0], xx, ALU.add, ALU.mult)

        sn = sc1[4]; cs_ = sc1[5]
        sin_of(sn, lid, 8.5)
        sin_of(cs_, lid, 8.75)
        tt(ar, mag, cs_, ALU.mult)
        tt(ai, mag, sn, ALU.mult)
        nre, den, t_a, t_b = sc1[0], sc1[1], sc1[2], sc1[3]
        ts(nre, ar, -1.0, ALU.add)
        tt(den, lamr, lamr, ALU.mult)
        tt(t_a, lami, lami, ALU.mult)
        tt(den, den, t_a, ALU.add)
        P.op("dve", "reciprocal", T, T, out=den, in_=den)
        tt(t_a, nre, lamr, ALU.mult)
        tt(t_b, ai, lami, ALU.mult)
        tt(t_a, t_a, t_b, ALU.add)
        tt(kr, t_a, den, ALU.mult)
        tt(t_a, ai, lamr, ALU.mult)
        tt(t_b, nre, lami, ALU.mult)
        tt(t_a, t_a, t_b, ALU.subtract)
        tt(ki, t_a, den, ALU.mult)
        apr = talloc(G * 9, tmp=True).rearrange("p (g e) -> p g e", g=G); api = talloc(G * 9, tmp=True).rearrange("p (g e) -> p g e", g=G)
        anr = talloc(G * 9, tmp=True).rearrange("p (g e) -> p g e", g=G); ani = talloc(G * 9, tmp=True).rearrange("p (g e) -> p g e", g=G)
        ainr, aini = talloc(G, tmp=True), talloc(G, tmp=True)
        tt(t_a, ar, ar, ALU.mult)
        tt(t_b, ai, ai, ALU.mult)
        tt(t_a, t_a, t_b, ALU.add)
        P.op("dve", "reciprocal", T, T, out=t_a, in_=t_a)
        tt(ainr, ar, t_a, ALU.mult)
        P.op("dve", "scalar_tensor_tensor", T, T, out=aini, in0=ai, scalar=-1.0, in1=t_a, op0=ALU.mult, op1=ALU.mult)
        for (pr, pi, br_, bi_) in ((apr, api, ar, ai), (anr, ani, ainr, aini)):
            P.op("dve", "memset", T, T, ap=pr[:, :, 0], constant=1.0)
            P.op("dve", "memset", T, T, ap=pi[:, :, 0], constant=0.0)
            P.op("dve", "tensor_copy", T, T, out=pr[:, :, 1], in_=br_)
            P.op("dve", "tensor_copy", T, T, out=pi[:, :, 1], in_=bi_)
            for e in range(2, 9):
                self.cmul("dve", pr[:, :, e], pi[:, :, e], pr[:, :, e - 1], pi[:, :, e - 1], br_, bi_, sc1[0], sc1[1], T, T)
        def t8(tmp=False):
            return talloc(G * 8, tmp=tmp).rearrange("p (g e) -> p g e", g=G)
        sXr, sXi = t8(True), t8(True)
        AYr, AYi, ATr, ATi, AKr, AKi = [t8() for _ in range(6)]
        F_, B_ = slice(0, 64), slice(64, 128)
        for (dst, src) in ((sXr, apr), (sXi, api)):
            P.op("act", "copy", T, T, out=dst[B_, :, :], in_=src[B_, :, 0:8])
            for e in range(8):
                P.op("act", "copy", T, T, out=dst[F_, :, e], in_=src[F_, :, 7 - e])
        for (dst, src) in ((AYr, apr), (AYi, api)):
            P.op("act", "copy", T, T, out=dst[F_, :, :], in_=src[F_, :, 1:9])
            for e in range(8):
                P.op("act", "copy", T, T, out=dst[B_, :, e], in_=src[B_, :, 8 - e])
        for (dst, src) in ((ATr, anr), (ATi, ani)):
            P.op("act", "copy", T, T, out=dst[B_, :, :], in_=src[B_, :, 0:8])
            for e in range(8):
                P.op("act", "copy", T, T, out=dst[F_, :, e], in_=src[F_, :, 7 - e])
        s8a = sc[0][:, 0:G * 8].rearrange("p (g e) -> p g e", g=G)
        s8b = sc[1][:, 0:G * 8].rearrange("p (g e) -> p g e", g=G)
        kbr = kr.unsqueeze(2).to_broadcast([128, G, 8])
        kbi = ki.unsqueeze(2).to_broadcast([128, G, 8])
        self.cmul("dve", AKr, AKi, sXr, sXi, kbr, kbi, s8a, s8b, T, T)
        def t32(tmp=False):
            return talloc(G * L1, tmp=tmp).rearrange("p (g e) -> p g e", g=G)
        Ppr, Ppi = t32(True), t32(True)
        Qr, Qi = t32(), t32()
        A32r, A32i = talloc(G, tmp=True), talloc(G, tmp=True)
        s32a = sc[0].rearrange("p (g e) -> p g e", g=G)
        s32b = sc[1].rearrange("p (g e) -> p g e", g=G)
        sqr, sqi = talloc(G, tmp=True), talloc(G, tmp=True)

        def build_pow(pr, pi, base_r, base_i, nmax, final_sq=None):
            P.op("dve", "memset", T, T, ap=pr[:, :, 0], constant=1.0)
            P.op("dve", "memset", T, T, ap=pi[:, :, 0], constant=0.0)
            P.op("dve", "tensor_copy", T, T, out=pr[:, :, 1], in_=base_r)
            P.op("dve", "tensor_copy", T, T, out=pi[:, :, 1], in_=base_i)
            P.op("dve", "tensor_copy", T, T, out=sqr, in_=base_r)
            P.op("dve", "tensor_copy", T, T, out=sqi, in_=base_i)
            n = 2
            while True:
                self.cmul("dve", sc1[2], sc1[3], sqr, sqi, sqr, sqi, sc1[0], sc1[1], T, T)
                P.op("dve", "tensor_copy", T, T, out=sqr, in_=sc1[2])
                P.op("dve", "tensor_copy", T, T, out=sqi, in_=sc1[3])
                if n >= nmax:
                    break
                m = min(n, nmax - n)
                self.cmul("dve", pr[:, :, n:n + m], pi[:, :, n:n + m], pr[:, :, 0:m], pi[:, :, 0:m],
                          sqr.unsqueeze(2).to_broadcast([128, G, m]), sqi.unsqueeze(2).to_broadcast([128, G, m]),
                          s32a[:, :, 0:m], s32b[:, :, 0:m], T, T)
                n *= 2
            if final_sq is not None:
                P.op("dve", "tensor_copy", T, T, out=final_sq[0], in_=sqr)
                P.op("dve", "tensor_copy", T, T, out=final_sq[1], in_=sqi)

        build_pow(Qr, Qi, anr[:, :, 8], ani[:, :, 8], L1)
        build_pow(Ppr, Ppi, apr[:, :, 8], api[:, :, 8], L1, final_sq=(A32r, A32i))
        a8ir = anr[:, :, 8].unsqueeze(2).to_broadcast([128, G, L1])
        a8ii = ani[:, :, 8].unsqueeze(2).to_broadcast([128, G, L1])
        tmpPr, tmpPi = t32(), t32()
        self.cmul("dve", tmpPr, tmpPi, Ppr, Ppi, a8ir, a8ii, s32a, s32b, T, T)
        Ppr, Ppi = tmpPr, tmpPi
        NM = NL1 + 1
        rr = talloc(G, tmp=True); ur = talloc(G, tmp=True); ui = talloc(G, tmp=True)
        act(rr, lrd, AF.Exp, scale=float(8 * L1))
        P.op("dve", "reciprocal", T, T, out=t_a, in_=rr)
        tt(ur, A32r, t_a, ALU.mult)
        tt(ui, A32i, t_a, ALU.mult)
        NMP = NM
        Umr = talloc(G * NMP).rearrange("p (g e) -> p g e", g=G); Umi = talloc(G * NMP).rearrange("p (g e) -> p g e", g=G)
        build_pow(Umr, Umi, ur, ui, NMP)
        Vr = talloc(G * NL1).rearrange("p (g e) -> p g e", g=G); Vi = talloc(G * NL1).rearrange("p (g e) -> p g e", g=G)
        rmask = talloc(G * NL1).rearrange("p (g e) -> p g e", g=G)
        rb3 = rr.unsqueeze(2).to_broadcast([128, G, NL1])
        tt(Vr, Umr[:, :, 0:NL1], rb3, ALU.mult)
        P.op("dve", "scalar_tensor_tensor", T, T, out=Vi, in0=Umi[:, :, 0:NL1], scalar=-1.0, in1=rb3, op0=ALU.mult, op1=ALU.mult)
        P.op("dve", "tensor_copy", T, T, out=rmask, in_=rb3)
        P.op("dve", "memset", T, T, ap=rmask[:, :, 0], constant=0.0)
        l2 = talloc(10 * NL1 + 4)
        l2B = talloc(10 * NL1 + 4)
        TABEND = o
        rec = P.record_end()
        per_tile = (len(rec) + NT - 1) // NT
        pend = self.norm_part1(0)
        for t in range(NT):
            hT, hTb = hTa[t % 2]
            nxt_ = self.norm_part1(t + 1) if t + 1 < NT else None
            self.norm_part2(pend, t, 0, hT, hTb, 0, bank=7)
            pend = nxt_
            P.dma(hT_d[:, :, t * 128:(t + 1) * 128].rearrange("k p n -> p k n"), hT, [hTb], [self.hTdb], f"s0st{t % 2}")
            P.replay(rec, per_tile)
        P.replay(rec, len(rec))
        stage = getattr(cfg, "s5_stage", 99)
        if getattr(cfg, "s5_dump", False):
            for nm, ap_, w in (("ar", ar, G), ("ai", ai, G), ("kr", kr, G), ("ki", ki, G), ("AKr", AKr, G * 8), ("AKi", AKi, G * 8), ("AYr", AYr, G * 8), ("ATi", ATi, G * 8),
                               ("Qr", Qr, G * L1), ("Qi", Qi, G * L1), ("Ppr", Ppr, G * L1), ("Ppi", Ppi, G * L1), ("Umr", Umr, G * NM), ("Umi", Umi, G * NM),
                               ("Vr", Vr, G * NL1), ("Vi", Vi, G * NL1), ("rmask", rmask, G * NL1)):
                flat = ap_ if len(ap_.shape) == 2 else ap_.rearrange("p g e -> p (g e)")
                self.dump("dbg_" + nm, flat, tb, [128, w])
        if stage <= 1:
            return

        hTf, hTfb = self.aalloc(o, NTOK // 2, BF16, "s2hT"); o += NTOK // 2
        Up, Upb = self.aalloc(o, 8 * NCH // 2, BF16, "s2U"); o += 8 * NCH // 2
        Up = Up.rearrange("p (g n) -> p g n", g=8)
        zp, zpb = self.aalloc(o, 8 * NCH // 2, BF16, "s2z"); o += 8 * NCH // 2
        zp = zp.rearrange("p (g n) -> p g n", g=8)
        wts = {}
        for nm in ("zr", "zi", "yr", "yi", "tg", "yrB", "yiB"):
            ap, b = self.aalloc(o, 512, BF16, f"s2w{nm}"); o += 512
            wts[nm] = (ap.rearrange("p (g n) -> p g n", g=8), b)
        for nm in ("yrB", "yiB"):
            P.op("dve", "memset", [], [wts[nm][1]], ap=wts[nm][0][0:64, :, :], constant=0.0)
        gt = {}
        for nm in ("xr", "xi", "ytr", "yti", "m1", "m2"):
            ap, b = self.aalloc(o, 1024, F32, f"s2g{nm}"); o += 1024
            gt[nm] = (ap, b)
        ch = []
        for q in range(4):
            ap, b = self.aalloc(o, NCH + 1, F32, f"s2c{q}"); o += NCH + 1
            ch.append((ap, b))
        Eb = []
        for q in range(2):
            ap, b = self.aalloc(o, NCH // 2, BF16, f"s2E{q}"); o += NCH // 2
            Eb.append((ap, b))
        bcft, bcftb = self.aalloc(o, 4 * 128, F32, "s2bc"); o += 512
        bcft = bcft.rearrange("p (a g h) -> p a g h", a=4, g=8)
        P.op("dve", "memset", [], [ch[0][1]], ap=ch[0][0][:, 0:1], constant=0.0)
        P.op("dve", "memset", [], [ch[2][1]], ap=ch[2][0][:, 0:1], constant=0.0)

        def seg_split(s0, s1, flat0):
            out = []
            s = s0
            while s < s1:
                f = flat0 + s
                bank, c0 = f // 512, f % 512
                n = min(s1 - s, 512 - c0)
                out.append((bank, c0, n, s))
                s += n
            return out

        def bwd_cols(s_start, n, seg_lo, seg_hi):
            k0 = seg_lo + seg_hi - 1 - s_start
            k1 = k0 - n
            return k0, (k1 if k1 >= 0 else None)

        segs = [(0, NCC), (NCC, NCH)]
        ZB0 = 0
        YB = 3

        rec_unpack_prev = None
        pbk = 0
        for ft in range(KD):
            gsl = slice(ft * 8, ft * 8 + 8)
            P.record_begin()
            P.dma(hTf, hT_d[ft], [self.hTdb], [hTfb], "s2ld")
            rec_load = P.record_end()
            P.record_begin()
            (xr, xrb), (xi, xib), (ytr, ytrb), (yti, ytib), (m1, m1b), (m2, m2b) = (gt[n] for n in ("xr", "xi", "ytr", "yti", "m1", "m2"))
            v4 = lambda ap: ap.rearrange("p (g i h) -> p g i h", g=8, i=8)
            for a_, nm_ in enumerate(("s5_br_t", "s5_bi_t", "s5_cr_t", "s5_ci_t")):
                P.dma(bcft[:, a_], d[nm_][j][:, gsl, :], [], [bcftb], "s2bc")
            bc_e = lambda tab: tab[:, gsl, :].unsqueeze(3).to_broadcast([128, 8, 8, 16])
            bc_h = lambda a_: bcft[:, a_].unsqueeze(2).to_broadcast([128, 8, 8, 16])
            Br, Bi, Cr, Ci = 0, 1, 2, 3
            T2 = [tb, bcftb]
            self.cmul("dve", v4(xr), v4(xi), bc_e(AKr), bc_e(AKi), bc_h(Br), bc_h(Bi), v4(m1), v4(m2), T2, [xrb, xib, m1b, m2b])
            (wyr, wyrb), (wyi, wyib) = wts["yr"], wts["yi"]
            self.cmul("dve", v4(wyr.rearrange("p g n -> p (g n)")), v4(wyi.rearrange("p g n -> p (g n)")), bc_e(AYr), bc_e(AYi), bc_h(Cr), bc_h(Ci),
                      v4(m1), v4(m2), T2, [wyrb, wyib, m1b, m2b], neg_im=True)
            for (src_, srcb_, nmB) in ((wyr, wyrb, "yrB"), (wyi, wyib, "yiB")):
                wB, wBb = wts[nmB]
                P.op("act", "copy", [srcb_], [wBb], out=wB[64:128, :, :], in_=src_[64:128, :, :])
                P.op("dve", "memset", [wBb], [srcb_], ap=src_[64:128, :, :], constant=0.0)
            self.cmul("dve", v4(ytr), v4(yti), bc_e(ATr), bc_e(ATi), bc_h(Cr), bc_h(Ci), v4(m1), v4(m2), T2, [ytrb, ytib, m1b, m2b], neg_im=True)
            rec_wg_dve = P.record_end()
            P.record_begin()
            for (src, srcb, nm) in ((xr, xrb, "zr"), (xi, xib, "zi")):
                wz, wzb = wts[nm]
                for g in range(8):
                    P.op("pe", "transpose", [srcb, tb], [self.psb[5 + g // 4]], out=ps[:, 5 + g // 4, (g % 4) * 128:(g % 4 + 1) * 128],
                         in_=src[:, g * 128:(g + 1) * 128], identity=id32)
                P.op("act", "copy", [self.psb[5], self.psb[6]], [wzb], out=wz.rearrange("p g n -> p (g n)"), in_=ps[:, 5:7, :].rearrange("p a n -> p (a n)"))
            tg, tgb = wts["tg"]
            for hb_ in range(2):
                for (dr, bank) in ((0, 5), (1, 6)):
                    psl = slice(dr * 64, dr * 64 + 64)
                    for g4 in range(4):
                        g = hb_ * 4 + g4
                        outp = ps[:, bank, g4 * 128:(g4 + 1) * 128]
                        P.op("pe", "matmul", [xrb, ytrb], [self.psb[bank]], out=outp, lhsT=xr[psl, g * 128:(g + 1) * 128], rhs=ytr[psl, g * 128:(g + 1) * 128], start=True, stop=False)
                        P.op("pe", "matmul", [xib, ytib], [self.psb[bank]], out=outp, lhsT=xi[psl, g * 128:(g + 1) * 128], rhs=yti[psl, g * 128:(g + 1) * 128], start=False, stop=True)
                mF = maskFB[:, 0, :].unsqueeze(1).to_broadcast([128, 4, 128])
                mB = maskFB[:, 1, :].unsqueeze(1).to_broadcast([128, 4, 128])
                m1v = m1[:, 0:512].rearrange("p (g n) -> p g n", g=4)
                m2v = m2[:, 0:512].rearrange("p (g n) -> p g n", g=4)
                P.op("dve", "tensor_tensor", [self.psb[5], tb], [m1b], out=m1v, in0=ps[:, 5, :].rearrange("p (g n) -> p g n", g=4), in1=mF, op=ALU.mult)
                P.op("dve", "tensor_tensor", [self.psb[6], tb], [m2b], out=m2v, in0=ps[:, 6, :].rearrange("p (g n) -> p g n", g=4), in1=mB, op=ALU.mult)
                P.op("dve", "tensor_tensor", [m1b, m2b], [tgb], out=tg[:, hb_ * 4:(hb_ + 1) * 4, :], in0=m1v, in1=m2v, op=ALU.add)
            rec_wg_pe = P.record_end()
            P.record_begin()
            hv = hTf.rearrange("p (k i) -> p k i", i=8)
            for gg in range(8):
                for (lo, hi) in segs:
                    n = hi - lo
                    bank = (7, 5, 6, 0, 1, 2)[pbk % 6]
                    pbk += 1
                    for ii in range(8):
                        P.op("pe", "matmul", [hTfb, tb], [self.psb[bank]], out=ps[:, bank, 0:n], lhsT=Jc[:, gg, 112 - 16 * ii:240 - 16 * ii], rhs=hv[:, lo:hi, ii],
                             start=(ii == 0), stop=(ii == 7))
                    P.op("act", "copy", [self.psb[bank]], [Upb], out=Up[:, gg, lo:hi], in_=ps[:, bank, 0:n])
            rec_pack = P.record_end()
            if rec_unpack_prev is not None:
                P.replay(rec_unpack_prev, len(rec_unpack_prev))
            P.replay(rec_wg_dve, len(rec_wg_dve))
            P.replay(rec_load, len(rec_load))
            P.replay(rec_pack, len(rec_pack))
            P.replay(rec_wg_pe, len(rec_wg_pe))
            wzr, wzrb = wts["zr"]
            wzi, wzib = wts["zi"]
            psflat = ps.rearrange("p a n -> p (a n)")
            setA = dict(ch=ch, Eb=Eb, l2=l2, zb0=0, zbanks=[0, 1, 2], memset=False)
            chB = [(xr[:, 0:NCH + 1], xrb), (xi[:, 0:NCH + 1], xib), (ytr[:, 0:NCH + 1], ytrb), (yti[:, 0:NCH + 1], ytib)]
            m1bf = m1.bitcast(BF16)
            EbB = [(m1bf[:, 0:NCH], m1b), (m1bf[:, NCH:2 * NCH], m1b)]
            setB = dict(ch=chB, Eb=EbB, l2=l2B, zb0=5 * 512, zbanks=[5, 6, 7], memset=True)

            def chain(gg, S):
                g = ft * 8 + gg
                ZB = S["zb0"]
                (c0_, c0b), (c1_, c1b), (c2_, c2b), (c3_, c3b) = S["ch"]
                if S["memset"]:
                    P.op("dve", "memset", [], [c0b], ap=c0_[:, 0:1], constant=0.0)
                    P.op("dve", "memset", [], [c2b], ap=c2_[:, 0:1], constant=0.0)
                for (part, (wz, wzb)) in enumerate(((wzr, wzrb), (wzi, wzib))):
                    flat0 = ZB + part * NCH
                    for (lo, hi) in segs:
                        for (bank, c0, n, s_start) in seg_split(lo, hi, flat0):
                            P.op("pe", "matmul", [wzb, Upb], [self.psb[bank]], out=ps[0:64, bank, c0:c0 + n], lhsT=wz[:, gg, 0:64], rhs=Up[:, gg, s_start:s_start + n],
                                 start=True, stop=True)
                            k0, k1 = bwd_cols(s_start - 0, n, lo, hi)
                            P.op("pe", "matmul", [wzb, Upb], [self.psb[bank]], out=ps[64:128, bank, c0:c0 + n], lhsT=wz[:, gg, 64:128], rhs=Up[:, gg, k0:k1:-1],
                                 start=True, stop=True)
                yield
                Zr = psflat[:, ZB:ZB + NCH].rearrange("p (m e) -> p m e", e=L1)
                Zi = psflat[:, ZB + NCH:ZB + 2 * NCH].rearrange("p (m e) -> p m e", e=L1)
                zbufs = [self.psb[q] for q in S["zbanks"]]
                v3 = lambda ap: ap[:, 1:NCH + 1].rearrange("p (m e) -> p m e", e=L1)
                qb_r = Qr[:, g, :].unsqueeze(1).to_broadcast([128, NL1, L1])
                qb_i = Qi[:, g, :].unsqueeze(1).to_broadcast([128, NL1, L1])
                P.op("dve", "tensor_tensor", zbufs + [tb], [c0b], out=v3(c0_), in0=Zr, in1=qb_r, op=ALU.mult); yield
                P.op("dve", "tensor_tensor", zbufs + [tb], [c1b], out=v3(c1_), in0=Zi, in1=qb_i, op=ALU.mult); yield
                P.op("dve", "tensor_tensor", [c0b, c1b], [c0b], out=v3(c0_), in0=v3(c0_), in1=v3(c1_), op=ALU.subtract); yield
                P.op("dve", "tensor_tensor", zbufs + [tb], [c2b], out=v3(c2_), in0=Zi, in1=qb_r, op=ALU.mult); yield
                P.op("dve", "tensor_tensor", zbufs + [tb], [c3b], out=v3(c3_), in0=Zr, in1=qb_i, op=ALU.mult); yield
                P.op("dve", "tensor_tensor", [c2b, c3b], [c2b], out=v3(c2_), in0=v3(c2_), in1=v3(c3_), op=ALU.add); yield
                P.op("dve", "tensor_tensor_scan", [c0b, tb], [c1b], out=c1_[:, 1:NCH + 1], data0=c0_[:, 0:NCH], data1=mask01, initial=0.0, op0=ALU.add, op1=ALU.mult); yield
                P.op("dve", "tensor_tensor_scan", [c2b, tb], [c3b], out=c3_[:, 1:NCH + 1], data0=c2_[:, 0:NCH], data1=mask01, initial=0.0, op0=ALU.add, op1=ALU.mult); yield
                Xr, Xi, Wr, Wi = v3(c1_), v3(c3_), v3(c0_), v3(c2_)
                l2_ = S["l2"]
                Dr, Di, Vvr, Vvi, Gr, Gi, Cr2, Ci2, t5, t6 = (l2_[:, q * NL1:(q + 1) * NL1] for q in range(10))
                Lb = S.setdefault("l2buf", Buf("l2"))
                L = [Lb]
                P.op("dve", "tensor_tensor", [c1b, c0b], L, out=Dr, in0=Xr[:, :, L1 - 1], in1=Wr[:, :, L1 - 1], op=ALU.add); yield
                P.op("dve", "tensor_tensor", [c3b, c2b], L, out=Di, in0=Xi[:, :, L1 - 1], in1=Wi[:, :, L1 - 1], op=ALU.add); yield
                LT = L + [tb]
                P.op("dve", "tensor_tensor", LT, L, out=t5, in0=Dr, in1=Vr[:, g, :], op=ALU.mult); yield
                P.op("dve", "tensor_tensor", LT, L, out=t6, in0=Di, in1=Vi[:, g, :], op=ALU.mult); yield
                P.op("dve", "tensor_tensor", L, L, out=Vvr, in0=t5, in1=t6, op=ALU.subtract); yield
                P.op("dve", "tensor_tensor", LT, L, out=t5, in0=Dr, in1=Vi[:, g, :], op=ALU.mult); yield
                P.op("dve", "tensor_tensor", LT, L, out=t6, in0=Di, in1=Vr[:, g, :], op=ALU.mult); yield
                P.op("dve", "tensor_tensor", L, L, out=Vvi, in0=t5, in1=t6, op=ALU.add); yield
                P.op("dve", "tensor_tensor_scan", LT, L, out=Gr, data0=rmask[:, g, :], data1=Vvr, initial=0.0, op0=ALU.mult, op1=ALU.add); yield
                P.op("dve", "tensor_tensor_scan", LT, L, out=Gi, data0=rmask[:, g, :], data1=Vvi, initial=0.0, op0=ALU.mult, op1=ALU.add); yield
                P.op("dve", "memset", L, L, ap=Cr2[:, 0:1], constant=0.0)
                P.op("dve", "memset", L, L, ap=Ci2[:, 0:1], constant=0.0); yield
                if NL1 > 1:
                    n1 = NL1 - 1
                    ur_, ui_ = Umr[:, g, 1:NL1], Umi[:, g, 1:NL1]
                    P.op("dve", "tensor_tensor", LT, L, out=t5[:, 0:n1], in0=Gr[:, 0:n1], in1=ur_, op=ALU.mult); yield
                    P.op("dve", "tensor_tensor", LT, L, out=t6[:, 0:n1], in0=Gi[:, 0:n1], in1=ui_, op=ALU.mult); yield
                    P.op("dve", "tensor_tensor", L, L, out=Cr2[:, 1:NL1], in0=t5[:, 0:n1], in1=t6[:, 0:n1], op=ALU.subtract); yield
                    P.op("dve", "tensor_tensor", LT, L, out=t5[:, 0:n1], in0=Gr[:, 0:n1], in1=ui_, op=ALU.mult); yield
                    P.op("dve", "tensor_tensor", LT, L, out=t6[:, 0:n1], in0=Gi[:, 0:n1], in1=ur_, op=ALU.mult); yield
                    P.op("dve", "tensor_tensor", L, L, out=Ci2[:, 1:NL1], in0=t5[:, 0:n1], in1=t6[:, 0:n1], op=ALU.add); yield
                P.op("dve", "tensor_tensor", [c1b] + L, [c1b], out=Xr, in0=Xr, in1=Cr2.unsqueeze(2).to_broadcast([128, NL1, L1]), op=ALU.add); yield
                P.op("dve", "tensor_tensor", [c3b] + L, [c3b], out=Xi, in0=Xi, in1=Ci2.unsqueeze(2).to_broadcast([128, NL1, L1]), op=ALU.add); yield
                pb_r = Ppr[:, g, :].unsqueeze(1).to_broadcast([128, NL1, L1])
                pb_i = Ppi[:, g, :].unsqueeze(1).to_broadcast([128, NL1, L1])
                (Er, Erb), (Ei, Eib) = S["Eb"]
                e3 = lambda ap: ap.rearrange("p (m e) -> p m e", e=L1)
                P.op("dve", "tensor_tensor", [c1b, tb], [c0b], out=Wr, in0=Xr, in1=pb_r, op=ALU.mult); yield
                P.op("dve", "tensor_tensor", [c3b, tb], [c2b], out=Wi, in0=Xi, in1=pb_i, op=ALU.mult); yield
                P.op("dve", "tensor_tensor", [c0b, c2b], [Erb], out=e3(Er), in0=Wr, in1=Wi, op=ALU.subtract); yield
                P.op("dve", "tensor_tensor", [c3b, tb], [c0b], out=Wr, in0=Xi, in1=pb_r, op=ALU.mult); yield
                P.op("dve", "tensor_tensor", [c1b, tb], [c2b], out=Wi, in0=Xr, in1=pb_i, op=ALU.mult); yield
                P.op("dve", "tensor_tensor", [c0b, c2b], [Eib], out=e3(Ei), in0=Wr, in1=Wi, op=ALU.add); yield
                tg, tgb = wts["tg"]
                (wyrB, wyrBb), (wyiB, wyiBb) = wts["yrB"], wts["yiB"]
                for (si, (lo, hi)) in enumerate(segs):
                    n = hi - lo
                    bank = YB + (0 if si == 1 else 1)
                    outp = ps[:, bank, 0:n]
                    P.op("pe", "matmul", [tgb, Upb], [self.psb[bank]], out=outp, lhsT=tg[:, gg, :], rhs=Up[:, gg, lo:hi], start=True, stop=False)
                    P.op("pe", "matmul", [wyrb, Erb], [self.psb[bank]], out=outp, lhsT=wyr[:, gg, :], rhs=Er[:, lo:hi], start=False, stop=False)
                    P.op("pe", "matmul", [wyib, Eib], [self.psb[bank]], out=outp, lhsT=wyi[:, gg, :], rhs=Ei[:, lo:hi], start=False, stop=False)
                    k0, k1 = hi - 1, (lo - 1 if lo > 0 else None)
                    P.op("pe", "matmul", [wyrBb, Erb], [self.psb[bank]], out=outp, lhsT=wyrB[:, gg, :], rhs=Er[:, k0:k1:-1], start=False, stop=False)
                    P.op("pe", "matmul", [wyiBb, Eib], [self.psb[bank]], out=outp, lhsT=wyiB[:, gg, :], rhs=Ei[:, k0:k1:-1], start=False, stop=True)
                    P.op("dve", "scalar_tensor_tensor", [Upb, tb, self.psb[bank]], [c0b], out=c0_[:, 1 + lo:1 + hi], in0=Up[:, gg, lo:hi], scalar=dcol[:, g:g + 1], in1=outp,
                         op0=ALU.mult, op1=ALU.add)
                    yield
                P.op("act", "activation", [c0b], [zpb], out=zp[:, gg, :], in_=c0_[:, 1:NCH + 1], func=AF.Gelu_apprx_tanh)
                yield

            import itertools
            for gp in range(0, 8, 2):
                gens = [chain(gp, setA), chain(gp + 1, setB)]
                for _ in itertools.zip_longest(*gens):
                    pass
            P.record_begin()
            zv = hTf.rearrange("p (k i) -> p k i", i=8)
            for jj in range(8):
                for (lo, hi) in segs:
                    n = hi - lo
                    bank = (7, 5, 6, 0, 1, 2)[pbk % 6]
                    pbk += 1
                    for gg in range(8):
                        P.op("pe", "matmul", [zpb, tb], [self.psb[bank]], out=ps[:, bank, 0:n], lhsT=Jc[:, jj, 112 - 16 * gg:240 - 16 * gg], rhs=zp[:, gg, lo:hi],
                             start=(gg == 0), stop=(gg == 7))
                    P.op("act", "copy", [self.psb[bank]], [hTfb], out=zv[:, lo:hi, jj], in_=ps[:, bank, 0:n])
            P.dma(hT_d[ft], hTf, [hTfb], [self.hTdb], "s2st")
            rec_unpack_prev = P.record_end()

        P.replay(rec_unpack_prev, len(rec_unpack_prev))
        o = S0END
        wg, wgb = self.load_weight(o, d["s5_w_glu"][j], KD, 2 * D, "wglu_", kgroup=2); o += KD * 2 * D // 2
        zTs = []
        for q in range(2):
            ap, b = self.aalloc(o, 2048, BF16, f"s3z{q}"); o += 2048
            zTs.append((ap.rearrange("p (k n) -> p k n", k=KD), b))
        sig, sigb = self.aalloc(o, D, F32, "s3sig"); o += D
        og, ogb = self.aalloc(o, D, F32, "s3o"); o += D
        tiles = list(range(0 if ctx_out else NTC, NT))
        blocks = [tiles[a:a + 4] for a in range(0, len(tiles), 4)]
        for bi, blk in enumerate(blocks):
            nb = len(blk) * 128
            tok0 = blk[0] * 128
            zT, zTb = zTs[bi % 2]
            P.dma(zT[:, :, :nb], hT_d[:, :, tok0:tok0 + nb].rearrange("k p n -> p k n"), [self.hTdb], [zTb], f"s3ld{bi % 2}")
            for q, t in enumerate(blk):
                b4 = 4 * (t % 2)
                for cb_ in range(4):
                    for k in range(KD):
                        P.op("pe", "matmul", [zTb, wgb[k]], [self.psb[b4 + cb_]], out=ps[:, b4 + cb_, :], lhsT=zT[:, k, q * 128:(q + 1) * 128], rhs=wg[:, k, cb_ * 512:(cb_ + 1) * 512],
                             start=(k == 0), stop=(k == KD - 1))
                P.op("act", "activation", [self.psb[b4 + 2], self.psb[b4 + 3]], [sigb], out=sig, in_=ps[:, b4 + 2:b4 + 4, :].rearrange("p a n -> p (a n)"), func=AF.Sigmoid)
                P.op("dve", "tensor_tensor", [self.psb[b4], self.psb[b4 + 1], sigb], [ogb], out=og, in0=ps[:, b4:b4 + 2, :].rearrange("p a n -> p (a n)"), in1=sig, op=ALU.mult)
                st = self.resid_begin(t)
                for hf in range(2):
                    self.resid_half(st, 0, hf, og[:, hf * 512:(hf + 1) * 512], ogb)
                self.resid_end(st)

def host_consts(cfg):
    bf = ml_dtypes.bfloat16
    out = {
        "ident_bf": np.eye(128, dtype=np.float32).astype(bf),
        "onehot": np.eye(128, 1, dtype=np.float32),
    }
    J = np.zeros((8, 128, 240), np.float32)
    for a in range(8):
        for h in range(16):
            J[a, 16 * a + h, h + 112] = 1.0
    out["s5_J"] = J.astype(bf)
    nch = cfg.ntok // 8
    m01 = np.ones(nch, np.float32)
    m01[::32] = 0.0
    out["s5_mask01"] = m01
    ii = np.arange(128) // 16
    out["s5_maskFB"] = np.stack([(ii[None, :] >= ii[:, None]), (ii[None, :] <= ii[:, None])]).astype(np.float32)
    out["ident32"] = np.eye(128, dtype=np.float32)
    k = np.arange(256, dtype=np.float64)
    ang = 2 * np.pi * np.outer(k, k) / 256
    out["cs256n"] = np.concatenate([np.cos(ang), -np.sin(ang)], axis=1).astype(np.float32).astype(bf)
    out["cs256p"] = np.concatenate([np.cos(ang), np.sin(ang)], axis=1).astype(np.float32).astype(bf)
    n = cfg.n_lat
    rows_ = n // 64
    row = np.repeat(np.arange(rows_, dtype=np.float32), 64)
    col = np.tile(np.arange(64, dtype=np.float32), rows_)
    inv = np.power(np.float32(10000.0), -np.arange(0, 32, 2, dtype=np.float32) / np.float32(32)).astype(np.float32)
    ang_r = (row[:, None] * inv).astype(np.float32)
    ang_c = (col[:, None] * inv).astype(np.float32)
    out["rope_tab"] = np.concatenate([np.cos(ang_r), np.cos(ang_c), np.sin(ang_r), np.sin(ang_c)], axis=1).astype(np.float32)
    t = np.arange(n, dtype=np.int64)
    m = np.outer(t, t) % n
    ang = (2 * np.pi / n) * m.astype(np.float64)
    nt = n // 128
    tabs = []
    for fn in (np.cos, np.sin):
        M = fn(ang).astype(np.float32)
        M = M.reshape(nt, 128, nt, 128).transpose(2, 1, 0, 3)
        tabs.append(M)
    out["dft_n"] = np.ascontiguousarray(np.stack(tabs, axis=2)).astype(bf)
    return out


def make_in_maps(inputs, cfg, n_cores=8):
    consts = host_consts(cfg)
    shared = {k: np.ascontiguousarray(inputs[k]) for k in ("w_mod", "b_mod", "norm_g", "mlp_w1", "mlp_w2", "fourier_w", "fourier_b",
                                                            "diff_w_qkv", "diff_q_norm", "diff_k_norm", "diff_lambda", "diff_subln", "diff_w_o")}
    for nm in ("s5_log_dt", "s5_w_glu"):
        shared[nm] = np.ascontiguousarray(inputs[nm])
    shared["s5_lamr_t"] = np.ascontiguousarray(inputs["s5_lambda_re"].transpose(0, 1, 3, 2).reshape(-1, 128, 64))
    shared["s5_lami_t"] = np.ascontiguousarray(inputs["s5_lambda_im"].transpose(0, 1, 3, 2).reshape(-1, 128, 64))
    shared["s5_br_t"] = np.ascontiguousarray(inputs["s5_b_re"].transpose(0, 1, 3, 2, 4).reshape(-1, 128, 64, 16))
    shared["s5_bi_t"] = np.ascontiguousarray(inputs["s5_b_im"].transpose(0, 1, 3, 2, 4).reshape(-1, 128, 64, 16))
    shared["s5_cr_t"] = np.ascontiguousarray(inputs["s5_c_re"].transpose(0, 1, 4, 2, 3).reshape(-1, 128, 64, 16))
    shared["s5_ci_t"] = np.ascontiguousarray(inputs["s5_c_im"].transpose(0, 1, 4, 2, 3).reshape(-1, 128, 64, 16))
    nS = inputs["s5_d"].shape[0]
    shared["s5_dcol"] = np.ascontiguousarray(np.tile(inputs["s5_d"].reshape(nS, 64, 16).transpose(0, 2, 1), (1, 8, 1)))
    cctx_t = np.ascontiguousarray(inputs["c_ctx"].reshape(KD, 128).T)
    maps = []
    for b in range(n_cores):
        m = dict(shared)
        m.update(consts)
        m["x"] = np.ascontiguousarray(inputs["x"][b])
        m["ctx"] = np.ascontiguousarray(inputs["ctx"][b])
        m["c_t"] = np.ascontiguousarray(inputs["c"][b].reshape(KD, 128).T)
        m["cctx_t"] = cctx_t
        maps.append(m)
    return maps


_NC_CACHE = {}


def kernel(**inputs):
    cfg = Cfg()
    nc = K(cfg).build()
    maps = make_in_maps(inputs, cfg, 8)
    res = run_bass_kernel_spmd(nc, maps, core_ids=list(range(8)))
    return np.stack([r["out"] for r in res.results], axis=0)
```

```python
import contextlib
import math
import numpy as np
import ml_dtypes
import concourse.bass as bass
import concourse.mybir as mybir
from concourse.bass_utils import run_bass_kernel_spmd

F32 = mybir.dt.float32
BF16 = mybir.dt.bfloat16
I32 = mybir.dt.int32
ALU = mybir.AluOpType
AF = mybir.ActivationFunctionType
AX = mybir.AxisListType

ENGS = ("pe", "act", "dve", "pool", "sp")


class Buf:
    __slots__ = ("name", "lw", "rd")

    def __init__(self, name="b"):
        self.name = name
        self.lw = None
        self.rd = {}

    def inherit(self, others):
        for o in others:
            for d in ([o.lw] if o.lw is not None else []) + list(o.rd.values()):
                _rd_add(self.rd, d, force=True)


def _rd_add(rd, d, force=False):
    k = (d[0], d[1]) + (("f",) if force else ())
    old = rd.get(k)
    if old is None or old[2] < d[2]:
        rd[k] = d


class Instr:
    __slots__ = ("eng", "fn", "waits", "needs_inc", "is_dma", "dsem", "dval", "idx")

    def __init__(self, eng, fn):
        self.eng = eng
        self.fn = fn
        self.waits = []
        self.needs_inc = False
        self.is_dma = False
        self.dsem = None
        self.dval = 0
        self.idx = -1


class Prog:
    def __init__(self, nc):
        self.nc = nc
        self.stack = contextlib.ExitStack()
        self.streams = {e: [] for e in ENGS}
        self.known = {e: {} for e in ENGS}
        self.dma_cnt = {}
        self._rr = {}

    def sbuf(self, name, shape, dtype):
        return self.stack.enter_context(self.nc.sbuf_tensor(name, list(shape), dtype))

    def psum(self, name, shape, dtype):
        return self.stack.enter_context(self.nc.psum_tensor(name, list(shape), dtype))

    def _need(self, ins, dep):
        if dep is None:
            return
        if dep[0] == 'E':
            _, e2, idx = dep
            if e2 == ins.eng and e2 == "pe":
                return
            key = ('E', e2)
            val = idx
        else:
            key = ('D', dep[1])
            val = dep[2]
        kn = self.known[ins.eng]
        if kn.get(key, -1) >= val:
            return
        kn[key] = val
        if dep[0] == 'E':
            self.streams[e2][idx].needs_inc = True
            ins.waits.append(('E', e2, idx))
        else:
            ins.waits.append(('D', dep[1], val))

    def _track(self, ins, reads, writes):
        for b in reads:
            self._need(ins, b.lw)
        for b in writes:
            self._need(ins, b.lw)
            for k, d in b.rd.items():
                if d[0] == 'E' and d[1] == ins.eng and not ins.is_dma and len(k) == 2:
                    continue
                self._need(ins, d)

    def _commit(self, dep, reads, writes):
        for b in reads:
            _rd_add(b.rd, dep)
        for b in writes:
            b.lw = dep
            b.rd = {}

    def record_begin(self):
        self._rec = []

    def record_end(self):
        r, self._rec = self._rec, None
        return r

    def replay(self, rec, n):
        for _ in range(min(n, len(rec))):
            kind, a, kw = rec.pop(0)
            if kind == "op":
                self.op(*a, **kw)
            else:
                self.dma(*a, **kw)

    def op(self, eng, fn, reads=(), writes=(), **kw):
        if getattr(self, "_rec", None) is not None:
            self._rec.append(("op", (eng, fn, list(reads), list(writes)), kw))
            return None
        if isinstance(fn, str):
            name = fn

            def fn(e, name=name, kw=kw):
                return getattr(e, name)(**kw)
        ins = Instr(eng, fn)
        st = self.streams[eng]
        ins.idx = len(st)
        self._track(ins, reads, writes)
        st.append(ins)
        self._commit(('E', eng, ins.idx), reads, writes)
        return ins

    def dma(self, out_ap, in_ap, reads, writes, semkey, q="sp", **kw):
        if getattr(self, "_rec", None) is not None:
            self._rec.append(("dma", (out_ap, in_ap, list(reads), list(writes), semkey), dict(q=q, **kw)))
            return None

        def fn(e, out_ap=out_ap, in_ap=in_ap, kw=kw):
            return e.dma_start(out=out_ap, in_=in_ap, **kw)
        ins = Instr(q, fn)
        ins.is_dma = True
        st = self.streams[q]
        ins.idx = len(st)
        pool = self.DMA_SEMS[q]
        rr = self._rr.get(q, 0)
        self._rr[q] = rr + 1
        semkey = f"{q}{rr % pool}"
        prev = self.dma_cnt.get(semkey, 0)
        if prev:
            self._need(ins, ('D', semkey, prev))
        self._track(ins, reads, writes)
        v = prev + 16
        self.dma_cnt[semkey] = v
        ins.dsem = semkey
        ins.dval = v
        st.append(ins)
        self._commit(('D', semkey, v), reads, writes)
        return ins

    DMA_SEMS = {"sp": 64, "pool": 32, "act": 2}

    def emit(self):
        nc = self.nc
        sems = {}
        for e in ENGS:
            sems[('E', e)] = self.stack.enter_context(nc.semaphore(f"s_{e}"))
        for k in self.dma_cnt:
            sems[('D', k)] = self.stack.enter_context(nc.semaphore(f"d_{k}"))
        cnt = {}
        for e in ENGS:
            c = 0
            for ins in self.streams[e]:
                if ins.needs_inc:
                    c += 1
                    cnt[(e, ins.idx)] = c
        engmap = {"pe": "tensor", "act": "scalar", "dve": "vector", "pool": "gpsimd", "sp": "sync"}

        def run(e, eng):
            for ins in self.streams[e]:
                for w in ins.waits:
                    if w[0] == 'E':
                        eng.wait_ge(sems[('E', w[1])], cnt[(w[1], w[2])])
                    else:
                        eng.wait_ge(sems[('D', w[1])], w[2])
                r = ins.fn(eng)
                if ins.is_dma:
                    r.then_inc(sems[('D', ins.dsem)], 16)
                elif ins.needs_inc:
                    r.then_inc(sems[('E', e)], 1)
            if e == "sp":
                for k, v in self.dma_cnt.items():
                    eng.wait_ge(sems[('D', k)], v)

        with nc.Block() as block:
            for e in ENGS:
                getattr(block, engmap[e])(lambda eng, e=e: run(e, eng))


D = 1024
DFF = 4096
NMOD = 6
EPS = 1e-6
KD = D // 128


class Cfg:
    def __init__(self, n_lat=4096, n_ctx=256, layers=(0, 1, 2, 3), depth=4, mixers=True):
        self.n_lat = n_lat
        self.n_ctx = n_ctx
        self.layers = tuple(layers)
        self.depth = depth
        self.mixers = mixers
        self.mixer_kinds = (0, 1, 2)
        self.nt_ctx = n_ctx // 128
        self.nt_lat = n_lat // 128
        self.nt = self.nt_ctx + self.nt_lat
        self.ntok = n_lat + n_ctx


class K:
    def __init__(self, cfg):
        self.cfg = cfg
        nc = self.nc = bass.Bass("TRN2", target_bir_lowering=False)
        self.P = Prog(nc)
        self.dram = {}

    def din(self, name, shape, dtype=F32):
        t = self.nc.dram_tensor(name, list(shape), dtype, kind="ExternalInput").ap()
        self.dram[name] = t
        return t

    def dump(self, name, ap, buf, shape, dtype=F32):
        t = self.nc.dram_tensor(name, list(shape), dtype, kind="ExternalOutput").ap()
        self.P.dma(t, ap, [buf], [], "dbg")

    def dscratch(self, name, shape, dtype):
        t = self.nc.dram_tensor(name, list(shape), dtype).ap()
        self.dram[name] = t
        return t

    def build(self):
        cfg = self.cfg
        nc = self.nc
        P = self.P
        x_in = self.din("x", [cfg.n_lat, D])
        ctx_in = self.din("ctx", [cfg.n_ctx, D])
        self.din("c_t", [128, KD])
        self.din("cctx_t", [128, KD])
        self.din("w_mod", [4, D, NMOD * D])
        self.din("b_mod", [4, NMOD * D])
        self.din("norm_g", [4, 2, D])
        self.din("mlp_w1", [4, D, DFF])
        self.din("mlp_w2", [4, DFF, D])
        self.din("s5_J", [8, 128, 240], BF16)
        self.din("s5_mask01", [cfg.ntok // 8])
        self.din("s5_maskFB", [2, 128, 128])
        self.din("ident32", [128, 128])
        self.din("s5_lamr_t", [2, 128, 64])
        self.din("s5_lami_t", [2, 128, 64])
        self.din("s5_log_dt", [2, 2, 64])
        for nm in ("s5_br_t", "s5_bi_t", "s5_cr_t", "s5_ci_t"):
            self.din(nm, [2, 128, 64, 16])
        self.din("s5_dcol", [2, 128, 64])
        self.din("s5_w_glu", [2, D, 2 * D])
        self.din("diff_w_qkv", [1, D, 3 * D])
        self.din("diff_q_norm", [1, 64])
        self.din("diff_k_norm", [1, 64])
        self.din("diff_lambda", [1, 4, 64])
        self.din("diff_subln", [1, 128])
        self.din("diff_w_o", [1, D, D])
        self.din("rope_tab", [cfg.n_lat, 64])
        self.dscratch("qT_d", [8, 128, cfg.ntok], BF16)
        self.dscratch("kT_d", [8, 128, cfg.ntok], BF16)
        self.dscratch("v_d", [cfg.ntok, D], BF16)
        self.dscratch("onT_d", [8, 128, cfg.ntok], BF16)
        self.din("fourier_w", [1, D, D])
        self.din("fourier_b", [1, D])
        self.din("cs256n", [256, 512], BF16)
        self.din("cs256p", [256, 512], BF16)
        self.din("dft_n", [cfg.nt_lat, 128, 2, cfg.nt_lat, 128], BF16)
        self.din("ident_bf", [128, 128], BF16)
        self.din("onehot", [128, 1])
        out = self.nc.dram_tensor("out", [cfg.n_lat, D], F32, kind="ExternalOutput").ap()
        self.dram["out"] = out
        cs = self.dscratch("cs", [cfg.n_ctx, D], F32)
        self.dscratch("hT_d", [KD, 128, cfg.ntok], BF16)
        self.dscratch("modrow", [2, 4, D], F32)
        self.modrowb = Buf("modrow")
        self.hTdb = Buf("hT_d")

        with P.stack:
            self.alloc()
            self.xsrc = [None] * cfg.nt
            self.xbuf = [Buf(f"x{t}") for t in range(cfg.nt)]
            for t in range(cfg.nt):
                if t < cfg.nt_ctx:
                    self.xsrc[t] = ctx_in[t * 128:(t + 1) * 128, :]
                else:
                    tl = t - cfg.nt_ctx
                    self.xsrc[t] = x_in[tl * 128:(tl + 1) * 128, :]
            self.xdst = []
            for t in range(cfg.nt):
                if t < cfg.nt_ctx:
                    self.xdst.append(cs[t * 128:(t + 1) * 128, :])
                else:
                    tl = t - cfg.nt_ctx
                    self.xdst.append(out[tl * 128:(tl + 1) * 128, :])
            self.prologue()
            for i in cfg.layers:
                last = (i == cfg.depth - 1)
                self.mod_phase(i)
                if getattr(cfg, "debug", False) and i == cfg.layers[0]:
                    self.dump("dbg_cols", self.cols[:].rearrange("p s v k -> p (s v k)"), self.colsb, [128, 64])
                    self.dump("dbg_gates", self.gates[:].rearrange("p s w d -> p (s w d)"), self.gatesb, [128, 4 * D])
                kind = i % 3
                if cfg.mixers and kind in cfg.mixer_kinds:
                    if kind == 2:
                        self.fourier_phase(i, last)
                    elif kind == 1:
                        self.attn_phase(i, last)
                    else:
                        self.s5_phase(i, last)
                self.mlp_phase(i, last)
            if self.xsrc[cfg.nt - 1] is not self.xdst[cfg.nt - 1]:
                raise RuntimeError("no layer wrote the output")
            P.emit()
        return nc

    ARENA_WORDS = 41472
    WREG = 16384

    def alloc(self):
        P = self.P
        self.arena = P.sbuf("arena", [128, self.ARENA_WORDS], F32)
        self.alive = []
        self.ident = P.sbuf("ident", [128, 128], BF16)
        self.onehot = P.sbuf("onehot_s", [128, 1], F32)
        self.eps_t = P.sbuf("eps_t", [128, 1], F32)
        self.cb = Buf("consts")
        self.cols = P.sbuf("cols", [128, 2, 4, KD], F32)
        self.colsb = Buf("cols")
        self.gates = P.sbuf("gates", [128, 2, 2, D], F32)
        self.gatesb = Buf("gates")
        self.cond = P.sbuf("cond", [128, 2, KD], F32)
        self.condrep = P.sbuf("condrep", [128, 2, KD, 128], BF16)
        self.condb = Buf("cond")
        self.xt = [P.sbuf(f"xt{j}", [128, D], F32) for j in range(2)]
        self.xtb = [Buf(f"xt{j}") for j in range(2)]
        self.xn = [P.sbuf(f"xn{j}", [128, D], BF16) for j in range(2)]
        self.xnb = [Buf(f"xn{j}") for j in range(2)]
        self.ss = [P.sbuf(f"ss{j}", [128, 2], F32) for j in range(2)]
        self.ssb = [Buf(f"ss{j}") for j in range(2)]
        self.norm_i = 0
        self.xe = [P.sbuf(f"xe{j}", [128, D], F32) for j in range(2)]
        self.xeb = [Buf(f"xe{j}") for j in range(2)]
        self.xe_i = 0
        self.tmpe = [P.sbuf(f"tmpe{j}", [128, 512], F32) for j in range(2)]
        self.tmpeb = [Buf(f"tmpe{j}") for j in range(2)]
        self.tmpe_i = 0
        self.ps = P.psum("ps", [128, 8, 512], F32)
        self.psb = [Buf(f"ps{j}") for j in range(8)]

    def aalloc(self, off, words, dtype=F32, name="a"):
        assert off >= 0 and off + words <= self.ARENA_WORDS, (name, off, words)
        ap = self.arena[:, off:off + words]
        if dtype != F32:
            ap = ap.bitcast(dtype)
        b = Buf(name)
        keep = []
        over = []
        for (o, n, ob) in self.alive:
            if o < off + words and off < o + n:
                over.append(ob)
                if not (off <= o and o + n <= off + words):
                    keep.append((o, n, ob))
            else:
                keep.append((o, n, ob))
        b.inherit(over)
        keep.append((off, words, b))
        self.alive = keep
        return ap, b

    def next_region(self):
        r = getattr(self, "_region", 1) ^ 1
        self._region = r
        return r

    def prologue(self):
        P = self.P
        d = self.dram
        P.dma(self.ident[:], d["ident_bf"], [], [self.cb], "const")
        P.dma(self.onehot[:], d["onehot"], [], [self.cb], "const")
        P.op("pool", "memset", [], [self.cb], ap=self.eps_t[:], constant=EPS)
        P.dma(self.cond[:, 0, :], d["c_t"], [], [self.condb], "const")
        P.dma(self.cond[:, 1, :], d["cctx_t"], [], [self.condb], "const")
        P.op("act", "activation", [self.condb], [self.condb], out=self.cond[:], in_=self.cond[:], func=AF.Silu)
        P.op("dve", "tensor_copy", [self.condb], [self.condb], out=self.condrep[:],
             in_=self.cond[:].unsqueeze(3).to_broadcast([128, 2, KD, 128]))

    def load_weight(self, off_words, src, n_k, n_cols, name, kgroup=1):
        P = self.P
        words = n_k * n_cols // 2
        bufs = []
        apfull = None
        for g0 in range(0, n_k, kgroup):
            gw = kgroup * n_cols // 2
            ap, b = self.aalloc(off_words + (g0 // kgroup) * gw, gw, BF16, f"{name}{g0}")
            ap3 = ap.rearrange("p (k n) -> p k n", k=kgroup)
            P.dma(ap3, src[g0 * 128:(g0 + kgroup) * 128, :].rearrange("(k p) n -> p k n", p=128), [], [b], f"w{(g0 // kgroup) % 4}", q="pool")
            bufs += [b] * kgroup
        full = self.arena[:, off_words:off_words + words].bitcast(BF16).rearrange("p (k n) -> p k n", k=n_k)
        return full, bufs

    def mod_phase(self, i):
        P = self.P
        d = self.dram
        ps = self.ps
        WOFF = 2 * self.WREG
        grow, gb = self.aalloc(WOFF, 2 * D, F32, "grow")
        P.dma(grow.rearrange("p (a n) -> p a n", a=2), d["norm_g"][i].partition_broadcast(128), [], [gb], "modc")
        tmps = [self.aalloc(WOFF + 2 * D + j * 512, 512, F32, f"modtmp{j}") for j in range(2)]
        ti = 0
        for half in range(2):
            region = self.next_region()
            slab, sbufs = self.load_weight(region * self.WREG, d["w_mod"][i][:, half * 3072:(half + 1) * 3072], KD, 3072, f"wmod{half}_", kgroup=2)
            bmod_bc, bmb = self.aalloc(WOFF + 2 * D + 1024, 3072, F32, f"bmod{half}")
            P.dma(bmod_bc, d["b_mod"][i, half * 3072:(half + 1) * 3072].partition_broadcast(128), [], [bmb], "modc")
            for bl in range(6):
                blk = half * 6 + bl
                v = blk // 2
                hcol = (blk % 2) * 512
                for s_ in range(2):
                    bank = 6 + s_
                    for k in range(KD):
                        P.op("pe", "matmul", [self.condb, sbufs[k]], [self.psb[bank]], out=ps[:, bank, :], lhsT=self.condrep[:, s_, k, :],
                             rhs=slab[:, k, bl * 512:(bl + 1) * 512], start=(k == 0), stop=(k == KD - 1))
                    bsl = bmod_bc[:, bl * 512:(bl + 1) * 512]
                    if v == 2 or v == 5:
                        dst = self.gates[:, s_, 0 if v == 2 else 1, hcol:hcol + 512]
                        P.op("dve", "tensor_tensor", [self.psb[bank], bmb], [self.gatesb], out=dst, in0=ps[:, bank, :], in1=bsl, op=ALU.add)
                        continue
                    tm, tmb = tmps[ti % 2]
                    ti += 1
                    P.op("dve", "tensor_tensor", [self.psb[bank], bmb], [tmb], out=tm, in0=ps[:, bank, :], in1=bsl, op=ALU.add)
                    is_scale = v in (1, 4)
                    which = 0 if v < 3 else 1
                    if is_scale:
                        P.op("dve", "scalar_tensor_tensor", [tmb, gb], [tmb], out=tm, in0=tm, scalar=1.0,
                             in1=grow[:, which * D + hcol: which * D + hcol + 512], op0=ALU.add, op1=ALU.mult)
                    vec = which * 2 + (0 if is_scale else 1)
                    P.dma(d["modrow"][s_, vec, hcol:hcol + 512].unsqueeze(0), tm[0:1, :], [tmb], [self.modrowb], "modrow")
                    for kk in range(4):
                        cidx = s_ * 32 + vec * KD + (blk % 2) * 4 + kk
                        P.op("pe", "matmul", [tmb, self.cb], [self.psb[5]], out=ps[:, 5, cidx:cidx + 1], lhsT=tm[:, kk * 128:(kk + 1) * 128],
                             rhs=self.onehot[:, 0:1], start=True, stop=True)
        P.op("dve", "tensor_copy", [self.psb[5]], [self.colsb], out=self.cols[:].rearrange("p s v k -> p (s v k)"), in_=ps[:, 5, 0:64])

    def norm_stats(self, t):
        P = self.P
        j = self.norm_i % 2
        self.norm_i += 1
        xt, xtb = self.xt[j], self.xtb[j]
        xn, xnb = self.xn[j], self.xnb[j]
        ss, ssb = self.ss[j], self.ssb[j]
        P.dma(xt[:], self.xsrc[t], [self.xbuf[t]], [xtb], f"xt{j}")
        P.op("act", "activation", [xtb], [xnb, ssb], out=xn[:], in_=xt[:], func=AF.Square, accum_out=ss[:, 0:1])
        P.op("act", "activation", [ssb, self.cb], [ssb], out=ss[:, 1:2], in_=ss[:, 0:1], func=AF.Sqrt, scale=1.0 / D, bias=self.eps_t[:])
        P.op("dve", "reciprocal", [ssb], [ssb], out=ss[:, 1:2], in_=ss[:, 1:2])
        return xt, xtb, xn, xnb, ss, ssb

    def transpose8(self, src, srcb, dst, dstb, engine="dve", bank=4):
        P = self.P
        psT = self.ps[:, bank, :].bitcast(BF16)
        for k in range(KD):
            P.op("pe", "transpose", [srcb, self.cb], [self.psb[bank]], out=psT[:, k * 128:(k + 1) * 128], in_=src[:, k * 128:(k + 1) * 128], identity=self.ident[:])
        if engine == "dve":
            P.op("dve", "tensor_copy", [self.psb[bank]], [dstb], out=dst.rearrange("p k n -> p (k n)"), in_=psT)
        else:
            P.op("act", "copy", [self.psb[bank]], [dstb], out=dst.rearrange("p k n -> p (k n)"), in_=psT)

    def norm_T(self, t, which, dstT, dstb, col0, bank=4):
        h = self.norm_part1(t)
        self.norm_part2(h, t, which, dstT, dstb, col0, bank)

    def norm_part1(self, t):
        P = self.P
        xt, xtb, xn, xnb, ss, ssb = self.norm_stats(t)
        P.op("dve", "tensor_scalar", [xtb, ssb], [xnb], out=xn[:], in0=xt[:], scalar1=ss[:, 1:2], scalar2=None, op0=ALU.mult)
        return (xn, xnb)

    def norm_part2(self, h, t, which, dstT, dstb, col0, bank=4):
        P = self.P
        xn, xnb = h
        s_ = 1 if t < self.cfg.nt_ctx else 0
        psT = self.ps[:, bank, :].bitcast(BF16)
        for k in range(KD):
            P.op("pe", "transpose", [xnb, self.cb], [self.psb[bank]], out=psT[:, k * 128:(k + 1) * 128], in_=xn[:, k * 128:(k + 1) * 128], identity=self.ident[:])
        gsi, shi = (0, 1) if which == 0 else (2, 3)
        for k in range(KD):
            P.op("act", "activation", [self.psb[bank], self.colsb], [dstb], out=dstT[:, k, col0:col0 + 128], in_=psT[:, k * 128:(k + 1) * 128], func=AF.Identity,
                 scale=self.cols[:, s_, gsi, k:k + 1], bias=self.cols[:, s_, shi, k:k + 1])

    def resid_begin(self, t):
        P = self.P
        j = self.xe_i % 2
        self.xe_i += 1
        xe, xeb = self.xe[j], self.xeb[j]
        P.dma(xe[:], self.xsrc[t], [self.xbuf[t]], [xeb], f"xe{j}")
        return (t, j, xe, xeb)

    def resid_half(self, st, which, h, pap, pb, bias_row=None, bias_buf=None):
        P = self.P
        t, j, xe, xeb = st
        s_ = 1 if t < self.cfg.nt_ctx else 0
        jj = self.tmpe_i % 2
        self.tmpe_i += 1
        tm, tmb = self.tmpe[jj], self.tmpeb[jj]
        g = self.gates[:, s_, which, h * 512:(h + 1) * 512]
        if bias_row is not None:
            P.op("dve", "tensor_tensor", [pb, bias_buf], [tmb], out=tm[:], in0=pap, in1=bias_row[:, h * 512:(h + 1) * 512], op=ALU.add)
            P.op("dve", "tensor_tensor", [tmb, self.gatesb], [tmb], out=tm[:], in0=tm[:], in1=g, op=ALU.mult)
        else:
            P.op("dve", "tensor_tensor", [pb, self.gatesb], [tmb], out=tm[:], in0=pap, in1=g, op=ALU.mult)
        P.op("dve", "tensor_tensor", [tmb, xeb], [xeb], out=xe[:, h * 512:(h + 1) * 512], in0=xe[:, h * 512:(h + 1) * 512], in1=tm[:], op=ALU.add)

    def resid_end(self, st):
        t, j, xe, xeb = st
        self.P.dma(self.xdst[t], xe[:], [xeb], [self.xbuf[t]], f"xe{j}")
        self.xsrc[t] = self.xdst[t]

    def resid(self, t, which, ps_halves, bias_row=None, bias_buf=None):
        st = self.resid_begin(t)
        for h, (pap, pb) in enumerate(ps_halves):
            self.resid_half(st, which, h, pap, pb, bias_row, bias_buf)
        self.resid_end(st)

    def mlp_phase(self, i, last):
        P = self.P
        cfg = self.cfg
        d = self.dram
        ps = self.ps
        FH = DFF // 2
        NFC = FH // 128
        t0 = cfg.nt_ctx if last else 0
        tiles = list(range(t0, cfg.nt))
        blocks = [tiles[a:a + 4] for a in range(0, len(tiles), 4)]
        hT_d = d["hT_d"]
        hTdb = self.hTdb
        WOFF = 2 * self.WREG
        for pas in range(2):
            region = 1 - pas
            roff = region * self.WREG
            w1, w1b = self.load_weight(roff, d["mlp_w1"][i][:, pas * FH:(pas + 1) * FH], KD, FH, f"w1p{pas}_", kgroup=2)
            w2, w2b = self.load_weight(roff + KD * FH // 2, d["mlp_w2"][i][pas * FH:(pas + 1) * FH, :], NFC, D, f"w2p{pas}_", kgroup=4)
            hTs = []
            for j in range(2):
                ap, b = self.aalloc(WOFF + j * 2048, 2048, BF16, f"hT{j}")
                hTs.append((ap.rearrange("p (k n) -> p k n", k=KD), b))
            aT, aTb = self.aalloc(WOFF + 4096, 4096, BF16, "aT")
            aT = aT.rearrange("p (k n) -> p k n", k=NFC)
            rl = [self.aalloc(WOFF + 8192 + j * 256, 256, BF16, f"relu{j}") for j in range(2)]
            def store_or_load(bi):
                blk = blocks[bi]
                nb = len(blk) * 128
                hT, hTb = hTs[bi % 2]
                tok0 = blk[0] * 128
                if pas == 0:
                    P.dma(hT_d[:, :, tok0:tok0 + nb].rearrange("k p n -> p k n"), hT[:, :, :nb], [hTb], [hTdb], f"hTd{bi % 2}")
                else:
                    P.dma(hT[:, :, :nb], hT_d[:, :, tok0:tok0 + nb].rearrange("k p n -> p k n"), [hTdb], [hTb], f"hTd{bi % 2}")

            if pas == 0:
                hT0, hT0b = hTs[0]
                for q, t in enumerate(blocks[0]):
                    self.norm_T(t, 1, hT0, hT0b, q * 128)
            store_or_load(0)
            for bi, blk in enumerate(blocks):
                nb = len(blk) * 128
                hT, hTb = hTs[bi % 2]
                for fc in range(NFC):
                    bank = fc % 2
                    for k in range(KD):
                        P.op("pe", "matmul", [w1b[k], hTb], [self.psb[bank]], out=ps[:, bank, :nb], lhsT=w1[:, k, fc * 128:(fc + 1) * 128], rhs=hT[:, k, :nb],
                             start=(k == 0), stop=(k == KD - 1))
                    r, rb_ = rl[fc % 2]
                    P.op("act", "activation", [self.psb[bank]], [rb_], out=r[:, :nb], in_=ps[:, bank, :nb], func=AF.Relu)
                    P.op("dve", "tensor_tensor", [rb_], [aTb], out=aT[:, fc, :nb], in0=r[:, :nb], in1=r[:, :nb], op=ALU.mult)
                nxt = blocks[bi + 1] if bi + 1 < len(blocks) else []
                hTn, hTnb = hTs[(bi + 1) % 2]
                pend = None
                if pas == 0 and nxt:
                    pend = self.norm_part1(nxt[0])
                elif pas == 1 and nxt:
                    store_or_load(bi + 1)
                for q, t in enumerate(blk):
                    halves = []
                    for h in range(2):
                        bank = 2 + h
                        for fc in range(NFC):
                            P.op("pe", "matmul", [aTb, w2b[fc]], [self.psb[bank]], out=ps[:, bank, :], lhsT=aT[:, fc, q * 128:(q + 1) * 128],
                                 rhs=w2[:, fc, h * 512:(h + 1) * 512], start=(fc == 0), stop=(fc == NFC - 1))
                        halves.append((ps[:, bank, :], self.psb[bank]))
                    self.resid(t, 1, halves)
                    if pas == 0 and q < len(nxt):
                        self.norm_part2(pend, nxt[q], 1, hTn, hTnb, q * 128)
                        if q + 1 < len(nxt):
                            pend = self.norm_part1(nxt[q + 1])
                if pas == 0 and nxt:
                    for q in range(len(blk), len(nxt)):
                        if q > len(blk):
                            pend = self.norm_part1(nxt[q])
                        self.norm_part2(pend, nxt[q], 1, hTn, hTnb, q * 128)
                        if q + 1 < len(nxt):
                            pend = self.norm_part1(nxt[q + 1])
                    store_or_load(bi + 1)

    def fourier_phase(self, i, last):
        P = self.P
        cfg = self.cfg
        d = self.dram
        ps = self.ps
        j = i // 3
        NTL = cfg.nt_lat
        o = 0
        H, Hb = self.aalloc(o, NTL * 512, BF16, "fH"); o += NTL * 512
        H = H.rearrange("p (c n) -> p c n", c=NTL)
        slabs = []
        for q in range(2):
            ap, b = self.aalloc(o, NTL * 128, BF16, f"fslab{q}"); o += NTL * 128
            slabs.append((ap.rearrange("p (a c k) -> p a c k", a=2, c=NTL), b))
        wf, wfb = self.load_weight(o, d["fourier_w"][j], KD, D, "wf_", kgroup=8); o += KD * D // 2
        csn, csnb = self.aalloc(o, 512, BF16, "csn"); o += 512
        csp, cspb = self.aalloc(o, 512, BF16, "csp"); o += 512
        csn = csn.rearrange("p (c n) -> p c n", c=2)
        csp = csp.rearrange("p (c n) -> p c n", c=2)
        P.dma(csn, d["cs256n"].rearrange("(c p) n -> p c n", p=128), [], [csnb], "fconst")
        P.dma(csp, d["cs256p"].rearrange("(c p) n -> p c n", p=128), [], [cspb], "fconst")
        rows, rowsb = self.aalloc(o, 4 * D, F32, "frows"); o += 4 * D
        rows4 = rows.rearrange("p (s v n) -> p s v n", s=2, v=2)
        for s_ in range(2):
            P.dma(rows4[:, s_], d["modrow"][s_, 0:2, :].partition_broadcast(128), [self.modrowb], [rowsb], "fconst")
        tmp32, tmp32b = self.aalloc(o, D, F32, "ftmp32"); o += D
        UV, UVb = self.aalloc(o, D, BF16, "fUV"); o += D
        UV = UV.rearrange("p (a n) -> p a n", a=2)
        UVT, UVTb = self.aalloc(o, D, BF16, "fUVT"); o += D
        UVT = UVT.rearrange("p (a k n) -> p a k n", a=2, k=KD)
        Fb, Fbb = self.aalloc(o, 512, BF16, "fF"); o += 512
        FT, FTb = self.aalloc(o, 512, BF16, "fFT"); o += 512
        FT = FT.rearrange("p (k n) -> p k n", k=KD)
        fbias, fbiasb = self.aalloc(o, D, F32, "fbias"); o += D
        P.dma(fbias, d["fourier_b"][j].partition_broadcast(128), [], [fbiasb], "fconst")
        Hc, Hcb = self.aalloc(o, cfg.nt_ctx * 512, BF16, "fHc"); o += cfg.nt_ctx * 512
        Hc = Hc.rearrange("p (c n) -> p c n", c=cfg.nt_ctx)

        for t in range(cfg.nt):
            if last and t < cfg.nt_ctx:
                continue
            s_ = 1 if t < cfg.nt_ctx else 0
            xt, xtb, xn, xnb, ss, ssb = self.norm_stats(t)
            P.op("dve", "scalar_tensor_tensor", [xtb, ssb, rowsb], [tmp32b], out=tmp32, in0=xt[:], scalar=ss[:, 1:2], in1=rows4[:, s_, 0, :],
                 op0=ALU.mult, op1=ALU.mult)
            dst = Hc[:, t, :] if s_ else H[:, t - cfg.nt_ctx, :]
            P.op("dve", "tensor_tensor", [tmp32b, rowsb], [Hcb if s_ else Hb], out=dst, in0=tmp32, in1=rows4[:, s_, 1, :], op=ALU.add)

        def out_tile(t, n_chunks, Hsrc, Hsrcb, lhs_c, lhs_s, lhsb, scale):
            dft_part(n_chunks, Hsrc, Hsrcb, lhs_c, lhs_s, lhsb)
            evac_part()
            tail_part(t, scale)

        def dft_part(n_chunks, Hsrc, Hsrcb, lhs_c, lhs_s, lhsb):
            for (a, lhs) in ((0, lhs_c), (1, lhs_s)):
                for h in range(2):
                    bank = a * 2 + h
                    for c in range(n_chunks):
                        P.op("pe", "matmul", [lhsb, Hsrcb], [self.psb[bank]], out=ps[:, bank, :], lhsT=lhs(c), rhs=Hsrc[:, c, h * 512:(h + 1) * 512],
                             start=(c == 0), stop=(c == n_chunks - 1))

        def evac_part():
            for a in range(2):
                P.op("act", "copy", [self.psb[a * 2], self.psb[a * 2 + 1]], [UVb], out=UV[:, a, :], in_=ps[:, a * 2:a * 2 + 2, :].rearrange("p a n -> p (a n)"))

        def tail_part(t, scale):
            for a in range(2):
                self.transpose8(UV[:, a, :], UVb, UVT[:, a], UVTb)
            for g in range(4):
                outp = ps[:, 5 + g // 2, (g % 2) * 256:(g % 2 + 1) * 256]
                n = 0
                for a in range(2):
                    for cc in range(2):
                        P.op("pe", "matmul", [UVTb, csnb], [self.psb[5 + g // 2]], out=outp, lhsT=UVT[:, a, 2 * g + cc, :], rhs=csn[:, cc, a * 256:(a + 1) * 256],
                             start=(n == 0), stop=(n == 3))
                        n += 1
            P.op("act", "activation", [self.psb[5], self.psb[6]], [Fbb], out=Fb, in_=ps[:, 5:7, :].rearrange("p a n -> p (a n)"), func=AF.Copy, scale=scale)
            self.transpose8(Fb, Fbb, FT, FTb)
            st = self.resid_begin(t)
            for h in range(2):
                for k in range(KD):
                    P.op("pe", "matmul", [FTb, wfb[k]], [self.psb[7]], out=ps[:, 7, :], lhsT=FT[:, k, :], rhs=wf[:, k, h * 512:(h + 1) * 512],
                         start=(k == 0), stop=(k == KD - 1))
                self.resid_half(st, 0, h, ps[:, 7, :], self.psb[7], fbias, fbiasb)
            self.resid_end(st)

        if not last:
            assert cfg.n_ctx == 256
            for kt in range(cfg.nt_ctx):
                out_tile(kt, cfg.nt_ctx, Hc, Hcb,
                         lambda c, kt=kt: csp[:, c, kt * 128:(kt + 1) * 128],
                         lambda c, kt=kt: csp[:, c, 256 + kt * 128:256 + (kt + 1) * 128], cspb, 1.0 / math.sqrt(cfg.n_ctx * 256))
        def lat_dft(kt):
            sl, slb = slabs[kt % 2]
            P.dma(sl, d["dft_n"][kt], [], [slb], f"fslab{kt % 2}")
            dft_part(NTL, H, Hb, lambda c, sl=sl: sl[:, 0, c, :], lambda c, sl=sl: sl[:, 1, c, :], slb)

        lat_dft(0)
        for kt in range(NTL):
            evac_part()
            if kt + 1 < NTL:
                lat_dft(kt + 1)
            tail_part(cfg.nt_ctx + kt, 1.0 / math.sqrt(cfg.n_lat * 256))

    def attn_phase(self, i, last):
        P = self.P
        cfg = self.cfg
        d = self.dram
        ps = self.ps
        j = i // 3
        lam_init = 0.8 - 0.6 * math.exp(-0.3 * i)
        NT, NTC, NTOK = cfg.nt, cfg.nt_ctx, cfg.ntok
        ctx_out = not last
        qT_d, kT_d, v_d, onT_d = d["qT_d"], d["kT_d"], d["v_d"], d["onT_d"]
        qTdb, kTdb, vdb, onTdb = Buf("qT_d"), Buf("kT_d"), Buf("v_d"), Buf("onT_d")

        o = 0
        wqkv, wqkvb = self.load_weight(o, d["diff_w_qkv"][j], KD, 3 * D, "wqkv_", kgroup=2); o += KD * 3 * D // 2
        hTa = []
        for q in range(2):
            ap, b = self.aalloc(o, 512, BF16, f"a1hT{q}"); o += 512
            hTa.append((ap.rearrange("p (k n) -> p k n", k=KD), b))
        dbl = []
        for par in range(2):
            S_ = {}
            S_["sq"] = self.aalloc(o, 2048, F32, f"a1sq{par}"); o += 2048
            ap, b = self.aalloc(o, 2048, F32, f"a1qk32{par}"); o += 2048
            S_["qk32"] = (ap.rearrange("p (a g e) -> p a g e", a=2, g=16), [Buf(f"a1qk32q{par}"), Buf(f"a1qk32k{par}")])
            for b_ in S_["qk32"][1]:
                b_.inherit([b])
            rt_ = {}
            for eng in ("dve", "pool"):
                rt_[eng] = []
                for q in range(4):
                    ap, b = self.aalloc(o, 512, F32, f"a1rt{eng}{q}{par}"); o += 512
                    rt_[eng].append((ap.rearrange("p (g a f) -> p g a f", g=16, a=2), b))
            S_["rt"] = rt_
            S_["qkbf"] = []
            for a in range(2):
                S_["qkbf"].append(self.aalloc(o, 512, BF16, f"a1qkbf{a}{par}")); o += 512
            S_["vbf"] = self.aalloc(o, 512, BF16, f"a1vbf{par}"); o += 512
            ap, b = self.aalloc(o, 64, F32, f"a1ssq{par}"); o += 64
            S_["ssq"] = (ap.rearrange("p (a g) -> p a g", a=2), b)
            dbl.append(S_)
        qkT = []
        for q in range(2):
            row = []
            for a in range(2):
                ap, b = self.aalloc(o, 512, BF16, f"a1qkT{q}{a}"); o += 512
                row.append((ap.rearrange("p (k n) -> p k n", k=KD), b))
            qkT.append(row)
        ropes = []
        for q in range(2):
            ap, b = self.aalloc(o, 64, F32, f"a1rope{q}"); o += 64
            ropes.append((ap.rearrange("p (cs a f) -> p cs a f", cs=2, a=2), b))
        gqk, gqkb = self.aalloc(o, 128, F32, "a1g"); o += 128
        gqk = gqk.rearrange("p (a e) -> p a e", a=2)
        P.dma(gqk[:, 0, :], d["diff_q_norm"][j].partition_broadcast(128), [], [gqkb], "aconst")
        P.dma(gqk[:, 1, :], d["diff_k_norm"][j].partition_broadcast(128), [], [gqkb], "aconst")
        P.op("dve", "tensor_scalar", [gqkb], [gqkb], out=gqk[:, 0, :], in0=gqk[:, 0, :], scalar1=64 ** -0.5, scalar2=None, op0=ALU.mult)

        pend_a1 = self.norm_part1(0)
        for t in range(NT):
            is_ctx = t < NTC
            hT, hTb = hTa[t % 2]
            nxt_a1 = self.norm_part1(t + 1) if t + 1 < NT else None
            self.norm_part2(pend_a1, t, 0, hT, hTb, 0, bank=6)
            pend_a1 = nxt_a1
            S_ = dbl[t % 2]
            sq, sqb_ = S_["sq"]
            qk32v, qk32bs = S_["qk32"]
            rt = S_["rt"]
            qkbf = [x[0] for x in S_["qkbf"]]
            qkbfb = [x[1] for x in S_["qkbf"]]
            vbf, vbfb = S_["vbf"]
            ssq, ssqb = S_["ssq"]
            for blk in range(6):
                for k in range(KD):
                    P.op("pe", "matmul", [hTb, wqkvb[k]], [self.psb[blk]], out=ps[:, blk, :], lhsT=hT[:, k, :], rhs=wqkv[:, k, blk * 512:(blk + 1) * 512],
                         start=(k == 0), stop=(k == KD - 1))
            qkps = ps[:, 0:4, :].rearrange("p a n -> p (a n)")
            P.op("act", "activation", [self.psb[0], self.psb[1], self.psb[2], self.psb[3]], [sqb_], out=sq, in_=qkps, func=AF.Square)
            P.op("act", "copy", [self.psb[4], self.psb[5]], [vbfb], out=vbf, in_=ps[:, 4:6, :].rearrange("p a n -> p (a n)"))
            P.dma(v_d[t * 128:(t + 1) * 128, :], vbf, [vbfb], [vdb], "a1v")
            P.op("dve", "tensor_reduce", [sqb_], [ssqb], out=ssq[:, 0, :], in_=sq.rearrange("p (g e) -> p g e", e=64), axis=AX.X, op=ALU.add)
            P.op("act", "activation", [ssqb, self.cb], [ssqb], out=ssq[:, 1, :], in_=ssq[:, 0, :], func=AF.Sqrt, scale=1.0 / 64, bias=self.eps_t[:])
            P.op("dve", "reciprocal", [ssqb], [ssqb], out=ssq[:, 1, :], in_=ssq[:, 1, :])
            if not is_ctx:
                rp, rpb = ropes[t % 2]
                tl = t - NTC
                P.dma(rp.rearrange("p cs a f -> p (cs a f)"), d["rope_tab"][tl * 128:(tl + 1) * 128, :], [], [rpb], f"a1rope{t % 2}")
            for a in range(2):
                eng = "dve" if a == 0 else "pool"
                qk32b = qk32bs[a]
                src = ps[:, 2 * a:2 * a + 2, :].rearrange("p a (g e) -> p (a g) e", e=64)
                P.op("dve", "tensor_tensor", [self.psb[2 * a], self.psb[2 * a + 1], ssqb], [qk32b], out=qk32v[:, a], in0=src,
                     in1=ssq[:, 1, a * 16:(a + 1) * 16].unsqueeze(2).to_broadcast([128, 16, 64]), op=ALU.mult)
                gb_ = gqk[:, a, :].unsqueeze(1).to_broadcast([128, 16, 64])
                if is_ctx:
                    P.op(eng, "tensor_tensor", [qk32b, gqkb], [qkbfb[a]], out=qkbf[a].rearrange("p (g e) -> p g e", e=64), in0=qk32v[:, a], in1=gb_, op=ALU.mult)
                else:
                    P.op(eng, "tensor_tensor", [qk32b, gqkb], [qk32b], out=qk32v[:, a], in0=qk32v[:, a], in1=gb_, op=ALU.mult)
                    xv = qk32v[:, a].rearrange("p g (a h f) -> p g a h f", a=2, h=2)
                    ov = qkbf[a].rearrange("p (g a h f) -> p g a h f", g=16, a=2, h=2)
                    x1, x2 = xv[:, :, :, 0, :], xv[:, :, :, 1, :]
                    cosb = rp[:, 0].unsqueeze(1).to_broadcast([128, 16, 2, 16])
                    sinb = rp[:, 1].unsqueeze(1).to_broadcast([128, 16, 2, 16])
                    (ta, tab), (tb, tbb), (tc, tcb), (td, tdb) = rt[eng]
                    P.op(eng, "tensor_tensor", [qk32b, rpb], [tab], out=ta, in0=x1, in1=cosb, op=ALU.mult)
                    P.op(eng, "tensor_tensor", [qk32b, rpb], [tbb], out=tb, in0=x2, in1=sinb, op=ALU.mult)
                    P.op(eng, "tensor_tensor", [tab, tbb], [qkbfb[a]], out=ov[:, :, :, 0, :], in0=ta, in1=tb, op=ALU.subtract)
                    P.op(eng, "tensor_tensor", [qk32b, rpb], [tcb], out=tc, in0=x1, in1=sinb, op=ALU.mult)
                    P.op(eng, "tensor_tensor", [qk32b, rpb], [tdb], out=td, in0=x2, in1=cosb, op=ALU.mult)
                    P.op(eng, "tensor_tensor", [tcb, tdb], [qkbfb[a]], out=ov[:, :, :, 1, :], in0=tc, in1=td, op=ALU.add)
                dT, dTb = qkT[t % 2][a]
                self.transpose8(qkbf[a], qkbfb[a], dT, dTb, engine="act" if a == 0 else "dve", bank=7)
                dst = (qT_d if a == 0 else kT_d)[:, :, t * 128:(t + 1) * 128].rearrange("h p n -> p h n")
                P.dma(dst, dT, [dTb], [qTdb if a == 0 else kTdb], f"a1qk{t % 2}{a}")

        o = 0
        HW = NTOK // 2
        hb = []
        for q in range(2):
            row = {}
            for nm in ("k", "v", "q"):
                ap, b = self.aalloc(o, HW, BF16, f"a2{nm}{q}"); o += HW
                row[nm] = (ap, b)
            hb.append(row)
        PT = []
        for q in range(3):
            ap, b = self.aalloc(o, 256, BF16, f"a2PT{q}"); o += 256
            PT.append((ap, b))
        ones, onesb = self.aalloc(o, 64, BF16, "a2ones"); o += 64
        ones = ones.rearrange("p (a b) -> p a b", a=1)[:, 0, :]
        onesm, onesmb = self.aalloc(o, 64, BF16, "a2onesm"); o += 64
        f32t = {}
        for nm in ("r0", "r1", "t0", "o32", "rstd"):
            f32t[nm] = self.aalloc(o, 512, F32, f"a2{nm}"); o += 512
        sqh, sqhb = self.aalloc(o, 256, BF16, "a2sq"); o += 256
        onTs = []
        for q in range(2):
            onTs.append(self.aalloc(o, 256, BF16, f"a2onT{q}")); o += 256
        lamt, lamb = self.aalloc(o, 256 + 8, F32, "a2lam"); o += 264
        lam4 = lamt[:, 0:256].rearrange("p (a e) -> p a e", a=4)
        lsc = lamt[:, 256:264]
        subc, subcb = self.aalloc(o, 2, F32, "a2subc"); o += 2
        A2END = o
        P.op("pool", "memset", [], [onesb], ap=ones, constant=1.0)
        P.op("pool", "memset", [], [onesmb], ap=onesm, constant=1.0 / 128)
        P.dma(lam4, d["diff_lambda"][j].partition_broadcast(128), [], [lamb], "aconst")
        P.op("dve", "tensor_tensor", [lamb], [lamb], out=lam4[:, 0, :], in0=lam4[:, 0, :], in1=lam4[:, 1, :], op=ALU.mult)
        P.op("dve", "tensor_tensor", [lamb], [lamb], out=lam4[:, 2, :], in0=lam4[:, 2, :], in1=lam4[:, 3, :], op=ALU.mult)
        P.op("dve", "tensor_reduce", [lamb], [lamb], out=lsc[:, 0:1], in_=lam4[:, 0, :], axis=AX.X, op=ALU.add)
        P.op("dve", "tensor_reduce", [lamb], [lamb], out=lsc[:, 1:2], in_=lam4[:, 2, :], axis=AX.X, op=ALU.add)
        P.op("act", "activation", [lamb], [lamb], out=lsc[:, 2:4], in_=lsc[:, 0:2], func=AF.Exp)
        P.op("dve", "tensor_tensor", [lamb], [lamb], out=lsc[:, 4:5], in0=lsc[:, 3:4], in1=lsc[:, 2:3], op=ALU.subtract)
        P.op("dve", "tensor_scalar", [lamb], [lamb], out=lsc[:, 5:6], in0=lsc[:, 4:5], scalar1=-lam_init, scalar2=None, op0=ALU.add)
        P.dma(subc[:, 0:1], d["diff_subln"][j].unsqueeze(1), [], [subcb], "aconst")
        P.op("dve", "tensor_scalar", [subcb], [subcb], out=subc[:, 1:2], in0=subc[:, 0:1], scalar1=1.0 - lam_init, scalar2=None, op0=ALU.mult)
        neglam = lsc[:, 5:6]

        A3OFF = 2 * self.WREG
        wo, wob = self.load_weight(A3OFF, d["diff_w_o"][j], KD, D, "wo_", kgroup=8)

        qblocks = []
        if ctx_out:
            qblocks.append((0, cfg.n_ctx, list(range(NTC))))
        for b0 in range(0, cfg.n_lat, 512):
            qblocks.append((cfg.n_ctx + b0, min(512, cfg.n_lat - b0), list(range(NT))))
        STAGES = [(0, 1), (6, 7)]
        MSB = 5
        (r0, r0b), (r1, r1b), (t0, t0b), (o32, o32b), (rstd, rstdb) = (f32t[n] for n in ("r0", "r1", "t0", "o32", "rstd"))

        def load_head(h):
            kh, khb = hb[h % 2]["k"]
            vh, vhb = hb[h % 2]["v"]
            qh, qhb = hb[h % 2]["q"]
            vh3 = vh.rearrange("p (t e) -> p t e", e=128)
            P.dma(kh, kT_d[h], [kTdb], [khb], f"a2k{h % 2}")
            P.dma(qh, qT_d[h], [qTdb], [qhb], f"a2q{h % 2}")
            P.dma(vh3, v_d[:, h * 128:(h + 1) * 128].rearrange("(t p) e -> p t e", p=128), [vdb], [vhb], f"a2v{h % 2}")

        items = []
        for h in range(8):
            for qi, (q0, nq, ktiles) in enumerate(qblocks):
                for c in range(2):
                    for g0 in range(0, len(ktiles), 2):
                        items.append((h, qi, c, g0, ktiles[g0:g0 + 2], len(ktiles)))
        PTP = []
        o2 = A2END
        for q in range(3):
            PTP.append(self.aalloc(o2, 512, BF16, f"a2PTP{q}")); o2 += 512
        A2END = o2
        oni = [0]

        def issue_S(i):
            h, qi, c, g0, kts, nk = items[i]
            q0, nq, _ = qblocks[qi]
            kh, khb = hb[h % 2]["k"]
            qh, qhb = hb[h % 2]["q"]
            for u, kt in enumerate(kts):
                bank = STAGES[i % 2][u]
                P.op("pe", "matmul", [khb, qhb], [self.psb[bank]], out=ps[:, bank, :nq], lhsT=kh[c * 64:(c + 1) * 64, kt * 128:(kt + 1) * 128],
                     rhs=qh[c * 64:(c + 1) * 64, q0:q0 + nq], start=True, stop=True)

        def issue_rest(i):
            h, qi, c, g0, kts, nk = items[i]
            q0, nq, _ = qblocks[qi]
            vh, vhb = hb[h % 2]["v"]
            vh3 = vh.rearrange("p (t e) -> p t e", e=128)
            b0 = STAGES[i % 2][0]
            ng = len(kts)
            pt, ptb = PTP[i % 3]
            pt3 = pt.rearrange("p (u n) -> p u n", u=2)
            ob, zb = 2 + 2 * c, 3 + 2 * c
            P.op("act", "activation", [self.psb[STAGES[i % 2][u]] for u in range(ng)], [ptb], out=pt3[:, 0:ng, :nq], in_=ps[:, b0:b0 + ng, :nq], func=AF.Exp)
            for u, kt in enumerate(kts):
                P.op("pe", "matmul", [ptb, vhb], [self.psb[ob]], out=ps[:, ob, :nq], lhsT=vh3[:, kt, :], rhs=pt3[:, u, :nq], start=(g0 + u == 0), stop=(g0 + u == nk - 1))
            for u, kt in enumerate(kts):
                P.op("pe", "matmul", [ptb, onesb], [self.psb[zb]], out=ps[:, zb, :nq], lhsT=ones, rhs=pt3[:, u, :nq], start=(g0 + u == 0), stop=(g0 + u == nk - 1))
            if g0 + ng < nk:
                return
            if c == 0:
                P.op("dve", "reciprocal", [self.psb[3]], [r0b], out=r0[:, :nq], in_=ps[:, 3, :nq])
                P.op("dve", "tensor_tensor", [self.psb[2], r0b], [t0b], out=t0[:, :nq], in0=ps[:, 2, :nq], in1=r0[:, :nq], op=ALU.mult)
                return
            P.op("dve", "reciprocal", [self.psb[5]], [r1b], out=r1[:, :nq], in_=ps[:, 5, :nq])
            P.op("dve", "tensor_tensor", [self.psb[4], r1b], [r1b], out=r1[:, :nq], in0=ps[:, 4, :nq], in1=r1[:, :nq], op=ALU.mult)
            P.op("dve", "scalar_tensor_tensor", [r1b, t0b, lamb], [o32b], out=o32[:, :nq], in0=r1[:, :nq], scalar=neglam, in1=t0[:, :nq], op0=ALU.mult, op1=ALU.add)
            P.op("dve", "tensor_tensor", [o32b], [sqhb], out=sqh[:, :nq], in0=o32[:, :nq], in1=o32[:, :nq], op=ALU.mult)
            P.op("pe", "matmul", [sqhb, onesmb], [self.psb[MSB]], out=ps[:, MSB, :nq], lhsT=onesm, rhs=sqh[:, :nq], start=True, stop=True)
            P.op("act", "activation", [self.psb[MSB], self.cb], [rstdb], out=rstd[:, :nq], in_=ps[:, MSB, :nq], func=AF.Ln, bias=self.eps_t[:])
            P.op("act", "activation", [rstdb], [rstdb], out=rstd[:, :nq], in_=rstd[:, :nq], func=AF.Exp, scale=-0.5)
            onT, onTb = onTs[oni[0] % 2]
            oni[0] += 1
            P.op("dve", "scalar_tensor_tensor", [o32b, rstdb, subcb], [onTb], out=onT[:, :nq], in0=o32[:, :nq], scalar=subc[:, 1:2], in1=rstd[:, :nq],
                 op0=ALU.mult, op1=ALU.mult)
            P.dma(onT_d[h, :, q0:q0 + nq], onT[:, :nq], [onTb], [onTdb], f"a2on{oni[0] % 2}")

        load_head(0)
        load_head(1)
        per_head = len(items) // 8
        DEPTH = 1
        for i in range(len(items) + DEPTH):
            if i < len(items):
                issue_S(i)
            jx = i - DEPTH
            if jx >= 0:
                issue_rest(jx)
                if jx % per_head == per_head - 1 and jx // per_head + 2 < 8:
                    load_head(jx // per_head + 2)

        o = A3OFF + KD * D // 2
        onb = []
        for q in range(2):
            ap, b = self.aalloc(o, 2048, BF16, f"a3on{q}"); o += 2048
            onb.append((ap.rearrange("p (h n) -> p h n", h=8), b))
        tiles = list(range(0 if ctx_out else NTC, NT))
        blocks = [tiles[a:a + 4] for a in range(0, len(tiles), 4)]
        for bi, blk in enumerate(blocks):
            nb = len(blk) * 128
            tok0 = blk[0] * 128
            on, onbb = onb[bi % 2]
            P.dma(on[:, :, :nb], onT_d[:, :, tok0:tok0 + nb].rearrange("h p n -> p h n"), [onTdb], [onbb], f"a3on{bi % 2}")
            for q, t in enumerate(blk):
                st = self.resid_begin(t)
                for hf in range(2):
                    bank = (2 * t + hf) % 4
                    for hh in range(8):
                        P.op("pe", "matmul", [onbb, wob[hh]], [self.psb[bank]], out=ps[:, bank, :], lhsT=on[:, hh, q * 128:(q + 1) * 128],
                             rhs=wo[:, hh, hf * 512:(hf + 1) * 512], start=(hh == 0), stop=(hh == 7))
                    self.resid_half(st, 0, hf, ps[:, bank, :], self.psb[bank])
                self.resid_end(st)

    def cmul(self, eng, o_r, o_i, a_r, a_i, b_r, b_i, t1, t2, rb, wb, neg_im=False):
        P = self.P
        P.op(eng, "tensor_tensor", rb, wb, out=t1, in0=a_r, in1=b_r, op=ALU.mult)
        P.op(eng, "tensor_tensor", rb, wb, out=t2, in0=a_i, in1=b_i, op=ALU.mult)
        P.op(eng, "tensor_tensor", rb + wb, wb, out=o_r, in0=t1, in1=t2, op=ALU.subtract)
        P.op(eng, "tensor_tensor", rb + wb, wb, out=t1, in0=a_r, in1=b_i, op=ALU.mult)
        P.op(eng, "tensor_tensor", rb + wb, wb, out=t2, in0=a_i, in1=b_r, op=ALU.mult)
        if neg_im:
            P.op(eng, "scalar_tensor_tensor", rb + wb, wb, out=o_i, in0=t1, scalar=-1.0, in1=t2, op0=ALU.mult, op1=ALU.subtract)
        else:
            P.op(eng, "tensor_tensor", rb + wb, wb, out=o_i, in0=t1, in1=t2, op=ALU.add)

    def s5_phase(self, i, last):
        P = self.P
        cfg = self.cfg
        d = self.dram
        ps = self.ps
        j = i // 3
        NT, NTC, NTOK = cfg.nt, cfg.nt_ctx, cfg.ntok
        NCH = NTOK // 8
        NCC = cfg.n_ctx // 8
        NCL = cfg.n_lat // 8
        L1 = 32
        NL1 = NCH // L1
        assert NCH % L1 == 0 and NCC == 32 and NCL <= 512
        ctx_out = not last
        hT_d = d["hT_d"]
        G = 64
        TWO_PI = 2.0 * math.pi

        o = 0
        hTa = []
        for q in range(2):
            ap, b = self.aalloc(o, 512, BF16, f"s0hT{q}"); o += 512
            hTa.append((ap.rearrange("p (k n) -> p k n", k=KD), b))
        S0END = o

        P.record_begin()
        o = S0END
        ot = self.ARENA_WORDS - 13600
        tb = Buf("s5tabs")

        def talloc(words, dtype=F32, tmp=False):
            nonlocal o, ot
            off = ot if tmp else o
            ap, b = self.aalloc(off, words, dtype, "s5t")
            if tmp:
                ot += words
            else:
                o += words
            tb.inherit([b])
            self.alive = [(a_, n_, b_) for (a_, n_, b_) in self.alive if b_ is not b] + [(off, words, tb)]
            return ap

        Jc = talloc(8 * 240 // 2, BF16).rearrange("p (a c) -> p a c", a=8)
        mask01 = talloc(NCH)
        maskFB = talloc(256).rearrange("p (a n) -> p a n", a=2)
        id32 = talloc(128)
        negpi = talloc(2)
        P.dma(Jc, d["s5_J"].rearrange("a p c -> p a c"), [], [tb], "s5c")
        P.dma(mask01, d["s5_mask01"].partition_broadcast(128), [], [tb], "s5c")
        P.dma(maskFB, d["s5_maskFB"].rearrange("a p n -> p a n"), [], [tb], "s5c")
        P.dma(id32, d["ident32"], [], [tb], "s5c")
        P.op("dve", "memset", [], [tb], ap=negpi[:, 0:1], constant=-math.pi * (1 - 1e-6))
        lamr = talloc(G, tmp=True); lami = talloc(G, tmp=True); dt = talloc(G, tmp=True)
        dcol = talloc(G)
        P.dma(lamr, d["s5_lamr_t"][j], [], [tb], "s5c")
        P.dma(lami, d["s5_lami_t"][j], [], [tb], "s5c")
        for dr in range(2):
            P.dma(dt[dr * 64:(dr + 1) * 64, :], d["s5_log_dt"][j, dr].partition_broadcast(64), [], [tb], "s5c")
        P.dma(dcol, d["s5_dcol"][j], [], [tb], "s5c")
        T = [tb]

        def tt(out, in0, in1, op):
            P.op("dve", "tensor_tensor", T, T, out=out, in0=in0, in1=in1, op=op)

        def ts(out, in0, s1, op0, s2=None, op1=None):
            if op1 is None:
                P.op("dve", "tensor_scalar", T, T, out=out, in0=in0, scalar1=s1, scalar2=None, op0=op0)
            else:
                P.op("dve", "tensor_scalar", T, T, out=out, in0=in0, scalar1=s1, scalar2=s2, op0=op0, op1=op1)

        def act(out, in_, func, **kw):
            P.op("act", "activation", T, T, out=out, in_=in_, func=func, **kw)

        sc = [talloc(G * 32, tmp=True) for _ in range(2)]
        sc1 = [talloc(G, tmp=True) for _ in range(6)]
        lrd = talloc(G, tmp=True); lid = talloc(G, tmp=True); mag = talloc(G, tmp=True)
        ar = talloc(G, tmp=True); ai = talloc(G, tmp=True)
        kr = talloc(G, tmp=True); ki = talloc(G, tmp=True)
        def stt(out, in0, scalar, in1, op0, op1):
            P.op("dve", "scalar_tensor_tensor", T, T, out=out, in0=in0, scalar=scalar, in1=in1, op0=op0, op1=op1)

        def exp_taylor(out, x, deg):
            r_ = sc1[0]
            P.op("dve", "memset", T, T, ap=r_, constant=1.0)
            for k in range(deg, 0, -1):
                tt(r_, r_, x, ALU.mult)
                ts(r_, r_, 1.0 / k, ALU.mult, 1.0, ALU.add)
            P.op("dve", "tensor_copy", T, T, out=out, in_=r_)

        ts(dt, dt, 0.125, ALU.mult)
        exp_taylor(dt, dt, 11)
        for _ in range(3):
            tt(dt, dt, dt, ALU.mult)
        tt(lrd, lamr, dt, ALU.mult)
        tt(lid, lami, dt, ALU.mult)
        exp_taylor(mag, lrd, 7)
        iscr = talloc(G, I32, tmp=True)

        def sin_of(out, theta, offs):
            y, n_, fr, lt = sc1[0], sc1[1], sc1[2], sc1[3]
            ts(y, theta, 1.0 / TWO_PI, ALU.mult, offs, ALU.add)
            P.op("dve", "tensor_copy", T, T, out=iscr, in_=y)
            P.op("dve", "tensor_copy", T, T, out=n_, in_=iscr)
            tt(fr, y, n_, ALU.subtract)
            ts(lt, fr, 0.0, ALU.is_lt)
            tt(fr, fr, lt, ALU.add)
            ts(lt, fr, 1.0, ALU.is_ge)
            tt(fr, fr, lt, ALU.subtract)
            xx, x2, r_ = sc1[0], sc1[1], sc1[3]
            ts(xx, fr, TWO_PI, ALU.mult, -math.pi, ALU.add)
            tt(x2, xx, xx, ALU.mult)
            coef = [(-1.0) ** k / math.factorial(2 * k + 1) for k in range(12)]
            ts(r_, x2, coef[11], ALU.mult)
            for k in range(10, 0, -1):
                stt(r_, r_, coef[k], x2, ALU.add, ALU.mult)
            stt(out, r_, coef[0], xx, ALU.add, ALU.mult)

        sn = sc1[4]; cs_ = sc1[5]
        sin_of(sn, lid, 8.5)
        sin_of(cs_, lid, 8.75)
        tt(ar, mag, cs_, ALU.mult)
        tt(ai, mag, sn, ALU.mult)
        nre, den, t_a, t_b = sc1[0], sc1[1], sc1[2], sc1[3]
        ts(nre, ar, -1.0, ALU.add)
        tt(den, lamr, lamr, ALU.mult)
        tt(t_a, lami, lami, ALU.mult)
        tt(den, den, t_a, ALU.add)
        P.op("dve", "reciprocal", T, T, out=den, in_=den)
        tt(t_a, nre, lamr, ALU.mult)
        tt(t_b, ai, lami, ALU.mult)
        tt(t_a, t_a, t_b, ALU.add)
        tt(kr, t_a, den, ALU.mult)
        tt(t_a, ai, lamr, ALU.mult)
        tt(t_b, nre, lami, ALU.mult)
        tt(t_a, t_a, t_b, ALU.subtract)
        tt(ki, t_a, den, ALU.mult)
        apr = talloc(G * 9, tmp=True).rearrange("p (g e) -> p g e", g=G); api = talloc(G * 9, tmp=True).rearrange("p (g e) -> p g e", g=G)
        anr = talloc(G * 9, tmp=True).rearrange("p (g e) -> p g e", g=G); ani = talloc(G * 9, tmp=True).rearrange("p (g e) -> p g e", g=G)
        ainr, aini = talloc(G, tmp=True), talloc(G, tmp=True)
        tt(t_a, ar, ar, ALU.mult)
        tt(t_b, ai, ai, ALU.mult)
        tt(t_a, t_a, t_b, ALU.add)
        P.op("dve", "reciprocal", T, T, out=t_a, in_=t_a)
        tt(ainr, ar, t_a, ALU.mult)
        P.op("dve", "scalar_tensor_tensor", T, T, out=aini, in0=ai, scalar=-1.0, in1=t_a, op0=ALU.mult, op1=ALU.mult)
        for (pr, pi, br_, bi_) in ((apr, api, ar, ai), (anr, ani, ainr, aini)):
            P.op("dve", "memset", T, T, ap=pr[:, :, 0], constant=1.0)
            P.op("dve", "memset", T, T, ap=pi[:, :, 0], constant=0.0)
            P.op("dve", "tensor_copy", T, T, out=pr[:, :, 1], in_=br_)
            P.op("dve", "tensor_copy", T, T, out=pi[:, :, 1], in_=bi_)
            for e in range(2, 9):
                self.cmul("dve", pr[:, :, e], pi[:, :, e], pr[:, :, e - 1], pi[:, :, e - 1], br_, bi_, sc1[0], sc1[1], T, T)
        def t8(tmp=False):
            return talloc(G * 8, tmp=tmp).rearrange("p (g e) -> p g e", g=G)
        sXr, sXi = t8(True), t8(True)
        AYr, AYi, ATr, ATi, AKr, AKi = [t8() for _ in range(6)]
        F_, B_ = slice(0, 64), slice(64, 128)
        for (dst, src) in ((sXr, apr), (sXi, api)):
            P.op("act", "copy", T, T, out=dst[B_, :, :], in_=src[B_, :, 0:8])
            for e in range(8):
                P.op("act", "copy", T, T, out=dst[F_, :, e], in_=src[F_, :, 7 - e])
        for (dst, src) in ((AYr, apr), (AYi, api)):
            P.op("act", "copy", T, T, out=dst[F_, :, :], in_=src[F_, :, 1:9])
            for e in range(8):
                P.op("act", "copy", T, T, out=dst[B_, :, e], in_=src[B_, :, 8 - e])
        for (dst, src) in ((ATr, anr), (ATi, ani)):
            P.op("act", "copy", T, T, out=dst[B_, :, :], in_=src[B_, :, 0:8])
            for e in range(8):
                P.op("act", "copy", T, T, out=dst[F_, :, e], in_=src[F_, :, 7 - e])
        s8a = sc[0][:, 0:G * 8].rearrange("p (g e) -> p g e", g=G)
        s8b = sc[1][:, 0:G * 8].rearrange("p (g e) -> p g e", g=G)
        kbr = kr.unsqueeze(2).to_broadcast([128, G, 8])
        kbi = ki.unsqueeze(2).to_broadcast([128, G, 8])
        self.cmul("dve", AKr, AKi, sXr, sXi, kbr, kbi, s8a, s8b, T, T)
        def t32(tmp=False):
            return talloc(G * L1, tmp=tmp).rearrange("p (g e) -> p g e", g=G)
        Ppr, Ppi = t32(True), t32(True)
        Qr, Qi = t32(), t32()
        A32r, A32i = talloc(G, tmp=True), talloc(G, tmp=True)
        s32a = sc[0].rearrange("p (g e) -> p g e", g=G)
        s32b = sc[1].rearrange("p (g e) -> p g e", g=G)
        sqr, sqi = talloc(G, tmp=True), talloc(G, tmp=True)

        def build_pow(pr, pi, base_r, base_i, nmax, final_sq=None):
            P.op("dve", "memset", T, T, ap=pr[:, :, 0], constant=1.0)
            P.op("dve", "memset", T, T, ap=pi[:, :, 0], constant=0.0)
            P.op("dve", "tensor_copy", T, T, out=pr[:, :, 1], in_=base_r)
            P.op("dve", "tensor_copy", T, T, out=pi[:, :, 1], in_=base_i)
            P.op("dve", "tensor_copy", T, T, out=sqr, in_=base_r)
            P.op("dve", "tensor_copy", T, T, out=sqi, in_=base_i)
            n = 2
            while True:
                self.cmul("dve", sc1[2], sc1[3], sqr, sqi, sqr, sqi, sc1[0], sc1[1], T, T)
                P.op("dve", "tensor_copy", T, T, out=sqr, in_=sc1[2])
                P.op("dve", "tensor_copy", T, T, out=sqi, in_=sc1[3])
                if n >= nmax:
                    break
                m = min(n, nmax - n)
                self.cmul("dve", pr[:, :, n:n + m], pi[:, :, n:n + m], pr[:, :, 0:m], pi[:, :, 0:m],
                          sqr.unsqueeze(2).to_broadcast([128, G, m]), sqi.unsqueeze(2).to_broadcast([128, G, m]),
                          s32a[:, :, 0:m], s32b[:, :, 0:m], T, T)
                n *= 2
            if final_sq is not None:
                P.op("dve", "tensor_copy", T, T, out=final_sq[0], in_=sqr)
                P.op("dve", "tensor_copy", T, T, out=final_sq[1], in_=sqi)

        build_pow(Qr, Qi, anr[:, :, 8], ani[:, :, 8], L1)
        build_pow(Ppr, Ppi, apr[:, :, 8], api[:, :, 8], L1, final_sq=(A32r, A32i))
        a8ir = anr[:, :, 8].unsqueeze(2).to_broadcast([128, G, L1])
        a8ii = ani[:, :, 8].unsqueeze(2).to_broadcast([128, G, L1])
        tmpPr, tmpPi = t32(), t32()
        self.cmul("dve", tmpPr, tmpPi, Ppr, Ppi, a8ir, a8ii, s32a, s32b, T, T)
        Ppr, Ppi = tmpPr, tmpPi
        NM = NL1 + 1
        rr = talloc(G, tmp=True); ur = talloc(G, tmp=True); ui = talloc(G, tmp=True)
        act(rr, lrd, AF.Exp, scale=float(8 * L1))
        P.op("dve", "reciprocal", T, T, out=t_a, in_=rr)
        tt(ur, A32r, t_a, ALU.mult)
        tt(ui, A32i, t_a, ALU.mult)
        NMP = NM
        Umr = talloc(G * NMP).rearrange("p (g e) -> p g e", g=G); Umi = talloc(G * NMP).rearrange("p (g e) -> p g e", g=G)
        build_pow(Umr, Umi, ur, ui, NMP)
        Vr = talloc(G * NL1).rearrange("p (g e) -> p g e", g=G); Vi = talloc(G * NL1).rearrange("p (g e) -> p g e", g=G)
        rmask = talloc(G * NL1).rearrange("p (g e) -> p g e", g=G)
        rb3 = rr.unsqueeze(2).to_broadcast([128, G, NL1])
        tt(Vr, Umr[:, :, 0:NL1], rb3, ALU.mult)
        P.op("dve", "scalar_tensor_tensor", T, T, out=Vi, in0=Umi[:, :, 0:NL1], scalar=-1.0, in1=rb3, op0=ALU.mult, op1=ALU.mult)
        P.op("dve", "tensor_copy", T, T, out=rmask, in_=rb3)
        P.op("dve", "memset", T, T, ap=rmask[:, :, 0], constant=0.0)
        l2 = talloc(10 * NL1 + 4)
        l2B = talloc(10 * NL1 + 4)
        TABEND = o
        rec = P.record_end()
        per_tile = (len(rec) + NT - 1) // NT
        pend = self.norm_part1(0)
        for t in range(NT):
            hT, hTb = hTa[t % 2]
            nxt_ = self.norm_part1(t + 1) if t + 1 < NT else None
            self.norm_part2(pend, t, 0, hT, hTb, 0, bank=7)
            pend = nxt_
            P.dma(hT_d[:, :, t * 128:(t + 1) * 128].rearrange("k p n -> p k n"), hT, [hTb], [self.hTdb], f"s0st{t % 2}")
            P.replay(rec, per_tile)
        P.replay(rec, len(rec))
        stage = getattr(cfg, "s5_stage", 99)
        if getattr(cfg, "s5_dump", False):
            for nm, ap_, w in (("ar", ar, G), ("ai", ai, G), ("kr", kr, G), ("ki", ki, G), ("AKr", AKr, G * 8), ("AKi", AKi, G * 8), ("AYr", AYr, G * 8), ("ATi", ATi, G * 8),
                               ("Qr", Qr, G * L1), ("Qi", Qi, G * L1), ("Ppr", Ppr, G * L1), ("Ppi", Ppi, G * L1), ("Umr", Umr, G * NM), ("Umi", Umi, G * NM),
                               ("Vr", Vr, G * NL1), ("Vi", Vi, G * NL1), ("rmask", rmask, G * NL1)):
                flat = ap_ if len(ap_.shape) == 2 else ap_.rearrange("p g e -> p (g e)")
                self.dump("dbg_" + nm, flat, tb, [128, w])
        if stage <= 1:
            return

        hTf, hTfb = self.aalloc(o, NTOK // 2, BF16, "s2hT"); o += NTOK // 2
        Up, Upb = self.aalloc(o, 8 * NCH // 2, BF16, "s2U"); o += 8 * NCH // 2
        Up = Up.rearrange("p (g n) -> p g n", g=8)
        zp, zpb = self.aalloc(o, 8 * NCH // 2, BF16, "s2z"); o += 8 * NCH // 2
        zp = zp.rearrange("p (g n) -> p g n", g=8)
        wts = {}
        for nm in ("zr", "zi", "yr", "yi", "tg", "yrB", "yiB"):
            ap, b = self.aalloc(o, 512, BF16, f"s2w{nm}"); o += 512
            wts[nm] = (ap.rearrange("p (g n) -> p g n", g=8), b)
        for nm in ("yrB", "yiB"):
            P.op("dve", "memset", [], [wts[nm][1]], ap=wts[nm][0][0:64, :, :], constant=0.0)
        gt = {}
        for nm in ("xr", "xi", "ytr", "yti", "m1", "m2"):
            ap, b = self.aalloc(o, 1024, F32, f"s2g{nm}"); o += 1024
            gt[nm] = (ap, b)
        ch = []
        for q in range(4):
            ap, b = self.aalloc(o, NCH + 1, F32, f"s2c{q}"); o += NCH + 1
            ch.append((ap, b))
        Eb = []
        for q in range(2):
            ap, b = self.aalloc(o, NCH // 2, BF16, f"s2E{q}"); o += NCH // 2
            Eb.append((ap, b))
        bcft, bcftb = self.aalloc(o, 4 * 128, F32, "s2bc"); o += 512
        bcft = bcft.rearrange("p (a g h) -> p a g h", a=4, g=8)
        P.op("dve", "memset", [], [ch[0][1]], ap=ch[0][0][:, 0:1], constant=0.0)
        P.op("dve", "memset", [], [ch[2][1]], ap=ch[2][0][:, 0:1], constant=0.0)

        def seg_split(s0, s1, flat0):
            out = []
            s = s0
            while s < s1:
                f = flat0 + s
                bank, c0 = f // 512, f % 512
                n = min(s1 - s, 512 - c0)
                out.append((bank, c0, n, s))
                s += n
            return out

        def bwd_cols(s_start, n, seg_lo, seg_hi):
            k0 = seg_lo + seg_hi - 1 - s_start
            k1 = k0 - n
            return k0, (k1 if k1 >= 0 else None)

        segs = [(0, NCC), (NCC, NCH)]
        ZB0 = 0
        YB = 3

        rec_unpack_prev = None
        pbk = 0
        for ft in range(KD):
            gsl = slice(ft * 8, ft * 8 + 8)
            P.record_begin()
            P.dma(hTf, hT_d[ft], [self.hTdb], [hTfb], "s2ld")
            rec_load = P.record_end()
            P.record_begin()
            (xr, xrb), (xi, xib), (ytr, ytrb), (yti, ytib), (m1, m1b), (m2, m2b) = (gt[n] for n in ("xr", "xi", "ytr", "yti", "m1", "m2"))
            v4 = lambda ap: ap.rearrange("p (g i h) -> p g i h", g=8, i=8)
            for a_, nm_ in enumerate(("s5_br_t", "s5_bi_t", "s5_cr_t", "s5_ci_t")):
                P.dma(bcft[:, a_], d[nm_][j][:, gsl, :], [], [bcftb], "s2bc")
            bc_e = lambda tab: tab[:, gsl, :].unsqueeze(3).to_broadcast([128, 8, 8, 16])
            bc_h = lambda a_: bcft[:, a_].unsqueeze(2).to_broadcast([128, 8, 8, 16])
            Br, Bi, Cr, Ci = 0, 1, 2, 3
            T2 = [tb, bcftb]
            self.cmul("dve", v4(xr), v4(xi), bc_e(AKr), bc_e(AKi), bc_h(Br), bc_h(Bi), v4(m1), v4(m2), T2, [xrb, xib, m1b, m2b])
            (wyr, wyrb), (wyi, wyib) = wts["yr"], wts["yi"]
            self.cmul("dve", v4(wyr.rearrange("p g n -> p (g n)")), v4(wyi.rearrange("p g n -> p (g n)")), bc_e(AYr), bc_e(AYi), bc_h(Cr), bc_h(Ci),
                      v4(m1), v4(m2), T2, [wyrb, wyib, m1b, m2b], neg_im=True)
            for (src_, srcb_, nmB) in ((wyr, wyrb, "yrB"), (wyi, wyib, "yiB")):
                wB, wBb = wts[nmB]
                P.op("act", "copy", [srcb_], [wBb], out=wB[64:128, :, :], in_=src_[64:128, :, :])
                P.op("dve", "memset", [wBb], [srcb_], ap=src_[64:128, :, :], constant=0.0)
            self.cmul("dve", v4(ytr), v4(yti), bc_e(ATr), bc_e(ATi), bc_h(Cr), bc_h(Ci), v4(m1), v4(m2), T2, [ytrb, ytib, m1b, m2b], neg_im=True)
            rec_wg_dve = P.record_end()
            P.record_begin()
            for (src, srcb, nm) in ((xr, xrb, "zr"), (xi, xib, "zi")):
                wz, wzb = wts[nm]
                for g in range(8):
                    P.op("pe", "transpose", [srcb, tb], [self.psb[5 + g // 4]], out=ps[:, 5 + g // 4, (g % 4) * 128:(g % 4 + 1) * 128],
                         in_=src[:, g * 128:(g + 1) * 128], identity=id32)
                P.op("act", "copy", [self.psb[5], self.psb[6]], [wzb], out=wz.rearrange("p g n -> p (g n)"), in_=ps[:, 5:7, :].rearrange("p a n -> p (a n)"))
            tg, tgb = wts["tg"]
            for hb_ in range(2):
                for (dr, bank) in ((0, 5), (1, 6)):
                    psl = slice(dr * 64, dr * 64 + 64)
                    for g4 in range(4):
                        g = hb_ * 4 + g4
                        outp = ps[:, bank, g4 * 128:(g4 + 1) * 128]
                        P.op("pe", "matmul", [xrb, ytrb], [self.psb[bank]], out=outp, lhsT=xr[psl, g * 128:(g + 1) * 128], rhs=ytr[psl, g * 128:(g + 1) * 128], start=True, stop=False)
                        P.op("pe", "matmul", [xib, ytib], [self.psb[bank]], out=outp, lhsT=xi[psl, g * 128:(g + 1) * 128], rhs=yti[psl, g * 128:(g + 1) * 128], start=False, stop=True)
                mF = maskFB[:, 0, :].unsqueeze(1).to_broadcast([128, 4, 128])
                mB = maskFB[:, 1, :].unsqueeze(1).to_broadcast([128, 4, 128])
                m1v = m1[:, 0:512].rearrange("p (g n) -> p g n", g=4)
                m2v = m2[:, 0:512].rearrange("p (g n) -> p g n", g=4)
                P.op("dve", "tensor_tensor", [self.psb[5], tb], [m1b], out=m1v, in0=ps[:, 5, :].rearrange("p (g n) -> p g n", g=4), in1=mF, op=ALU.mult)
                P.op("dve", "tensor_tensor", [self.psb[6], tb], [m2b], out=m2v, in0=ps[:, 6, :].rearrange("p (g n) -> p g n", g=4), in1=mB, op=ALU.mult)
                P.op("dve", "tensor_tensor", [m1b, m2b], [tgb], out=tg[:, hb_ * 4:(hb_ + 1) * 4, :], in0=m1v, in1=m2v, op=ALU.add)
            rec_wg_pe = P.record_end()
            P.record_begin()
            hv = hTf.rearrange("p (k i) -> p k i", i=8)
            for gg in range(8):
                for (lo, hi) in segs:
                    n = hi - lo
                    bank = (7, 5, 6, 0, 1, 2)[pbk % 6]
                    pbk += 1
                    for ii in range(8):
                        P.op("pe", "matmul", [hTfb, tb], [self.psb[bank]], out=ps[:, bank, 0:n], lhsT=Jc[:, gg, 112 - 16 * ii:240 - 16 * ii], rhs=hv[:, lo:hi, ii],
                             start=(ii == 0), stop=(ii == 7))
                    P.op("act", "copy", [self.psb[bank]], [Upb], out=Up[:, gg, lo:hi], in_=ps[:, bank, 0:n])
            rec_pack = P.record_end()
            if rec_unpack_prev is not None:
                P.replay(rec_unpack_prev, len(rec_unpack_prev))
            P.replay(rec_wg_dve, len(rec_wg_dve))
            P.replay(rec_load, len(rec_load))
            P.replay(rec_pack, len(rec_pack))
            P.replay(rec_wg_pe, len(rec_wg_pe))
            wzr, wzrb = wts["zr"]
            wzi, wzib = wts["zi"]
            psflat = ps.rearrange("p a n -> p (a n)")
            setA = dict(ch=ch, Eb=Eb, l2=l2, zb0=0, zbanks=[0, 1, 2], memset=False)
            chB = [(xr[:, 0:NCH + 1], xrb), (xi[:, 0:NCH + 1], xib), (ytr[:, 0:NCH + 1], ytrb), (yti[:, 0:NCH + 1], ytib)]
            m1bf = m1.bitcast(BF16)
            EbB = [(m1bf[:, 0:NCH], m1b), (m1bf[:, NCH:2 * NCH], m1b)]
            setB = dict(ch=chB, Eb=EbB, l2=l2B, zb0=5 * 512, zbanks=[5, 6, 7], memset=True)

            def chain(gg, S):
                g = ft * 8 + gg
                ZB = S["zb0"]
                (c0_, c0b), (c1_, c1b), (c2_, c2b), (c3_, c3b) = S["ch"]
                if S["memset"]:
                    P.op("dve", "memset", [], [c0b], ap=c0_[:, 0:1], constant=0.0)
                    P.op("dve", "memset", [], [c2b], ap=c2_[:, 0:1], constant=0.0)
                for (part, (wz, wzb)) in enumerate(((wzr, wzrb), (wzi, wzib))):
                    flat0 = ZB + part * NCH
                    for (lo, hi) in segs:
                        for (bank, c0, n, s_start) in seg_split(lo, hi, flat0):
                            P.op("pe", "matmul", [wzb, Upb], [self.psb[bank]], out=ps[0:64, bank, c0:c0 + n], lhsT=wz[:, gg, 0:64], rhs=Up[:, gg, s_start:s_start + n],
                                 start=True, stop=True)
                            k0, k1 = bwd_cols(s_start - 0, n, lo, hi)
                            P.op("pe", "matmul", [wzb, Upb], [self.psb[bank]], out=ps[64:128, bank, c0:c0 + n], lhsT=wz[:, gg, 64:128], rhs=Up[:, gg, k0:k1:-1],
                                 start=True, stop=True)
                yield
                Zr = psflat[:, ZB:ZB + NCH].rearrange("p (m e) -> p m e", e=L1)
                Zi = psflat[:, ZB + NCH:ZB + 2 * NCH].rearrange("p (m e) -> p m e", e=L1)
                zbufs = [self.psb[q] for q in S["zbanks"]]
                v3 = lambda ap: ap[:, 1:NCH + 1].rearrange("p (m e) -> p m e", e=L1)
                qb_r = Qr[:, g, :].unsqueeze(1).to_broadcast([128, NL1, L1])
                qb_i = Qi[:, g, :].unsqueeze(1).to_broadcast([128, NL1, L1])
                P.op("dve", "tensor_tensor", zbufs + [tb], [c0b], out=v3(c0_), in0=Zr, in1=qb_r, op=ALU.mult); yield
                P.op("dve", "tensor_tensor", zbufs + [tb], [c1b], out=v3(c1_), in0=Zi, in1=qb_i, op=ALU.mult); yield
                P.op("dve", "tensor_tensor", [c0b, c1b], [c0b], out=v3(c0_), in0=v3(c0_), in1=v3(c1_), op=ALU.subtract); yield
                P.op("dve", "tensor_tensor", zbufs + [tb], [c2b], out=v3(c2_), in0=Zi, in1=qb_r, op=ALU.mult); yield
                P.op("dve", "tensor_tensor", zbufs + [tb], [c3b], out=v3(c3_), in0=Zr, in1=qb_i, op=ALU.mult); yield
                P.op("dve", "tensor_tensor", [c2b, c3b], [c2b], out=v3(c2_), in0=v3(c2_), in1=v3(c3_), op=ALU.add); yield
                P.op("dve", "tensor_tensor_scan", [c0b, tb], [c1b], out=c1_[:, 1:NCH + 1], data0=c0_[:, 0:NCH], data1=mask01, initial=0.0, op0=ALU.add, op1=ALU.mult); yield
                P.op("dve", "tensor_tensor_scan", [c2b, tb], [c3b], out=c3_[:, 1:NCH + 1], data0=c2_[:, 0:NCH], data1=mask01, initial=0.0, op0=ALU.add, op1=ALU.mult); yield
                Xr, Xi, Wr, Wi = v3(c1_), v3(c3_), v3(c0_), v3(c2_)
                l2_ = S["l2"]
                Dr, Di, Vvr, Vvi, Gr, Gi, Cr2, Ci2, t5, t6 = (l2_[:, q * NL1:(q + 1) * NL1] for q in range(10))
                Lb = S.setdefault("l2buf", Buf("l2"))
                L = [Lb]
                P.op("dve", "tensor_tensor", [c1b, c0b], L, out=Dr, in0=Xr[:, :, L1 - 1], in1=Wr[:, :, L1 - 1], op=ALU.add); yield
                P.op("dve", "tensor_tensor", [c3b, c2b], L, out=Di, in0=Xi[:, :, L1 - 1], in1=Wi[:, :, L1 - 1], op=ALU.add); yield
                LT = L + [tb]
                P.op("dve", "tensor_tensor", LT, L, out=t5, in0=Dr, in1=Vr[:, g, :], op=ALU.mult); yield
                P.op("dve", "tensor_tensor", LT, L, out=t6, in0=Di, in1=Vi[:, g, :], op=ALU.mult); yield
                P.op("dve", "tensor_tensor", L, L, out=Vvr, in0=t5, in1=t6, op=ALU.subtract); yield
                P.op("dve", "tensor_tensor", LT, L, out=t5, in0=Dr, in1=Vi[:, g, :], op=ALU.mult); yield
                P.op("dve", "tensor_tensor", LT, L, out=t6, in0=Di, in1=Vr[:, g, :], op=ALU.mult); yield
                P.op("dve", "tensor_tensor", L, L, out=Vvi, in0=t5, in1=t6, op=ALU.add); yield
                P.op("dve", "tensor_tensor_scan", LT, L, out=Gr, data0=rmask[:, g, :], data1=Vvr, initial=0.0, op0=ALU.mult, op1=ALU.add); yield
                P.op("dve", "tensor_tensor_scan", LT, L, out=Gi, data0=rmask[:, g, :], data1=Vvi, initial=0.0, op0=ALU.mult, op1=ALU.add); yield
                P.op("dve", "memset", L, L, ap=Cr2[:, 0:1], constant=0.0)
                P.op("dve", "memset", L, L, ap=Ci2[:, 0:1], constant=0.0); yield
                if NL1 > 1:
                    n1 = NL1 - 1
                    ur_, ui_ = Umr[:, g, 1:NL1], Umi[:, g, 1:NL1]
                    P.op("dve", "tensor_tensor", LT, L, out=t5[:, 0:n1], in0=Gr[:, 0:n1], in1=ur_, op=ALU.mult); yield
                    P.op("dve", "tensor_tensor", LT, L, out=t6[:, 0:n1], in0=Gi[:, 0:n1], in1=ui_, op=ALU.mult); yield
                    P.op("dve", "tensor_tensor", L, L, out=Cr2[:, 1:NL1], in0=t5[:, 0:n1], in1=t6[:, 0:n1], op=ALU.subtract); yield
                    P.op("dve", "tensor_tensor", LT, L, out=t5[:, 0:n1], in0=Gr[:, 0:n1], in1=ui_, op=ALU.mult); yield
                    P.op("dve", "tensor_tensor", LT, L, out=t6[:, 0:n1], in0=Gi[:, 0:n1], in1=ur_, op=ALU.mult); yield
                    P.op("dve", "tensor_tensor", L, L, out=Ci2[:, 1:NL1], in0=t5[:, 0:n1], in1=t6[:, 0:n1], op=ALU.add); yield
                P.op("dve", "tensor_tensor", [c1b] + L, [c1b], out=Xr, in0=Xr, in1=Cr2.unsqueeze(2).to_broadcast([128, NL1, L1]), op=ALU.add); yield
                P.op("dve", "tensor_tensor", [c3b] + L, [c3b], out=Xi, in0=Xi, in1=Ci2.unsqueeze(2).to_broadcast([128, NL1, L1]), op=ALU.add); yield
                pb_r = Ppr[:, g, :].unsqueeze(1).to_broadcast([128, NL1, L1])
                pb_i = Ppi[:, g, :].unsqueeze(1).to_broadcast([128, NL1, L1])
                (Er, Erb), (Ei, Eib) = S["Eb"]
                e3 = lambda ap: ap.rearrange("p (m e) -> p m e", e=L1)
                P.op("dve", "tensor_tensor", [c1b, tb], [c0b], out=Wr, in0=Xr, in1=pb_r, op=ALU.mult); yield
                P.op("dve", "tensor_tensor", [c3b, tb], [c2b], out=Wi, in0=Xi, in1=pb_i, op=ALU.mult); yield
                P.op("dve", "tensor_tensor", [c0b, c2b], [Erb], out=e3(Er), in0=Wr, in1=Wi, op=ALU.subtract); yield
                P.op("dve", "tensor_tensor", [c3b, tb], [c0b], out=Wr, in0=Xi, in1=pb_r, op=ALU.mult); yield
                P.op("dve", "tensor_tensor", [c1b, tb], [c2b], out=Wi, in0=Xr, in1=pb_i, op=ALU.mult); yield
                P.op("dve", "tensor_tensor", [c0b, c2b], [Eib], out=e3(Ei), in0=Wr, in1=Wi, op=ALU.add); yield
                tg, tgb = wts["tg"]
                (wyrB, wyrBb), (wyiB, wyiBb) = wts["yrB"], wts["yiB"]
                for (si, (lo, hi)) in enumerate(segs):
                    n = hi - lo
                    bank = YB + (0 if si == 1 else 1)
                    outp = ps[:, bank, 0:n]
                    P.op("pe", "matmul", [tgb, Upb], [self.psb[bank]], out=outp, lhsT=tg[:, gg, :], rhs=Up[:, gg, lo:hi], start=True, stop=False)
                    P.op("pe", "matmul", [wyrb, Erb], [self.psb[bank]], out=outp, lhsT=wyr[:, gg, :], rhs=Er[:, lo:hi], start=False, stop=False)
                    P.op("pe", "matmul", [wyib, Eib], [self.psb[bank]], out=outp, lhsT=wyi[:, gg, :], rhs=Ei[:, lo:hi], start=False, stop=False)
                    k0, k1 = hi - 1, (lo - 1 if lo > 0 else None)
                    P.op("pe", "matmul", [wyrBb, Erb], [self.psb[bank]], out=outp, lhsT=wyrB[:, gg, :], rhs=Er[:, k0:k1:-1], start=False, stop=False)
                    P.op("pe", "matmul", [wyiBb, Eib], [self.psb[bank]], out=outp, lhsT=wyiB[:, gg, :], rhs=Ei[:, k0:k1:-1], start=False, stop=True)
                    P.op("dve", "scalar_tensor_tensor", [Upb, tb, self.psb[bank]], [c0b], out=c0_[:, 1 + lo:1 + hi], in0=Up[:, gg, lo:hi], scalar=dcol[:, g:g + 1], in1=outp,
                         op0=ALU.mult, op1=ALU.add)
                    yield
                P.op("act", "activation", [c0b], [zpb], out=zp[:, gg, :], in_=c0_[:, 1:NCH + 1], func=AF.Gelu_apprx_tanh)
                yield

            import itertools
            for gp in range(0, 8, 2):
                gens = [chain(gp, setA), chain(gp + 1, setB)]
                for _ in itertools.zip_longest(*gens):
                    pass
            P.record_begin()
            zv = hTf.rearrange("p (k i) -> p k i", i=8)
            for jj in range(8):
                for (lo, hi) in segs:
                    n = hi - lo
                    bank = (7, 5, 6, 0, 1, 2)[pbk % 6]
                    pbk += 1
                    for gg in range(8):
                        P.op("pe", "matmul", [zpb, tb], [self.psb[bank]], out=ps[:, bank, 0:n], lhsT=Jc[:, jj, 112 - 16 * gg:240 - 16 * gg], rhs=zp[:, gg, lo:hi],
                             start=(gg == 0), stop=(gg == 7))
                    P.op("act", "copy", [self.psb[bank]], [hTfb], out=zv[:, lo:hi, jj], in_=ps[:, bank, 0:n])
            P.dma(hT_d[ft], hTf, [hTfb], [self.hTdb], "s2st")
            rec_unpack_prev = P.record_end()

        P.replay(rec_unpack_prev, len(rec_unpack_prev))
        o = S0END
        wg, wgb = self.load_weight(o, d["s5_w_glu"][j], KD, 2 * D, "wglu_", kgroup=2); o += KD * 2 * D // 2
        zTs = []
        for q in range(2):
            ap, b = self.aalloc(o, 2048, BF16, f"s3z{q}"); o += 2048
            zTs.append((ap.rearrange("p (k n) -> p k n", k=KD), b))
        sig, sigb = self.aalloc(o, D, F32, "s3sig"); o += D
        og, ogb = self.aalloc(o, D, F32, "s3o"); o += D
        tiles = list(range(0 if ctx_out else NTC, NT))
        blocks = [tiles[a:a + 4] for a in range(0, len(tiles), 4)]
        for bi, blk in enumerate(blocks):
            nb = len(blk) * 128
            tok0 = blk[0] * 128
            zT, zTb = zTs[bi % 2]
            P.dma(zT[:, :, :nb], hT_d[:, :, tok0:tok0 + nb].rearrange("k p n -> p k n"), [self.hTdb], [zTb], f"s3ld{bi % 2}")
            for q, t in enumerate(blk):
                b4 = 4 * (t % 2)
                for cb_ in range(4):
                    for k in range(KD):
                        P.op("pe", "matmul", [zTb, wgb[k]], [self.psb[b4 + cb_]], out=ps[:, b4 + cb_, :], lhsT=zT[:, k, q * 128:(q + 1) * 128], rhs=wg[:, k, cb_ * 512:(cb_ + 1) * 512],
                             start=(k == 0), stop=(k == KD - 1))
                P.op("act", "activation", [self.psb[b4 + 2], self.psb[b4 + 3]], [sigb], out=sig, in_=ps[:, b4 + 2:b4 + 4, :].rearrange("p a n -> p (a n)"), func=AF.Sigmoid)
                P.op("dve", "tensor_tensor", [self.psb[b4], self.psb[b4 + 1], sigb], [ogb], out=og, in0=ps[:, b4:b4 + 2, :].rearrange("p a n -> p (a n)"), in1=sig, op=ALU.mult)
                st = self.resid_begin(t)
                for hf in range(2):
                    self.resid_half(st, 0, hf, og[:, hf * 512:(hf + 1) * 512], ogb)
                self.resid_end(st)

def host_consts(cfg):
    bf = ml_dtypes.bfloat16
    out = {
        "ident_bf": np.eye(128, dtype=np.float32).astype(bf),
        "onehot": np.eye(128, 1, dtype=np.float32),
    }
    J = np.zeros((8, 128, 240), np.float32)
    for a in range(8):
        for h in range(16):
            J[a, 16 * a + h, h + 112] = 1.0
    out["s5_J"] = J.astype(bf)
    nch = cfg.ntok // 8
    m01 = np.ones(nch, np.float32)
    m01[::32] = 0.0
    out["s5_mask01"] = m01
    ii = np.arange(128) // 16
    out["s5_maskFB"] = np.stack([(ii[None, :] >= ii[:, None]), (ii[None, :] <= ii[:, None])]).astype(np.float32)
    out["ident32"] = np.eye(128, dtype=np.float32)
    k = np.arange(256, dtype=np.float64)
    ang = 2 * np.pi * np.outer(k, k) / 256
    out["cs256n"] = np.concatenate([np.cos(ang), -np.sin(ang)], axis=1).astype(np.float32).astype(bf)
    out["cs256p"] = np.concatenate([np.cos(ang), np.sin(ang)], axis=1).astype(np.float32).astype(bf)
    n = cfg.n_lat
    rows_ = n // 64
    row = np.repeat(np.arange(rows_, dtype=np.float32), 64)
    col = np.tile(np.arange(64, dtype=np.float32), rows_)
    inv = np.power(np.float32(10000.0), -np.arange(0, 32, 2, dtype=np.float32) / np.float32(32)).astype(np.float32)
    ang_r = (row[:, None] * inv).astype(np.float32)
    ang_c = (col[:, None] * inv).astype(np.float32)
    out["rope_tab"] = np.concatenate([np.cos(ang_r), np.cos(ang_c), np.sin(ang_r), np.sin(ang_c)], axis=1).astype(np.float32)
    t = np.arange(n, dtype=np.int64)
    m = np.outer(t, t) % n
    ang = (2 * np.pi / n) * m.astype(np.float64)
    nt = n // 128
    tabs = []
    for fn in (np.cos, np.sin):
        M = fn(ang).astype(np.float32)
        M = M.reshape(nt, 128, nt, 128).transpose(2, 1, 0, 3)
        tabs.append(M)
    out["dft_n"] = np.ascontiguousarray(np.stack(tabs, axis=2)).astype(bf)
    return out


def make_in_maps(inputs, cfg, n_cores=8):
    consts = host_consts(cfg)
    shared = {k: np.ascontiguousarray(inputs[k]) for k in ("w_mod", "b_mod", "norm_g", "mlp_w1", "mlp_w2", "fourier_w", "fourier_b",
                                                            "diff_w_qkv", "diff_q_norm", "diff_k_norm", "diff_lambda", "diff_subln", "diff_w_o")}
    for nm in ("s5_log_dt", "s5_w_glu"):
        shared[nm] = np.ascontiguousarray(inputs[nm])
    shared["s5_lamr_t"] = np.ascontiguousarray(inputs["s5_lambda_re"].transpose(0, 1, 3, 2).reshape(-1, 128, 64))
    shared["s5_lami_t"] = np.ascontiguousarray(inputs["s5_lambda_im"].transpose(0, 1, 3, 2).reshape(-1, 128, 64))
    shared["s5_br_t"] = np.ascontiguousarray(inputs["s5_b_re"].transpose(0, 1, 3, 2, 4).reshape(-1, 128, 64, 16))
    shared["s5_bi_t"] = np.ascontiguousarray(inputs["s5_b_im"].transpose(0, 1, 3, 2, 4).reshape(-1, 128, 64, 16))
    shared["s5_cr_t"] = np.ascontiguousarray(inputs["s5_c_re"].transpose(0, 1, 4, 2, 3).reshape(-1, 128, 64, 16))
    shared["s5_ci_t"] = np.ascontiguousarray(inputs["s5_c_im"].transpose(0, 1, 4, 2, 3).reshape(-1, 128, 64, 16))
    nS = inputs["s5_d"].shape[0]
    shared["s5_dcol"] = np.ascontiguousarray(np.tile(inputs["s5_d"].reshape(nS, 64, 16).transpose(0, 2, 1), (1, 8, 1)))
    cctx_t = np.ascontiguousarray(inputs["c_ctx"].reshape(KD, 128).T)
    maps = []
    for b in range(n_cores):
        m = dict(shared)
        m.update(consts)
        m["x"] = np.ascontiguousarray(inputs["x"][b])
        m["ctx"] = np.ascontiguousarray(inputs["ctx"][b])
        m["c_t"] = np.ascontiguousarray(inputs["c"][b].reshape(KD, 128).T)
        m["cctx_t"] = cctx_t
        maps.append(m)
    return maps


_NC_CACHE = {}


def kernel(**inputs):
    cfg = Cfg()
    nc = K(cfg).build()
    maps = make_in_maps(inputs, cfg, 8)
    res = run_bass_kernel_spmd(nc, maps, core_ids=list(range(8)))
    return np.stack([r["out"] for r in res.results], axis=0)
```
